# Optimizing a Trainium2 kernel written in Bass

```python
import jax
import jax.numpy as jnp
from jax import lax
import numpy as np

D_MODEL = 4096
BATCH = 32
SEQ = 256
DEPTH = 1
DEC_BATCH = 4
DEC_SEQ = 2048
PAST_LEN = 256

GRID_W = 64
D_ATT = D_MODEL // 2
D_FOURIER = D_MODEL - D_ATT
D_MIX = D_ATT + D_FOURIER
HEAD_DIM = 128
N_HEADS = D_ATT // HEAD_DIM
N_FGROUPS = 4
FGROUP_DIM = D_FOURIER // N_FGROUPS
WIN_H = 8
WIN_W = 16
D_FF = ((8 * D_MODEL + 3 * 256 - 1) // (3 * 256)) * 256
Q_BLOCK = 128
EPS = 1e-6
NEG_INF = -1e30

kernel_name = "hybrid_natten_fnet_dit_step"


def rmsnorm(x, g):
    xf = x.astype(jnp.float32)
    y = xf * lax.rsqrt(jnp.mean(xf * xf, axis=-1, keepdims=True) + EPS)
    return (y * g.astype(jnp.float32)).astype(x.dtype)


def ada_mod(cvec, w_ada, b_ada):
    m = jax.nn.silu(cvec) @ w_ada + b_ada
    return jnp.split(m[:, None, :], 6, axis=-1)


def modulate(h, shift, scale):
    return h * (1 + scale) + shift


def swiglu(h, w_gate, w_up, w_down):
    return (jax.nn.silu(h @ w_gate) * (h @ w_up)) @ w_down


def mixer_split(h, w_in):
    b, t, _ = h.shape
    proj = h @ w_in
    q, k, v, u = jnp.split(proj, [D_ATT, 2 * D_ATT, 3 * D_ATT], axis=-1)
    return (q.reshape(b, t, N_HEADS, HEAD_DIM), k.reshape(b, t, N_HEADS, HEAD_DIM),
            v.reshape(b, t, N_HEADS, HEAD_DIM), u)


def fourier_mix(u):
    b, t, _ = u.shape
    ug = u.reshape(b, t, N_FGROUPS, FGROUP_DIM).astype(jnp.float32)
    f = jnp.fft.fft2(ug, axes=(1, 3), norm="ortho").real
    return f.reshape(b, t, D_FOURIER).astype(u.dtype)


def _dense_attend(q, k, v):
    s = jnp.einsum("bqhd,bkhd->bhqk", q, k).astype(jnp.float32) * (HEAD_DIM ** -0.5)
    p = jax.nn.softmax(s, axis=-1).astype(v.dtype)
    return jnp.einsum("bhqk,bkhd->bqhd", p, v)


def context_attention(q, k, v):
    b, s, h, d = q.shape
    qb = q.reshape(b, s // Q_BLOCK, Q_BLOCK, h, d).transpose(1, 0, 2, 3, 4)
    o = lax.map(lambda qi: _dense_attend(qi, k, v), qb)
    return o.transpose(1, 0, 2, 3, 4).reshape(b, s, h * d)


def neighbourhood_attention(q, k, v, k_ctx, v_ctx, rpb):
    b, t, h, d = q.shape
    rows = t // GRID_W
    win_h = min(WIN_H, rows)
    n_loc = win_h * GRID_W
    col = np.arange(GRID_W)
    cstart = np.clip(col - WIN_W // 2, 0, GRID_W - WIN_W)
    col_mask = (col[None, :] >= cstart[:, None]) & (col[None, :] < cstart[:, None] + WIN_W)
    dc_idx = jnp.asarray(np.clip(col[None, :] - col[:, None] + WIN_W - 1, 0, 2 * WIN_W - 2))
    mask = jnp.asarray(np.tile(col_mask, (1, win_h)))
    r = np.arange(rows)
    rstart = np.clip(r - WIN_H // 2, 0, rows - win_h)
    key_rows = rstart[:, None] + np.arange(win_h)[None, :]
    dr_idx = key_rows - r[:, None] + WIN_H - 1
    qg = q.reshape(b, rows, GRID_W, h, d).transpose(1, 0, 2, 3, 4)
    kg = k.reshape(b, rows, GRID_W, h, d)
    vg = v.reshape(b, rows, GRID_W, h, d)
    scale = d ** -0.5

    def row_block(args):
        q_r, rows_r, dr_r = args
        k_win = jnp.take(kg, rows_r, axis=1).reshape(b, n_loc, h, d)
        v_win = jnp.take(vg, rows_r, axis=1).reshape(b, n_loc, h, d)
        bias = rpb[:, dr_r[:, None, None], dc_idx[None, :, :]]
        bias = bias.transpose(0, 2, 1, 3).reshape(h, GRID_W, n_loc).astype(jnp.float32)
        s_loc = jnp.einsum("bqhd,bkhd->bhqk", q_r, k_win).astype(jnp.float32) * scale + bias
        s_loc = jnp.where(mask, s_loc, NEG_INF)
        s_ctx = jnp.einsum("bqhd,bkhd->bhqk", q_r, k_ctx).astype(jnp.float32) * scale
        p = jax.nn.softmax(jnp.concatenate([s_loc, s_ctx], axis=-1), axis=-1).astype(v.dtype)
        return (jnp.einsum("bhqk,bkhd->bqhd", p[..., :n_loc], v_win)
                + jnp.einsum("bhqk,bkhd->bqhd", p[..., n_loc:], v_ctx))

    o = lax.map(row_block, (qg, jnp.asarray(key_rows), jnp.asarray(dr_idx)))
    return o.transpose(1, 0, 2, 3, 4).reshape(b, t, h * d)


def setup_inputs(seed: int = 0) -> dict:
    key = jax.random.key(seed)
    ks = jax.random.split(key, 17)
    f32 = jnp.float32

    def nrm(k, shape, s=1.0):
        return jax.random.normal(k, shape, f32) * s

    return {
        "x_prompt": nrm(ks[0], (BATCH, SEQ, D_MODEL)),
        "x_sample": nrm(ks[1], (DEC_BATCH, DEC_SEQ, D_MODEL)),
        "cache_k": nrm(ks[2], (DEC_BATCH, DEPTH, PAST_LEN, N_HEADS, HEAD_DIM)),
        "cache_v": nrm(ks[3], (DEC_BATCH, DEPTH, PAST_LEN, N_HEADS, HEAD_DIM)),
        "c": nrm(ks[4], (DEC_BATCH, D_MODEL)),
        "c_ctx": nrm(ks[5], (D_MODEL,)),
        "w_ada": nrm(ks[6], (DEPTH, D_MODEL, 6 * D_MODEL), 0.5 * D_MODEL ** -0.5),
        "b_ada": nrm(ks[7], (DEPTH, 6 * D_MODEL), 0.01),
        "norm1_g": 1.0 + nrm(ks[8], (DEPTH, D_MODEL), 0.02),
        "w_in": nrm(ks[9], (DEPTH, D_MODEL, 3 * D_ATT + D_FOURIER), D_MODEL ** -0.5),
        "rpb": nrm(ks[10], (DEPTH, N_HEADS, 2 * WIN_H - 1, 2 * WIN_W - 1), 0.1),
        "w_out": nrm(ks[11], (DEPTH, D_MIX, D_MODEL), D_MIX ** -0.5),
        "norm2_g": 1.0 + nrm(ks[12], (DEPTH, D_MODEL), 0.02),
        "w_gate": nrm(ks[13], (DEPTH, D_MODEL, D_FF), D_MODEL ** -0.5),
        "w_up": nrm(ks[14], (DEPTH, D_MODEL, D_FF), D_MODEL ** -0.5),
        "w_down": nrm(ks[15], (DEPTH, D_FF, D_MODEL), D_FF ** -0.5),
        "final_g": 1.0 + nrm(ks[16], (D_MODEL,), 0.02),
    }


def reference(x_prompt, x_sample, cache_k, cache_v, c, c_ctx, w_ada, b_ada, norm1_g, w_in,
              rpb, w_out, norm2_g, w_gate, w_up, w_down, final_g):
    xp = x_prompt
    xs = x_sample
    new_k = []
    new_v = []
    for l in range(DEPTH):
        sh1, sc1, g1, sh2, sc2, g2 = ada_mod(c_ctx[None, :], w_ada[l], b_ada[l])
        h = modulate(rmsnorm(xp, norm1_g[l]), sh1, sc1)
        q, k, v, u = mixer_split(h, w_in[l])
        mix = jnp.concatenate([context_attention(q, k, v), fourier_mix(u)], axis=-1)
        xp = xp + g1 * (mix @ w_out[l])
        h = modulate(rmsnorm(xp, norm2_g[l]), sh2, sc2)
        xp = xp + g2 * swiglu(h, w_gate[l], w_up[l], w_down[l])
        new_k.append(k)
        new_v.append(v)
        sh1, sc1, g1, sh2, sc2, g2 = ada_mod(c, w_ada[l], b_ada[l])
        h = modulate(rmsnorm(xs, norm1_g[l]), sh1, sc1)
        q, k, v, u = mixer_split(h, w_in[l])
        att = neighbourhood_attention(q, k, v, cache_k[:, l], cache_v[:, l], rpb[l])
        mix = jnp.concatenate([att, fourier_mix(u)], axis=-1)
        xs = xs + g1 * (mix @ w_out[l])
        h = modulate(rmsnorm(xs, norm2_g[l]), sh2, sc2)
        xs = xs + g2 * swiglu(h, w_gate[l], w_up[l], w_down[l])
    y_prompt = rmsnorm(xp, final_g)
    y_sample = rmsnorm(xs, final_g)
    new_cache_k = jnp.stack(new_k, axis=1)
    new_cache_v = jnp.stack(new_v, axis=1)
    return (y_prompt, y_sample, new_cache_k, new_cache_v)
```

```python
from contextlib import ExitStack, contextmanager
import numpy as np
import concourse.bass as bass
import concourse.mybir as mybir
from concourse.bass_utils import run_bass_kernel_spmd

F32 = mybir.dt.float32
BF16 = mybir.dt.bfloat16
AF = mybir.ActivationFunctionType
ALU = mybir.AluOpType

D = 4096
T = 2048
NTB = 16
DFF = 11008
NF = 86
EPS = 1e-6
SCALE = 128.0 ** -0.5
NEG = -30000.0
import os as _os
NOPOOL = bool(_os.environ.get('NOPOOL'))


class Sem:
    def __init__(self, nc, name):
        self.h = nc.alloc_semaphore(name)
        self.v = 0


class Lazy:
    def __init__(self, f):
        self._f = f
        self._v = None

    def get(self):
        if self._v is None:
            self._v = self._f()
        return self._v

    def __getitem__(self, k):
        return self.get()[k]

    def __getattr__(self, n):
        return getattr(self.get(), n)


def _un(a):
    return a.get() if isinstance(a, Lazy) else a


class B:
    def __init__(self):
        self.nc = bass.Bass("TRN2", target_bir_lowering=False)
        self.nsem = 0
        self.dma_sems = {"sync": set(), "gpsimd": set()}
        self.bar = Sem(self.nc, "bar")
        self.bar_n = 0
        self.pool = []
        self.scopes = []

    def sem(self, name):
        self.nsem += 1
        sm = self.pool.pop() if (self.pool and not NOPOOL) else Sem(self.nc, f"{name}_{self.nsem}")
        if self.scopes:
            self.scopes[-1].append(sm)
        return sm

    def semg(self, name):
        self.nsem += 1
        return Sem(self.nc, f"{name}_{self.nsem}")

    @contextmanager
    def scope(self):
        self.scopes.append([])
        yield
        self.pool.extend(self.scopes.pop())

    @staticmethod
    def inc(instr, sem, n=1):
        instr.then_inc(sem.h, n)
        sem.v += n
        return (sem, sem.v)

    @staticmethod
    def wait(eng, tok):
        if tok is not None:
            eng.wait_ge(tok[0].h, tok[1])

    def dma(self, q, out, in_, sem):
        eng = self.nc.sync if q == "sync" else self.nc.gpsimd
        ins = eng.dma_start(out=_un(out), in_=_un(in_))
        self.dma_sems[q].add(sem)
        return self.inc(ins, sem, 16)

    def barrier(self, dummies):
        nc = self.nc
        self.bar_n += 1
        dps, idb, dv, da, dg = dummies
        self.inc(nc.vector.memset(dv[:], 0.0), self.bar)
        self.inc(nc.scalar.activation(out=da[:, 0:1], in_=da[:, 1:2], func=AF.Copy), self.bar)
        for s in self.dma_sems["gpsimd"]:
            nc.gpsimd.wait_ge(s.h, s.v)
        self.inc(nc.gpsimd.memset(dg[:], 0.0), self.bar)
        for s in self.dma_sems["sync"]:
            nc.sync.wait_ge(s.h, s.v)
        nc.sync.sem_inc(self.bar.h, 1)
        self.bar.v += 1
        nc.tensor.wait_ge(self.bar.h, self.bar.v)
        self.inc(nc.tensor.matmul(dps[:, 511:512], idb[:], idb[:, 0:1], start=True, stop=True), self.bar)
        for e in (nc.tensor, nc.vector, nc.scalar, nc.gpsimd, nc.sync):
            e.wait_ge(self.bar.h, self.bar.v)


def build(stop=None, skip0=False):
    b = B()
    nc = b.nc
    inc, wait, dma = b.inc, b.wait, b.dma
    PE, DVE, ACT, POOL, SP = nc.tensor, nc.vector, nc.scalar, nc.gpsimd, nc.sync

    def din(name, shape):
        return Lazy(lambda: nc.dram_tensor(name, shape, F32, kind="ExternalInput").ap())

    def dout(name, shape):
        return Lazy(lambda: nc.dram_tensor(name, shape, F32, kind="ExternalOutput").ap())

    x = din("x", [T, D])
    cvec = din("cvec", [32, 128])
    w_ada = din("w_ada", [D, 6 * D])
    b_ada = din("b_ada", [1, 6 * D])
    n1g = din("n1g", [32, 128])
    n2g = din("n2g", [32, 128])
    fg = din("fg", [1, D])
    w_in = din("w_in", [D, 2 * D])
    w_out = din("w_out", [D, D])
    w_gate = din("w_gate", [D, DFF])
    w_up = din("w_up", [D, DFF])
    w_down = din("w_down", [DFF, D])
    kc_in = din("kc", [256, 2048])
    vc_in = din("vc", [256, 2048])
    bias_in = din("bias", [16, 128, 6 * 5 * 128])
    ctxb_in = din("ctxb", [128, 1])
    ct_in = din("ct", [T, T])
    st_in = din("st", [T, T])
    cs_in = din("cs", [512, 1024])
    ident_in = din("ident", [128, 128])
    y = dout("y", [T, D])
    nk = dout("nk", [T, 2048])
    nv = dout("nv", [T, 2048])
    mod_row = nc.dram_tensor("mod_row", [1, 6 * D], F32).ap()
    DBG = bool(_os.environ.get("DBG"))
    skind = "ExternalOutput" if DBG else "Internal"
    mix_scr = nc.dram_tensor("mix_scr", [NTB, 128, 32, 128], BF16, kind=skind).ap()
    ab_scr = nc.dram_tensor("ab_scr", [4, NTB, 128, 1024], BF16, kind=skind).ap()
    x1_scr = nc.dram_tensor("x1_scr", [T, D], F32, kind=skind).ap()
    act_scr = nc.dram_tensor("act_scr", [NTB, 128, NF, 128], BF16, kind=skind).ap()
    x2_scr = nc.dram_tensor("x2_scr", [T, D], F32, kind=skind).ap()

    idf = nc.alloc_sbuf_tensor("idf", [128, 128], F32)
    idb = nc.alloc_sbuf_tensor("idb", [128, 128], BF16)
    onesb = nc.alloc_sbuf_tensor("onesb", [128, 128], BF16)
    modT = nc.alloc_sbuf_tensor("modT", [128, 192], F32)
    a1 = nc.alloc_sbuf_tensor("a1", [128, 32], F32)
    a2 = nc.alloc_sbuf_tensor("a2", [128, 32], F32)
    ctxb = nc.alloc_sbuf_tensor("ctxb_sb", [128, 1], F32)
    epsc = nc.alloc_sbuf_tensor("epsc", [128, 1], F32)
    dv = nc.alloc_sbuf_tensor("dv", [128, 1], F32)
    da = nc.alloc_sbuf_tensor("da", [128, 2], F32)
    dg = nc.alloc_sbuf_tensor("dg", [128, 1], F32)
    BK = [nc.alloc_psum_tensor(f"bk{i}", [128, 512], F32) for i in range(6)]
    PB = [nc.alloc_psum_tensor(f"pb{i}", [128, 1024], BF16) for i in range(2)]
    dummies = (BK[5], idb, dv, da, dg)

    s0 = b.sem("init")
    dma("sync", idf[:], ident_in, s0)
    s0g = b.semg("initg")
    dma("gpsimd", idb[:], ident_in, s0g)
    dma("sync", ctxb[:], ctxb_in, s0)
    for e in (PE, DVE, ACT, POOL):
        e.wait_ge(s0.h, s0.v)
        e.wait_ge(s0g.h, s0g.v)
    DVE.memset(onesb[:], 1.0)
    DVE.memset(epsc[:], EPS)
    s0c = b.sem("initc")
    t = inc(DVE.memset(da[:], 0.0), s0c)
    for e in (PE, ACT, POOL, SP):
        wait(e, t)
    b.barrier(dummies)

    if skip0:
        DVE.memset(modT[:], 0.5)
        DVE.memset(a1[:], 1.5)
        t = inc(DVE.memset(a2[:], 1.5), s0c)
        wait(SP, t)
        dma("sync", mod_row.rearrange("o (p j) -> (o p) j", p=128), modT[:], s0)
        for e in (PE, ACT, POOL, SP):
            wait(e, t)
        b.barrier(dummies)
    if not skip0:
        with ExitStack() as es:
            es.enter_context(b.scope())
            def sb(name, shape, dt):
                return es.enter_context(nc.sbuf_tensor(name, shape, dt))
            cv = sb("cv", [32, 128], F32)
            cvT = sb("cvT", [128, 32], BF16)
            ring = [sb(f"adar{i}", [128, 32, 512], BF16) for i in range(3)]
            brow = [sb(f"brow{i}", [1, 512], F32) for i in range(2)]
            rowsb = [sb(f"rowsb{i}", [1, 512], F32) for i in range(2)]
            mr = [sb(f"mr{i}", [96, 128], F32) for i in range(2)]
            gt = [sb(f"gt{i}", [32, 128], F32) for i in range(2)]
            s_ld = b.sem("ld"); s_w = [b.semg("w") for _ in range(3)]; s_b = [b.sem("b") for _ in range(2)]; s_mm = b.sem("mm"); s_row = b.sem("row")
            s_st = [b.sem("st") for _ in range(2)]; s_a = b.sem("a")
            t = dma("sync", cv[:], cvec, s_ld)
            wait(PE, t)
            t = inc(PE.transpose(BK[0][:, 0:32], cv[:], idf[0:32, 0:32]), s_a)
            wait(ACT, t)
            t_cvT = inc(ACT.activation(out=cvT[:], in_=BK[0][:, 0:32], func=AF.Silu), s_a)
            wait(PE, t_cvT)
            NU = 48
            mm_tok = [None] * NU
            row_tok = [None] * NU
            st_tok = [None] * NU
            for u in range(NU):
                if u >= 3:
                    wait(POOL, mm_tok[u - 3])
                tw = dma("gpsimd", ring[u % 3][:], w_ada[:, u * 512:(u + 1) * 512].rearrange("(kc p) n -> p kc n", p=128), s_w[u % 3])
                if u >= 2:
                    wait(SP, row_tok[u - 2])
                tb_ = dma("sync", brow[u % 2][:], b_ada[:, u * 512:(u + 1) * 512], s_b[u % 2])
                wait(PE, tw)
                if u >= 2:
                    wait(PE, row_tok[u - 2])
                for k in range(32):
                    mm = PE.matmul(BK[u % 2][0:1, :], cvT[:, k:k + 1], ring[u % 3][:, k, :], start=(k == 0), stop=(k == 31))
                mm_tok[u] = inc(mm, s_mm)
                wait(DVE, mm_tok[u]); wait(DVE, tb_)
                if u >= 2:
                    wait(DVE, st_tok[u - 2])
                row_tok[u] = inc(DVE.tensor_tensor(out=rowsb[u % 2][:], in0=BK[u % 2][0:1, :], in1=brow[u % 2][:], op=ALU.add), s_row)
                wait(SP, row_tok[u])
                st_tok[u] = dma("sync", mod_row[:, u * 512:(u + 1) * 512], rowsb[u % 2][:], s_st[u % 2])
            wait(SP, st_tok[NU - 1]); wait(SP, st_tok[NU - 2])
            mrv = mod_row.rearrange("o (j p) -> (o j) p", p=128)
            t0_ = dma("sync", mr[0][:], mrv[0:96, :], s_ld)
            t1_ = dma("sync", mr[1][:], mrv[96:192, :], s_ld)
            t2_ = dma("sync", gt[0][:], n1g, s_ld)
            t3_ = dma("sync", gt[1][:], n2g, s_ld)
            wait(PE, t3_)
            PE.transpose(BK[2][:, 0:96], mr[0][:], idf[0:96, 0:96])
            PE.transpose(BK[2][:, 96:192], mr[1][:], idf[0:96, 0:96])
            PE.transpose(BK[2][:, 192:224], gt[0][:], idf[0:32, 0:32])
            t = inc(PE.transpose(BK[2][:, 224:256], gt[1][:], idf[0:32, 0:32]), s_a)
            wait(DVE, t)
            t = inc(DVE.tensor_copy(out=modT[:], in_=BK[2][:, 0:192]), s_a)
            wait(DVE, t)
            DVE.scalar_tensor_tensor(out=a1[:], in0=modT[:, 32:64], scalar=1.0, in1=BK[2][:, 192:224], op0=ALU.add, op1=ALU.mult)
            DVE.scalar_tensor_tensor(out=a2[:], in0=modT[:, 128:160], scalar=1.0, in1=BK[2][:, 224:256], op0=ALU.add, op1=ALU.mult)
            b.barrier(dummies)
    b1 = modT[:, 0:32]
    b2 = modT[:, 96:128]

    def dbg_out(name, src_ap, shape):
        o = nc.dram_tensor(name, shape, F32, kind="ExternalOutput").ap()
        sd = b.sem("dbg")
        tk = dma("sync", o, src_ap, sd)
        wait(SP, tk)

    if stop == 0:
        dbg_out("d_modT", modT[:], [128, 192])
        dbg_out("d_a1", a1[:], [128, 32])
        return nc

    def phase_norm(src, av, bv, hT, tag):
        with ExitStack() as es:
            es.enter_context(b.scope())
            def sb(name, shape, dt):
                return es.enter_context(nc.sbuf_tensor(name + tag, shape, dt))
            xt = [sb(f"xt{i}", [128, D], F32) for i in range(2)]
            xh = [sb(f"xh{i}", [128, D], BF16) for i in range(2)]
            junk = sb("junk", [128, D], BF16)
            ss = sb("ss", [128, NTB], F32)
            sq = sb("sq", [128, NTB], F32)
            rs = sb("rs", [128, NTB], F32)
            s_ld = [b.sem("ld"), b.sem("ld")]; s_sq = b.sem("sq"); s_rt = b.sem("rt"); s_r = b.sem("r"); s_xh = b.sem("xh")
            s_tr = b.sem("tr"); s_ea = b.sem("ea"); s_ed = b.sem("ed")
            xh_tok = [None] * NTB
            tr_last = [None] * NTB
            ev_tok = {}
            t = inc(DVE.memset(ss[:], 0.0), s_r)
            wait(ACT, t)
            gi = 0
            LVL = int(_os.environ.get("LVL", "9"))
            sq_tok = [None] * NTB
            for tb in range(NTB):
                if tb >= 2:
                    wait(SP, xh_tok[tb - 2] if LVL >= 2 else sq_tok[tb - 2])
                tl = dma("sync", xt[tb % 2][:], src[tb * 128:(tb + 1) * 128, :], s_ld[tb % 2])
                wait(ACT, tl)
                t = inc(ACT.activation(out=junk[:], in_=xt[tb % 2][:], func=AF.Square, accum_out=ss[:, tb:tb + 1]), s_sq)
                sq_tok[tb] = t
                if LVL < 2:
                    continue
                wait(ACT, t)
                t = inc(ACT.activation(out=sq[:, tb:tb + 1], in_=ss[:, tb:tb + 1], func=AF.Sqrt, bias=epsc[:], scale=1.0 / D), s_rt)
                wait(DVE, t)
                t = inc(DVE.reciprocal(out=rs[:, tb:tb + 1], in_=sq[:, tb:tb + 1]), s_r)
                wait(DVE, t)
                if tb >= 2 and LVL >= 3:
                    wait(DVE, tr_last[tb - 2])
                xh_tok[tb] = inc(DVE.tensor_scalar(out=xh[tb % 2][:], in0=xt[tb % 2][:], scalar1=rs[:, tb:tb + 1], scalar2=None, op0=ALU.mult), s_xh)
                if LVL < 3:
                    continue
                wait(PE, xh_tok[tb])
                for g in range(4):
                    pb = PB[gi % 2]
                    if gi >= 2 and LVL >= 4:
                        wait(PE, ev_tok[gi - 2])
                    for j in range(8):
                        kc = g * 8 + j
                        tr = PE.transpose(pb[:, j * 128:(j + 1) * 128], xh[tb % 2][:, kc * 128:(kc + 1) * 128], idb[:])
                    ttr = inc(tr, s_tr)
                    if g == 3:
                        tr_last[tb] = ttr
                    if LVL < 4:
                        gi += 1
                        continue
                    use_act = (gi % 2 == 0)
                    wait(ACT if use_act else DVE, ttr)
                    for j in range(8):
                        kc = g * 8 + j
                        o = hT[:, kc, tb * 128:(tb + 1) * 128]
                        i_ = pb[:, j * 128:(j + 1) * 128]
                        if use_act:
                            iv = ACT.activation(out=o, in_=i_, func=AF.Identity, bias=bv[:, kc:kc + 1], scale=av[:, kc:kc + 1])
                        else:
                            iv = DVE.tensor_scalar(out=o, in0=i_, scalar1=av[:, kc:kc + 1], scalar2=bv[:, kc:kc + 1], op0=ALU.mult, op1=ALU.add)
                    ev_tok[gi] = inc(iv, s_ea if use_act else s_ed)
                    gi += 1
            b.barrier(dummies)

    def wload(dst, W, c0, n, sem):
        return dma("gpsimd", dst, W[:, c0:c0 + n].rearrange("(kc p) n -> p kc n", p=128), sem)

    with ExitStack() as es_h:
        hT = es_h.enter_context(nc.sbuf_tensor("hT", [128, 32, T], BF16))
        phase_norm(x, a1, b1, hT, "n1")
        if stop == 1:
            with nc.sbuf_tensor("dbgh", [128, 32, 128], F32) as dbgh:
                if int(_os.environ.get('LVL', '9')) < 4:
                    DVE.memset(hT[:, :, 0:128], 1.0)
                DVE.tensor_copy(out=dbgh[:], in_=hT[:, :, 0:128])
                b.barrier(dummies)
                dbg_out("d_hT", dbgh[:], [128, 32, 128])
            return nc

        NSLOT = 3
        wr = [es_h.enter_context(nc.sbuf_tensor(f"wr{i}", [128, 32, 128], BF16)) for i in range(NSLOT)]
        s_wr = [b.semg(f"wr{i}") for i in range(NSLOT)]
        s_gm = b.sem("gm")
        state = {"u": 0, "g": 0}
        unit_last_grp = {}
        grp_rel = {}

        def ws_unit(W, c0, epilogue):
            u = state["u"]; state["u"] += 1
            slot = u % NSLOT
            if u >= NSLOT:
                wait(POOL, unit_last_grp[u - NSLOT])
            tw = wload(wr[slot][:], W, c0, 128, s_wr[slot])
            wait(PE, tw)
            for tg in range(4):
                g = state["g"]; state["g"] += 1
                bank = BK[g % 2]
                if g >= 2:
                    for tk in grp_rel[g - 2]:
                        wait(PE, tk)
                for k in range(32):
                    mm = PE.matmul(bank[:], wr[slot][:, k, :], hT[:, k, tg * 512:(tg + 1) * 512], start=(k == 0), stop=(k == 31))
                tk = inc(mm, s_gm)
                if tg == 3:
                    unit_last_grp[u] = tk
                grp_rel[g] = epilogue(tg, bank, tk)

        with ExitStack() as es:
            es.enter_context(b.scope())
            def sb(name, shape, dt):
                return es.enter_context(nc.sbuf_tensor(name, shape, dt))
            qT = sb("qT", [128, T], BF16)
            kT = sb("kT", [128, T], BF16)
            vbf = sb("vbf", [128, NTB, 128], BF16)
            fst = [sb(f"fst{i}", [128, 512], F32) for i in range(2)]
            kvo = [sb(f"kvo{i}", [128, 4, 128], F32) for i in range(2)]
            biast = sb("biast", [128, 6 * 5 * 128], F32)
            PT = [sb(f"PT{i}", [128, 7 * 128], BF16) for i in range(2)]
            rinv = [sb(f"rinv{i}", [128, 128], F32) for i in range(2)]
            mixst = sb("mixst", [128, NTB, 128], BF16)
            kcs2 = [sb(f"kcs{i}", [128, 2, 128], BF16) for i in range(2)]
            kcT = sb("kcT", [128, 256], BF16)
            vcs2 = [sb(f"vcs{i}", [128, 2, 128], BF16) for i in range(2)]
            att_hist = {}
            s_q = b.sem("q"); s_kA = b.sem("kA"); s_kD = b.sem("kD"); s_tr = b.sem("tr"); s_ko = b.sem("ko")
            s_va = b.sem("va"); s_st = [b.sem("st"), b.sem("st")]; s_bias = b.sem("bias"); s_ctx = [b.semg("ctx"), b.semg("ctx")]; s_kct = b.sem("kct")
            s_qk = b.sem("qk"); s_e1 = b.sem("e1"); s_e2 = b.sem("e2"); s_ex = b.sem("ex"); s_pv = b.sem("pv")
            s_ri = b.sem("ri"); s_no = b.sem("no"); s_mx = b.sem("mx")
            fs_n = {"n": 0}
            tr_rel = {}
            fst_rel = {}
            kvo_rel = {}
            att = {"n": 0, "pv": {}, "ex": {}, "no": {}, "mx": None, "att_done": None}

            def ep_q(tg, bank, tk):
                wait(ACT, tk)
                if tg == 0:
                    wait(ACT, att["att_done"])
                return [inc(ACT.activation(out=qT[:, tg * 512:(tg + 1) * 512], in_=bank[:], func=AF.Copy), s_q)]

            def make_ep_kv(is_k, j):
                def ep(tg, bank, tk):
                    n = fs_n["n"]; fs_n["n"] += 1
                    f = fst[n % 2]
                    wait(DVE, tk)
                    if n >= 2:
                        for tk2 in fst_rel[n - 2]:
                            wait(DVE, tk2)
                    tf = inc(DVE.tensor_copy(out=f[:], in_=bank[:]), s_kD)
                    rel = [tf]
                    fst_rel[n] = []
                    if is_k:
                        wait(ACT, tf)
                        if tg == 0:
                            wait(ACT, att["att_done"])
                        fst_rel[n].append(inc(ACT.activation(out=kT[:, tg * 512:(tg + 1) * 512], in_=f[:], func=AF.Copy), s_kA))
                    wait(PE, tf)
                    if n >= 1:
                        for tk2 in tr_rel[n - 1]:
                            wait(PE, tk2)
                    for bb in range(4):
                        tr = PE.transpose(BK[2][:, bb * 128:(bb + 1) * 128], f[:, bb * 128:(bb + 1) * 128], idf[:])
                    ttr = inc(tr, s_tr)
                    fst_rel[n].append(ttr)
                    ko = kvo[n % 2]
                    wait(DVE, ttr)
                    if n >= 2:
                        for tk2 in kvo_rel[n - 2]:
                            wait(DVE, tk2)
                    tko = inc(DVE.tensor_copy(out=ko[:].rearrange("p b d -> p (b d)"), in_=BK[2][:]), s_ko)
                    tr_rel[n] = [tko]
                    kvo_rel[n] = []
                    if not is_k:
                        wait(ACT, tko)
                        if tg == 0:
                            wait(ACT, att["att_done"])
                        kvo_rel[n].append(inc(ACT.activation(out=vbf[:, tg * 4:(tg + 1) * 4, :].rearrange("p b d -> p (b d)"), in_=ko[:].rearrange("p b d -> p (b d)"), func=AF.Copy), s_va))
                    wait(SP, tko)
                    dst = (nk if is_k else nv)[tg * 512:(tg + 1) * 512, j * 128:(j + 1) * 128].rearrange("(b p) d -> p b d", p=128)
                    kvo_rel[n].append(dma("sync", dst, ko[:], s_st[n % 2]))
                    return rel
                return ep

            def tmap(i):
                return {0: 0, 1: 1, 14: 4, 15: 5}.get(i, 2 + (i % 2))

            for j in range(16):
                wait(SP, att["att_done"])
                t_bias = dma("sync", biast[:], bias_in[j], s_bias)
                kcs = kcs2[j % 2]; vcs = vcs2[j % 2]
                wait(POOL, att_hist.get(j - 2))
                t_kc = dma("gpsimd", kcs[:], kc_in[:, j * 128:(j + 1) * 128].rearrange("(b p) d -> p b d", p=128), s_ctx[j % 2])
                t_vc = dma("gpsimd", vcs[:], vc_in[:, j * 128:(j + 1) * 128].rearrange("(b p) d -> p b d", p=128), s_ctx[j % 2])
                t_kc = t_vc
                ws_unit(w_in, j * 128, ep_q)
                ws_unit(w_in, 2048 + j * 128, make_ep_kv(True, j))
                ws_unit(w_in, 4096 + j * 128, make_ep_kv(False, j))
                wait(PE, t_kc)
                for bb in range(2):
                    tr = PE.transpose(PB[0][:, bb * 128:(bb + 1) * 128], kcs[:, bb, :], idb[:])
                ttr = inc(tr, s_kct)
                wait(ACT, ttr)
                t_kcT = inc(ACT.activation(out=kcT[:], in_=PB[0][:, 0:256], func=AF.Copy), s_kct)
                last_g = state["g"] - 1
                pend = []
                for gg in range(last_g - 11, last_g + 1):
                    pend += grp_rel[gg]
                for nn in range(fs_n["n"] - 8, fs_n["n"]):
                    pend += tr_rel[nn] + [t_ for t_ in kvo_rel[nn] if t_[0] is s_va] + [t_ for t_ in fst_rel[nn] if t_[0] is s_kA]
                for tk in pend + [t_kcT, t_vc, t_bias]:
                    wait(PE, tk)
                wait(DVE, t_bias)
                wait(DVE, att["mx"])
                for i in range(16):
                    n = att["n"]; att["n"] += 1
                    base = min(max(i - 2, 0), 11)
                    ty = tmap(i)
                    X, Y, Z = BK[3], BK[4], BK[5]
                    if n >= 1:
                        wait(PE, att["ex"][n - 1])
                    for s in range(5):
                        kb = base + s
                        dst = X[:, s * 128:(s + 1) * 128] if s < 4 else Y[:, 0:128]
                        PE.matmul(dst, kT[:, kb * 128:(kb + 1) * 128], qT[:, i * 128:(i + 1) * 128], start=True, stop=True)
                    for s in range(2):
                        mm = PE.matmul(Y[:, (1 + s) * 128:(2 + s) * 128], kcT[:, s * 128:(s + 1) * 128], qT[:, i * 128:(i + 1) * 128], start=True, stop=True)
                    tqk = inc(mm, s_qk)
                    wait(DVE, tqk)
                    bo = ty * 640
                    DVE.scalar_tensor_tensor(out=X[:], in0=X[:], scalar=SCALE, in1=biast[:, bo:bo + 512], op0=ALU.mult, op1=ALU.add)
                    te = inc(DVE.scalar_tensor_tensor(out=Y[:, 0:128], in0=Y[:, 0:128], scalar=SCALE, in1=biast[:, bo + 512:bo + 640], op0=ALU.mult, op1=ALU.add), s_e1)
                    pt = PT[n % 2]
                    wait(ACT, te)
                    if n >= 2:
                        wait(ACT, att["pv"][n - 2])
                    ACT.activation(out=pt[:, 0:512], in_=X[:], func=AF.Exp)
                    ACT.activation(out=pt[:, 512:640], in_=Y[:, 0:128], func=AF.Exp)
                    tex = inc(ACT.activation(out=pt[:, 640:896], in_=Y[:, 128:384], func=AF.Exp, bias=ctxb[:], scale=SCALE), s_ex)
                    att["ex"][n] = tex
                    wait(PE, tex)
                    if n >= 1:
                        wait(PE, att["no"][n - 1])
                    for s in range(7):
                        lhs = vbf[:, base + s, :] if s < 5 else vcs[:, s - 5, :]
                        PE.matmul(Z[:, 0:128], lhs, pt[:, s * 128:(s + 1) * 128], start=(s == 0), stop=(s == 6))
                    for s in range(7):
                        mm = PE.matmul(Z[:, 128:256], onesb[:], pt[:, s * 128:(s + 1) * 128], start=(s == 0), stop=(s == 6))
                    tpv = inc(mm, s_pv)
                    att["pv"][n] = tpv
                    ri = rinv[n % 2]
                    wait(DVE, tpv)
                    tri = inc(DVE.reciprocal(out=ri[:], in_=Z[:, 128:256]), s_ri)
                    wait(DVE, tri)
                    tno = inc(DVE.tensor_tensor(out=mixst[:, i, :], in0=Z[:, 0:128], in1=ri[:], op=ALU.mult), s_no)
                    att["no"][n] = tno
                wait(SP, tno)
                att["mx"] = dma("sync", mix_scr.rearrange("tb p kc t -> p tb kc t")[:, :, j, :], mixst[:], s_mx)
                att["att_done"] = tpv
                att_hist[j] = tpv
            b.barrier(dummies)

        with ExitStack() as es:
            es.enter_context(b.scope())
            def sb(name, shape, dt):
                return es.enter_context(nc.sbuf_tensor(name, shape, dt))
            uT = [sb(f"uT{i}", [128, T], BF16) for i in range(4)]
            cst = sb("cst", [128, 4, 1024], BF16)
            abst = [sb(f"abst{i}", [128, 1024], BF16) for i in range(2)]
            s_u = b.sem("u"); s_cs = b.semg("cs"); s_cd = b.sem("cd"); s_ab = b.sem("ab"); s_abo = [b.sem("abo"), b.sem("abo")]
            t_cs = dma("gpsimd", cst[:], cs_in.rearrange("(c p) n -> p c n", p=128), s_cs)
            cd = {"n": 0, "ev": {}, "out": {}, "last_mm": None}

            def make_ep_u(c):
                def ep(tg, bank, tk):
                    wait(ACT, tk)
                    if tg == 0:
                        wait(ACT, cd["last_mm"])
                    return [inc(ACT.activation(out=uT[c][:, tg * 512:(tg + 1) * 512], in_=bank[:], func=AF.Copy), s_u)]
                return ep

            for g4 in range(4):
                for c in range(4):
                    ws_unit(w_in, 6144 + g4 * 512 + c * 128, make_ep_u(c))
                last_g = state["g"] - 1
                for gg in range(last_g - 15, last_g + 1):
                    for tk in grp_rel[gg]:
                        wait(PE, tk)
                wait(PE, t_cs)
                for tb in range(NTB):
                    n = cd["n"]; cd["n"] += 1
                    A_, B_ = BK[3], BK[4]
                    if n >= 1:
                        wait(PE, cd["ev"][n - 1])
                    for c in range(4):
                        PE.matmul(A_[:], uT[c][:, tb * 128:(tb + 1) * 128], cst[:, c, 0:512], start=(c == 0), stop=(c == 3))
                    for c in range(4):
                        mm = PE.matmul(B_[:], uT[c][:, tb * 128:(tb + 1) * 128], cst[:, c, 512:1024], start=(c == 0), stop=(c == 3))
                    tmm = inc(mm, s_cd)
                    cd["last_mm"] = tmm
                    ab = abst[n % 2]
                    wait(ACT, tmm); wait(DVE, tmm)
                    if n >= 2:
                        wait(ACT, cd["out"][n - 2]); wait(DVE, cd["out"][n - 2])
                    ta = inc(ACT.activation(out=ab[:, 0:512], in_=A_[:], func=AF.Copy), s_ab)
                    td = inc(DVE.tensor_copy(out=ab[:, 512:1024], in_=B_[:]), s_ab)
                    cd["ev"][n] = td
                    wait(SP, td)
                    cd["out"][n] = dma("sync", ab_scr[g4, tb], ab[:], s_abo[n % 2])
            b.barrier(dummies)

    if stop == 2:
        return nc
    with ExitStack() as es:
        es.enter_context(b.scope())
        def sb(name, shape, dt):
            return es.enter_context(nc.sbuf_tensor(name, shape, dt))
        ctt = sb("ctt", [128, NTB, T], BF16)
        stt = sb("stt", [128, NTB, T], BF16)
        abt = [sb(f"abt{i}", [128, NTB, 1024], BF16) for i in range(2)]
        mxf = [sb(f"mxf{i}", [128, T], BF16) for i in range(2)]
        s_c = b.semg("c"); s_ab = b.sem("ab"); s_mm = b.sem("mm"); s_ev = b.sem("ev"); s_o = [b.sem("o"), b.sem("o")]
        tc1 = dma("gpsimd", ctt[:], ct_in.rearrange("(tb p) n -> p tb n", p=128), s_c)
        tc2 = dma("gpsimd", stt[:], st_in.rearrange("(tb p) n -> p tb n", p=128), s_c)
        wait(PE, tc2)
        gcount = 0
        mm_tok = {}
        ev_tok = {}
        o_tok = {}
        ab_last = {}
        ci = 0
        for g4 in range(4):
            if g4 >= 2:
                wait(SP, ab_last[g4 - 2])
            tab = dma("sync", abt[g4 % 2][:], ab_scr[g4].rearrange("tb p n -> p tb n"), s_ab)
            wait(PE, tab)
            for c in range(4):
                mx = mxf[ci % 2]
                for tg in range(4):
                    bank = BK[gcount % 2]
                    if gcount >= 2:
                        wait(PE, ev_tok[gcount - 2])
                    for tb in range(NTB):
                        PE.matmul(bank[:], abt[g4 % 2][:, tb, c * 128:(c + 1) * 128], ctt[:, tb, tg * 512:(tg + 1) * 512], start=(tb == 0), stop=False)
                    for tb in range(NTB):
                        mm = PE.matmul(bank[:], abt[g4 % 2][:, tb, 512 + c * 128:512 + (c + 1) * 128], stt[:, tb, tg * 512:(tg + 1) * 512], start=False, stop=(tb == NTB - 1))
                    tmm = inc(mm, s_mm)
                    ab_last[g4] = tmm
                    wait(ACT, tmm)
                    if tg == 0 and ci >= 2:
                        wait(ACT, o_tok[ci - 2])
                    ev_tok[gcount] = inc(ACT.activation(out=mx[:, tg * 512:(tg + 1) * 512], in_=bank[:], func=AF.Copy), s_ev)
                    gcount += 1
                wait(SP, ev_tok[gcount - 1])
                o_tok[ci] = dma("sync", mix_scr.rearrange("tb p kc t -> p tb kc t")[:, :, 16 + g4 * 4 + c, :], mx[:].rearrange("p (tb t) -> p tb t", t=128), s_o[ci % 2])
                ci += 1
        b.barrier(dummies)

    def as_gemm(tag, KC, NB, a_scr, W, gate_off, res_src, res_dst, ssq=None):
        ncb = D // NB
        with ExitStack() as es:
            es.enter_context(b.scope())
            def sb(name, shape, dt):
                return es.enter_context(nc.sbuf_tensor(name + tag, shape, dt))
            gb = sb("gb", [128, D], F32)
            wb = [sb(f"wb{i}", [128, KC, NB], BF16) for i in range(2)]
            at = [sb(f"at{i}", [128, KC, 128], BF16) for i in range(2)]
            xi = [sb(f"xi{i}", [128, NB], F32) for i in range(2)]
            xo = [sb(f"xo{i}", [128, NB], F32) for i in range(2)]
            junk = sb("junk", [128, NB], BF16)
            s_g = b.sem("g"); s_w = [b.semg("w0"), b.semg("w1")]; s_a = [b.sem("a0"), b.sem("a1")]
            s_x = [b.sem("x0"), b.sem("x1")]; s_mm = b.sem("mm")
            s_m = b.sem("m"); s_ad = b.sem("ad"); s_o = [b.sem("o"), b.sem("o")]; s_sq = b.sem("sq")
            tg_ = dma("sync", gb[:], mod_row[:, gate_off:gate_off + D].partition_broadcast(128), s_g)
            wait(DVE, tg_)
            mm_tok = {}; m_tok = {}; ad_tok = {}; o_tok = {}; sq_tok = {}; cb_last = {}
            ta_tok = {}; tx_tok = {}; tw_tok = {}
            NI = ncb * NTB

            def loads(idx):
                cb, tb = divmod(idx, NTB)
                if idx >= 2:
                    wait(SP, mm_tok[idx - 2])
                ta_tok[idx] = dma("sync", at[idx % 2][:], a_scr[tb], s_a[idx % 2])
                if idx >= 2:
                    wait(SP, ad_tok[idx - 2])
                tx_tok[idx] = dma("sync", xi[idx % 2][:], res_src[tb * 128:(tb + 1) * 128, cb * NB:(cb + 1) * NB], s_x[idx % 2])

            def wloads(cb):
                if cb >= 2:
                    wait(POOL, cb_last[cb - 2])
                tw_tok[cb] = wload(wb[cb % 2][:], W, cb * NB, NB, s_w[cb % 2])

            wloads(0)
            loads(0)
            for idx in range(NI):
                cb, tb = divmod(idx, NTB)
                if tb == 0 and cb + 1 < ncb:
                    wloads(cb + 1)
                if idx + 1 < NI:
                    loads(idx + 1)
                bank = BK[idx % 2]
                wait(PE, ta_tok[idx])
                if tb == 0:
                    wait(PE, tw_tok[cb])
                if idx >= 2:
                    wait(PE, m_tok[idx - 2])
                for k in range(KC):
                    mm = PE.matmul(bank[:, 0:NB], at[idx % 2][:, k, :], wb[cb % 2][:, k, :], start=(k == 0), stop=(k == KC - 1))
                mm_tok[idx] = inc(mm, s_mm)
                cb_last[cb] = mm_tok[idx]
                wait(DVE, mm_tok[idx]); wait(DVE, tx_tok[idx])
                if idx >= 2:
                    wait(DVE, o_tok[idx - 2])
                    if ssq is not None:
                        wait(DVE, sq_tok[idx - 2])
                m_tok[idx] = inc(DVE.tensor_tensor(out=xo[idx % 2][:], in0=bank[:, 0:NB], in1=gb[:, cb * NB:(cb + 1) * NB], op=ALU.mult), s_m)
                wait(DVE, m_tok[idx])
                ad_tok[idx] = inc(DVE.tensor_tensor(out=xo[idx % 2][:], in0=xo[idx % 2][:], in1=xi[idx % 2][:], op=ALU.add), s_ad)
                if ssq is not None:
                    wait(ACT, ad_tok[idx])
                    sq_tok[idx] = inc(ACT.activation(out=junk[:], in_=xo[idx % 2][:], func=AF.Square, accum_out=ssq[:, cb * NTB + tb:cb * NTB + tb + 1]), s_sq)
                wait(SP, ad_tok[idx])
                o_tok[idx] = dma("sync", res_dst[tb * 128:(tb + 1) * 128, cb * NB:(cb + 1) * NB], xo[idx % 2][:], s_o[idx % 2])
            b.barrier(dummies)

    if stop == 25:
        return nc
    as_gemm("op", 32, 512, mix_scr, w_out, 2 * D, x, x1_scr)
    if stop == 3:
        return nc

    with ExitStack() as es_h:
        h2T = es_h.enter_context(nc.sbuf_tensor("h2T", [128, 32, T], BF16))
        phase_norm(x1_scr, a2, b2, h2T, "n2")
        with ExitStack() as es:
            es.enter_context(b.scope())
            def sb(name, shape, dt):
                return es.enter_context(nc.sbuf_tensor(name, shape, dt))
            NS = 4
            wr = [sb(f"fwr{i}", [128, 32, 128], BF16) for i in range(NS)]
            sgt = [sb(f"sgt{i}", [128, 512], F32) for i in range(2)]
            actst = [sb(f"actst{i}", [128, T], BF16) for i in range(2)]
            s_wr = [b.semg(f"fwr{i}") for i in range(NS)]
            s_mm = b.sem("mm"); s_sg = b.sem("sg"); s_ac = b.sem("ac"); s_o = [b.sem("o"), b.sem("o")]
            f_last = {}
            mm_tok = {}
            sg_tok = {}
            ac_tok = {}
            o_tok = {}
            gidx = 0
            for f in range(NF):
                sl_g = (2 * f) % NS
                sl_u = (2 * f + 1) % NS
                if f >= 2:
                    wait(POOL, f_last[f - 2])
                twg = wload(wr[sl_g][:], w_gate, f * 128, 128, s_wr[sl_g])
                twu = wload(wr[sl_u][:], w_up, f * 128, 128, s_wr[sl_u])
                ast = actst[f % 2]
                for tg in range(4):
                    Bg = BK[(gidx % 2) * 2]
                    Bu = BK[(gidx % 2) * 2 + 1]
                    if tg == 0:
                        wait(PE, twg); wait(PE, twu)
                    if gidx >= 2:
                        wait(PE, ac_tok[gidx - 2])
                    for k in range(32):
                        PE.matmul(Bg[:], wr[sl_g][:, k, :], h2T[:, k, tg * 512:(tg + 1) * 512], start=(k == 0), stop=(k == 31))
                    for k in range(32):
                        mm = PE.matmul(Bu[:], wr[sl_u][:, k, :], h2T[:, k, tg * 512:(tg + 1) * 512], start=(k == 0), stop=(k == 31))
                    mm_tok[gidx] = inc(mm, s_mm)
                    f_last[f] = mm_tok[gidx]
                    sg = sgt[gidx % 2]
                    wait(ACT, mm_tok[gidx])
                    if gidx >= 2:
                        wait(ACT, ac_tok[gidx - 2])
                    sg_tok[gidx] = inc(ACT.activation(out=sg[:], in_=Bg[:], func=AF.Silu), s_sg)
                    wait(DVE, sg_tok[gidx])
                    if tg == 0 and f >= 2:
                        wait(DVE, o_tok[f - 2])
                    ac_tok[gidx] = inc(DVE.tensor_tensor(out=ast[:, tg * 512:(tg + 1) * 512], in0=Bu[:], in1=sg[:], op=ALU.mult), s_ac)
                    gidx += 1
                wait(SP, ac_tok[gidx - 1])
                o_tok[f] = dma("sync", act_scr.rearrange("tb p f t -> p tb f t")[:, :, f, :], ast[:].rearrange("p (tb t) -> p tb t", t=128), s_o[f % 2])
            b.barrier(dummies)

    if stop == 5:
        return nc
    NB6 = 256
    ssq = nc.alloc_sbuf_tensor("ssq", [128, (D // NB6) * NTB], F32)
    DVE.memset(ssq[:], 0.0)
    b.barrier(dummies)
    as_gemm("dn", NF, NB6, act_scr, w_down, 5 * D, x1_scr, x2_scr, ssq=ssq)

    with ExitStack() as es:
        es.enter_context(b.scope())
        def sb(name, shape, dt):
            return es.enter_context(nc.sbuf_tensor(name, shape, dt))
        fgb = sb("fgb", [128, D], F32)
        xt = [sb(f"fx{i}", [128, D], F32) for i in range(2)]
        yo = [sb(f"fy{i}", [128, D], F32) for i in range(2)]
        tot = sb("tot", [128, NTB], F32)
        sq = sb("fsq", [128, NTB], F32)
        rs = sb("frs", [128, NTB], F32)
        s_ld = [b.sem("ld"), b.sem("ld")]; s_t = b.sem("t"); s_y = b.sem("y"); s_o = [b.sem("o"), b.sem("o")]; s_fg = b.sem("fg")
        tf = dma("sync", fgb[:], fg.partition_broadcast(128), s_fg)
        ncb = D // NB6
        t = inc(DVE.tensor_reduce(out=tot[:], in_=ssq[:].rearrange("p (cb tb) -> p tb cb", tb=NTB), axis=mybir.AxisListType.X, op=ALU.add), s_t)
        wait(ACT, t)
        t = inc(ACT.activation(out=sq[:], in_=tot[:], func=AF.Sqrt, bias=epsc[:], scale=1.0 / D), s_t)
        wait(DVE, t)
        t = inc(DVE.reciprocal(out=rs[:], in_=sq[:]), s_t)
        wait(DVE, t); wait(DVE, tf)
        y_tok = {}
        o_tok = {}
        ld_tok = {}

        def fload(tb):
            if tb >= 2:
                wait(SP, y_tok[tb - 2])
            ld_tok[tb] = dma("sync", xt[tb % 2][:], x2_scr[tb * 128:(tb + 1) * 128, :], s_ld[tb % 2])

        fload(0)
        for tb in range(NTB):
            if tb + 1 < NTB:
                fload(tb + 1)
            wait(DVE, ld_tok[tb])
            if tb >= 2:
                wait(DVE, o_tok[tb - 2])
            y_tok[tb] = inc(DVE.scalar_tensor_tensor(out=yo[tb % 2][:], in0=xt[tb % 2][:], scalar=rs[:, tb:tb + 1], in1=fgb[:], op0=ALU.mult, op1=ALU.mult), s_y)
            wait(SP, y_tok[tb])
            o_tok[tb] = dma("sync", y[tb * 128:(tb + 1) * 128, :], yo[tb % 2][:], s_o[tb % 2])
        b.barrier(dummies)
    return nc


def _bias_tables(rpb):
    reps = [0, 1, 2, 3, 14, 15]
    sb_ = np.empty((16, 128, 6, 5, 128), np.float32)
    pb_ = np.empty((16, 128, 6, 5, 128), np.float32)
    ql = np.arange(128)
    kl = np.arange(128)
    for ti, i in enumerate(reps):
        base = min(max(i - 2, 0), 11)
        qr = 2 * i + ql // 64
        qc = ql % 64
        rs_ = np.clip(qr - 4, 0, 24)
        cs_ = np.clip(qc - 8, 0, 48)
        for s in range(5):
            kb = base + s
            kr = 2 * kb + kl // 64
            kc = kl % 64
            valid = ((kr[:, None] >= rs_[None, :]) & (kr[:, None] < rs_[None, :] + 8)
                     & (kc[:, None] >= cs_[None, :]) & (kc[:, None] < cs_[None, :] + 16))
            dr = np.clip(kr[:, None] - qr[None, :] + 7, 0, 14)
            dc = np.clip(kc[:, None] - qc[None, :] + 15, 0, 30)
            vals = rpb[:, dr, dc]
            sb_[:, :, ti, s, :] = np.where(valid[None], vals, np.float32(NEG))
            pb_[:, :, ti, s, :] = 0.0 if (kb // 2 == i // 2) else NEG
    return sb_.reshape(16, 128, -1), pb_.reshape(16, 128, -1)


def _dft_tables():
    def cs(n, scale):
        idx = np.arange(n)
        m = (idx[:, None] * idx[None, :]) % n
        ang = 2.0 * np.pi * m / n
        return (np.cos(ang) * scale), (np.sin(ang) * scale)
    c2048, s2048 = cs(2048, 1.0 / np.sqrt(2048 * 512.0))
    c256, s256 = cs(256, 1.0 / np.sqrt(256 * 512.0))
    ctp = np.zeros((2048, 2048)); stp = np.zeros((2048, 2048))
    for bq in range(8):
        ctp[bq * 256:(bq + 1) * 256, bq * 256:(bq + 1) * 256] = c256
        stp[bq * 256:(bq + 1) * 256, bq * 256:(bq + 1) * 256] = s256
    cc, sc = cs(512, 1.0)
    csm = np.concatenate([cc, -sc], axis=1)
    f = np.float32
    return c2048.astype(f), s2048.astype(f), ctp.astype(f), stp.astype(f), csm.astype(f)


def _prep(x_prompt, x_sample, cache_k, cache_v, c, c_ctx, w_ada, b_ada, norm1_g, w_in,
          rpb, w_out, norm2_g, w_gate, w_up, w_down, final_g):
    f = np.float32
    A = lambda a: np.ascontiguousarray(np.asarray(a), dtype=f)
    x_prompt = A(x_prompt); x_sample = A(x_sample)
    sbias, pbias = _bias_tables(A(rpb)[0])
    c2048, s2048, ctp, stp, csm = _dft_tables()
    common = {
        "w_ada": A(w_ada)[0], "b_ada": A(b_ada)[0].reshape(1, -1), "n1g": A(norm1_g)[0].reshape(32, 128),
        "n2g": A(norm2_g)[0].reshape(32, 128), "fg": A(final_g).reshape(1, -1), "w_in": A(w_in)[0],
        "w_out": A(w_out)[0], "w_gate": A(w_gate)[0], "w_up": A(w_up)[0], "w_down": A(w_down)[0],
        "cs": csm, "ident": np.eye(128, dtype=f),
    }
    zkv = np.zeros((256, 2048), f)
    in_maps = []
    for core in range(8):
        m = dict(common)
        if core < 4:
            m["x"] = x_prompt[core * 8:(core + 1) * 8].reshape(T, D)
            m["cvec"] = A(c_ctx).reshape(32, 128)
            m["kc"] = zkv; m["vc"] = zkv
            m["bias"] = pbias
            m["ctxb"] = np.full((128, 1), NEG, f)
            m["ct"] = ctp; m["st"] = stp
        else:
            bi = core - 4
            m["x"] = x_sample[bi]
            m["cvec"] = A(c)[bi].reshape(32, 128)
            m["kc"] = A(cache_k)[bi, 0].reshape(256, 2048)
            m["vc"] = A(cache_v)[bi, 0].reshape(256, 2048)
            m["bias"] = sbias
            m["ctxb"] = np.zeros((128, 1), f)
            m["ct"] = c2048; m["st"] = s2048
        in_maps.append(m)
    return in_maps


def kernel(**inputs):
    f = np.float32
    in_maps = _prep(**inputs)
    nc = build()
    res = run_bass_kernel_spmd(nc, in_maps, core_ids=list(range(8)))
    r = res.results
    y_prompt = np.stack([r[i]["y"] for i in range(4)]).reshape(32, 256, D)
    y_sample = np.stack([r[4 + i]["y"] for i in range(4)])
    nkk = np.stack([r[i]["nk"] for i in range(4)]).reshape(32, 1, 256, 16, 128)
    nvv = np.stack([r[i]["nv"] for i in range(4)]).reshape(32, 1, 256, 16, 128)
    return (y_prompt.astype(f), y_sample.astype(f), nkk.astype(f), nvv.astype(f))
```

```python
from contextlib import ExitStack, contextmanager
import numpy as np
import concourse.bass as bass
import concourse.mybir as mybir
from concourse.bass_utils import run_bass_kernel_spmd

F32 = mybir.dt.float32
BF16 = mybir.dt.bfloat16
AF = mybir.ActivationFunctionType
ALU = mybir.AluOpType

D = 4096
T = 2048
NTB = 16
DFF = 11008
NF = 86
EPS = 1e-6
SCALE = 128.0 ** -0.5
NEG = -30000.0
import os as _os
NOPOOL = bool(_os.environ.get('NOPOOL'))


class Sem:
    def __init__(self, nc, name):
        self.h = nc.alloc_semaphore(name)
        self.v = 0


class Lazy:
    def __init__(self, f):
        self._f = f
        self._v = None

    def get(self):
        if self._v is None:
            self._v = self._f()
        return self._v

    def __getitem__(self, k):
        return self.get()[k]

    def __getattr__(self, n):
        return getattr(self.get(), n)


def _un(a):
    return a.get() if isinstance(a, Lazy) else a


class B:
    def __init__(self):
        self.nc = bass.Bass("TRN2", target_bir_lowering=False)
        self.nsem = 0
        self.dma_sems = {"sync": set(), "gpsimd": set()}
        self.bar = Sem(self.nc, "bar")
        self.bar_n = 0
        self.pool = []
        self.scopes = []

    def sem(self, name):
        self.nsem += 1
        sm = self.pool.pop() if (self.pool and not NOPOOL) else Sem(self.nc, f"{name}_{self.nsem}")
        if self.scopes:
            self.scopes[-1].append(sm)
        return sm

    def semg(self, name):
        self.nsem += 1
        return Sem(self.nc, f"{name}_{self.nsem}")

    @contextmanager
    def scope(self):
        self.scopes.append([])
        yield
        self.pool.extend(self.scopes.pop())

    @staticmethod
    def inc(instr, sem, n=1):
        instr.then_inc(sem.h, n)
        sem.v += n
        return (sem, sem.v)

    @staticmethod
    def wait(eng, tok):
        if tok is not None:
            eng.wait_ge(tok[0].h, tok[1])

    def dma(self, q, out, in_, sem):
        eng = self.nc.sync if q == "sync" else self.nc.gpsimd
        ins = eng.dma_start(out=_un(out), in_=_un(in_))
        self.dma_sems[q].add(sem)
        return self.inc(ins, sem, 16)

    def barrier(self, dummies):
        nc = self.nc
        self.bar_n += 1
        dps, idb, dv, da, dg = dummies
        self.inc(nc.vector.memset(dv[:], 0.0), self.bar)
        self.inc(nc.scalar.activation(out=da[:, 0:1], in_=da[:, 1:2], func=AF.Copy), self.bar)
        for s in self.dma_sems["gpsimd"]:
            nc.gpsimd.wait_ge(s.h, s.v)
        self.inc(nc.gpsimd.memset(dg[:], 0.0), self.bar)
        for s in self.dma_sems["sync"]:
            nc.sync.wait_ge(s.h, s.v)
        nc.sync.sem_inc(self.bar.h, 1)
        self.bar.v += 1
        nc.tensor.wait_ge(self.bar.h, self.bar.v)
        self.inc(nc.tensor.matmul(dps[:, 511:512], idb[:], idb[:, 0:1], start=True, stop=True), self.bar)
        for e in (nc.tensor, nc.vector, nc.scalar, nc.gpsimd, nc.sync):
            e.wait_ge(self.bar.h, self.bar.v)


def build(stop=None, skip0=False):
    b = B()
    nc = b.nc
    inc, wait, dma = b.inc, b.wait, b.dma
    PE, DVE, ACT, POOL, SP = nc.tensor, nc.vector, nc.scalar, nc.gpsimd, nc.sync

    def din(name, shape):
        return Lazy(lambda: nc.dram_tensor(name, shape, F32, kind="ExternalInput").ap())

    def dout(name, shape):
        return Lazy(lambda: nc.dram_tensor(name, shape, F32, kind="ExternalOutput").ap())

    x = din("x", [T, D])
    cvec = din("cvec", [32, 128])
    w_ada = din("w_ada", [D, 6 * D])
    b_ada = din("b_ada", [1, 6 * D])
    n1g = din("n1g", [32, 128])
    n2g = din("n2g", [32, 128])
    fg = din("fg", [1, D])
    w_in = din("w_in", [D, 2 * D])
    w_out = din("w_out", [D, D])
    w_gate = din("w_gate", [D, DFF])
    w_up = din("w_up", [D, DFF])
    w_down = din("w_down", [DFF, D])
    kc_in = din("kc", [256, 2048])
    vc_in = din("vc", [256, 2048])
    bias_in = din("bias", [16, 128, 6 * 5 * 128])
    ctxb_in = din("ctxb", [128, 1])
    ct_in = din("ct", [T, T])
    st_in = din("st", [T, T])
    cs_in = din("cs", [512, 1024])
    ident_in = din("ident", [128, 128])
    y = dout("y", [T, D])
    nk = dout("nk", [T, 2048])
    nv = dout("nv", [T, 2048])
    mod_row = nc.dram_tensor("mod_row", [1, 6 * D], F32).ap()
    DBG = bool(_os.environ.get("DBG"))
    skind = "ExternalOutput" if DBG else "Internal"
    mix_scr = nc.dram_tensor("mix_scr", [NTB, 128, 32, 128], BF16, kind=skind).ap()
    ab_scr = nc.dram_tensor("ab_scr", [4, NTB, 128, 1024], BF16, kind=skind).ap()
    x1_scr = nc.dram_tensor("x1_scr", [T, D], F32, kind=skind).ap()
    act_scr = nc.dram_tensor("act_scr", [NTB, 128, NF, 128], BF16, kind=skind).ap()
    x2_scr = nc.dram_tensor("x2_scr", [T, D], F32, kind=skind).ap()

    idf = nc.alloc_sbuf_tensor("idf", [128, 128], F32)
    idb = nc.alloc_sbuf_tensor("idb", [128, 128], BF16)
    onesb = nc.alloc_sbuf_tensor("onesb", [128, 128], BF16)
    modT = nc.alloc_sbuf_tensor("modT", [128, 192], F32)
    a1 = nc.alloc_sbuf_tensor("a1", [128, 32], F32)
    a2 = nc.alloc_sbuf_tensor("a2", [128, 32], F32)
    ctxb = nc.alloc_sbuf_tensor("ctxb_sb", [128, 1], F32)
    epsc = nc.alloc_sbuf_tensor("epsc", [128, 1], F32)
    dv = nc.alloc_sbuf_tensor("dv", [128, 1], F32)
    da = nc.alloc_sbuf_tensor("da", [128, 2], F32)
    dg = nc.alloc_sbuf_tensor("dg", [128, 1], F32)
    BK = [nc.alloc_psum_tensor(f"bk{i}", [128, 512], F32) for i in range(6)]
    PB = [nc.alloc_psum_tensor(f"pb{i}", [128, 1024], BF16) for i in range(2)]
    dummies = (BK[5], idb, dv, da, dg)

    s0 = b.sem("init")
    dma("sync", idf[:], ident_in, s0)
    s0g = b.semg("initg")
    dma("gpsimd", idb[:], ident_in, s0g)
    dma("sync", ctxb[:], ctxb_in, s0)
    for e in (PE, DVE, ACT, POOL):
        e.wait_ge(s0.h, s0.v)
        e.wait_ge(s0g.h, s0g.v)
    DVE.memset(onesb[:], 1.0)
    DVE.memset(epsc[:], EPS)
    s0c = b.sem("initc")
    t = inc(DVE.memset(da[:], 0.0), s0c)
    for e in (PE, ACT, POOL, SP):
        wait(e, t)
    b.barrier(dummies)

    if skip0:
        DVE.memset(modT[:], 0.5)
        DVE.memset(a1[:], 1.5)
        t = inc(DVE.memset(a2[:], 1.5), s0c)
        wait(SP, t)
        dma("sync", mod_row.rearrange("o (p j) -> (o p) j", p=128), modT[:], s0)
        for e in (PE, ACT, POOL, SP):
            wait(e, t)
        b.barrier(dummies)
    if not skip0:
        with ExitStack() as es:
            es.enter_context(b.scope())
            def sb(name, shape, dt):
                return es.enter_context(nc.sbuf_tensor(name, shape, dt))
            cv = sb("cv", [32, 128], F32)
            cvT = sb("cvT", [128, 32], BF16)
            ring = [sb(f"adar{i}", [128, 32, 512], BF16) for i in range(3)]
            brow = [sb(f"brow{i}", [1, 512], F32) for i in range(2)]
            rowsb = [sb(f"rowsb{i}", [1, 512], F32) for i in range(2)]
            mr = [sb(f"mr{i}", [96, 128], F32) for i in range(2)]
            gt = [sb(f"gt{i}", [32, 128], F32) for i in range(2)]
            s_ld = b.sem("ld"); s_w = [b.semg("w") for _ in range(3)]; s_b = [b.sem("b") for _ in range(2)]; s_mm = b.sem("mm"); s_row = b.sem("row")
            s_st = [b.sem("st") for _ in range(2)]; s_a = b.sem("a")
            t = dma("sync", cv[:], cvec, s_ld)
            wait(PE, t)
            t = inc(PE.transpose(BK[0][:, 0:32], cv[:], idf[0:32, 0:32]), s_a)
            wait(ACT, t)
            t_cvT = inc(ACT.activation(out=cvT[:], in_=BK[0][:, 0:32], func=AF.Silu), s_a)
            wait(PE, t_cvT)
            NU = 48
            mm_tok = [None] * NU
            row_tok = [None] * NU
            st_tok = [None] * NU
            for u in range(NU):
                if u >= 3:
                    wait(POOL, mm_tok[u - 3])
                tw = dma("gpsimd", ring[u % 3][:], w_ada[:, u * 512:(u + 1) * 512].rearrange("(kc p) n -> p kc n", p=128), s_w[u % 3])
                if u >= 2:
                    wait(SP, row_tok[u - 2])
                tb_ = dma("sync", brow[u % 2][:], b_ada[:, u * 512:(u + 1) * 512], s_b[u % 2])
                wait(PE, tw)
                if u >= 2:
                    wait(PE, row_tok[u - 2])
                for k in range(32):
                    mm = PE.matmul(BK[u % 2][0:1, :], cvT[:, k:k + 1], ring[u % 3][:, k, :], start=(k == 0), stop=(k == 31))
                mm_tok[u] = inc(mm, s_mm)
                wait(DVE, mm_tok[u]); wait(DVE, tb_)
                if u >= 2:
                    wait(DVE, st_tok[u - 2])
                row_tok[u] = inc(DVE.tensor_tensor(out=rowsb[u % 2][:], in0=BK[u % 2][0:1, :], in1=brow[u % 2][:], op=ALU.add), s_row)
                wait(SP, row_tok[u])
                st_tok[u] = dma("sync", mod_row[:, u * 512:(u + 1) * 512], rowsb[u % 2][:], s_st[u % 2])
            wait(SP, st_tok[NU - 1]); wait(SP, st_tok[NU - 2])
            mrv = mod_row.rearrange("o (j p) -> (o j) p", p=128)
            t0_ = dma("sync", mr[0][:], mrv[0:96, :], s_ld)
            t1_ = dma("sync", mr[1][:], mrv[96:192, :], s_ld)
            t2_ = dma("sync", gt[0][:], n1g, s_ld)
            t3_ = dma("sync", gt[1][:], n2g, s_ld)
            wait(PE, t3_)
            PE.transpose(BK[2][:, 0:96], mr[0][:], idf[0:96, 0:96])
            PE.transpose(BK[2][:, 96:192], mr[1][:], idf[0:96, 0:96])
            PE.transpose(BK[2][:, 192:224], gt[0][:], idf[0:32, 0:32])
            t = inc(PE.transpose(BK[2][:, 224:256], gt[1][:], idf[0:32, 0:32]), s_a)
            wait(DVE, t)
            t = inc(DVE.tensor_copy(out=modT[:], in_=BK[2][:, 0:192]), s_a)
            wait(DVE, t)
            DVE.scalar_tensor_tensor(out=a1[:], in0=modT[:, 32:64], scalar=1.0, in1=BK[2][:, 192:224], op0=ALU.add, op1=ALU.mult)
            DVE.scalar_tensor_tensor(out=a2[:], in0=modT[:, 128:160], scalar=1.0, in1=BK[2][:, 224:256], op0=ALU.add, op1=ALU.mult)
            b.barrier(dummies)
    b1 = modT[:, 0:32]
    b2 = modT[:, 96:128]

    def dbg_out(name, src_ap, shape):
        o = nc.dram_tensor(name, shape, F32, kind="ExternalOutput").ap()
        sd = b.sem("dbg")
        tk = dma("sync", o, src_ap, sd)
        wait(SP, tk)

    if stop == 0:
        dbg_out("d_modT", modT[:], [128, 192])
        dbg_out("d_a1", a1[:], [128, 32])
        return nc

    def phase_norm(src, av, bv, hT, tag):
        with ExitStack() as es:
            es.enter_context(b.scope())
            def sb(name, shape, dt):
                return es.enter_context(nc.sbuf_tensor(name + tag, shape, dt))
            xt = [sb(f"xt{i}", [128, D], F32) for i in range(2)]
            xh = [sb(f"xh{i}", [128, D], BF16) for i in range(2)]
            junk = sb("junk", [128, D], BF16)
            ss = sb("ss", [128, NTB], F32)
            sq = sb("sq", [128, NTB], F32)
            rs = sb("rs", [128, NTB], F32)
            s_ld = [b.sem("ld"), b.sem("ld")]; s_sq = b.sem("sq"); s_rt = b.sem("rt"); s_r = b.sem("r"); s_xh = b.sem("xh")
            s_tr = b.sem("tr"); s_ea = b.sem("ea"); s_ed = b.sem("ed")
            xh_tok = [None] * NTB
            tr_last = [None] * NTB
            ev_tok = {}
            t = inc(DVE.memset(ss[:], 0.0), s_r)
            wait(ACT, t)
            gi = 0
            LVL = int(_os.environ.get("LVL", "9"))
            sq_tok = [None] * NTB
            for tb in range(NTB):
                if tb >= 2:
                    wait(SP, xh_tok[tb - 2] if LVL >= 2 else sq_tok[tb - 2])
                tl = dma("sync", xt[tb % 2][:], src[tb * 128:(tb + 1) * 128, :], s_ld[tb % 2])
                wait(ACT, tl)
                t = inc(ACT.activation(out=junk[:], in_=xt[tb % 2][:], func=AF.Square, accum_out=ss[:, tb:tb + 1]), s_sq)
                sq_tok[tb] = t
                if LVL < 2:
                    continue
                wait(ACT, t)
                t = inc(ACT.activation(out=sq[:, tb:tb + 1], in_=ss[:, tb:tb + 1], func=AF.Sqrt, bias=epsc[:], scale=1.0 / D), s_rt)
                wait(DVE, t)
                t = inc(DVE.reciprocal(out=rs[:, tb:tb + 1], in_=sq[:, tb:tb + 1]), s_r)
                wait(DVE, t)
                if tb >= 2 and LVL >= 3:
                    wait(DVE, tr_last[tb - 2])
                xh_tok[tb] = inc(DVE.tensor_scalar(out=xh[tb % 2][:], in0=xt[tb % 2][:], scalar1=rs[:, tb:tb + 1], scalar2=None, op0=ALU.mult), s_xh)
                if LVL < 3:
                    continue
                wait(PE, xh_tok[tb])
                for g in range(4):
                    pb = PB[gi % 2]
                    if gi >= 2 and LVL >= 4:
                        wait(PE, ev_tok[gi - 2])
                    for j in range(8):
                        kc = g * 8 + j
                        tr = PE.transpose(pb[:, j * 128:(j + 1) * 128], xh[tb % 2][:, kc * 128:(kc + 1) * 128], idb[:])
                    ttr = inc(tr, s_tr)
                    if g == 3:
                        tr_last[tb] = ttr
                    if LVL < 4:
                        gi += 1
                        continue
                    use_act = (gi % 2 == 0)
                    wait(ACT if use_act else DVE, ttr)
                    for j in range(8):
                        kc = g * 8 + j
                        o = hT[:, kc, tb * 128:(tb + 1) * 128]
                        i_ = pb[:, j * 128:(j + 1) * 128]
                        if use_act:
                            iv = ACT.activation(out=o, in_=i_, func=AF.Identity, bias=bv[:, kc:kc + 1], scale=av[:, kc:kc + 1])
                        else:
                            iv = DVE.tensor_scalar(out=o, in0=i_, scalar1=av[:, kc:kc + 1], scalar2=bv[:, kc:kc + 1], op0=ALU.mult, op1=ALU.add)
                    ev_tok[gi] = inc(iv, s_ea if use_act else s_ed)
                    gi += 1
            b.barrier(dummies)

    def wload(dst, W, c0, n, sem):
        return dma("gpsimd", dst, W[:, c0:c0 + n].rearrange("(kc p) n -> p kc n", p=128), sem)

    with ExitStack() as es_h:
        hT = es_h.enter_context(nc.sbuf_tensor("hT", [128, 32, T], BF16))
        phase_norm(x, a1, b1, hT, "n1")
        if stop == 1:
            with nc.sbuf_tensor("dbgh", [128, 32, 128], F32) as dbgh:
                if int(_os.environ.get('LVL', '9')) < 4:
                    DVE.memset(hT[:, :, 0:128], 1.0)
                DVE.tensor_copy(out=dbgh[:], in_=hT[:, :, 0:128])
                b.barrier(dummies)
                dbg_out("d_hT", dbgh[:], [128, 32, 128])
            return nc

        NSLOT = 3
        wr = [es_h.enter_context(nc.sbuf_tensor(f"wr{i}", [128, 32, 128], BF16)) for i in range(NSLOT)]
        s_wr = [b.semg(f"wr{i}") for i in range(NSLOT)]
        s_gm = b.sem("gm")
        state = {"u": 0, "g": 0}
        unit_last_grp = {}
        grp_rel = {}

        def ws_unit(W, c0, epilogue):
            u = state["u"]; state["u"] += 1
            slot = u % NSLOT
            if u >= NSLOT:
                wait(POOL, unit_last_grp[u - NSLOT])
            tw = wload(wr[slot][:], W, c0, 128, s_wr[slot])
            wait(PE, tw)
            for tg in range(4):
                g = state["g"]; state["g"] += 1
                bank = BK[g % 2]
                if g >= 2:
                    for tk in grp_rel[g - 2]:
                        wait(PE, tk)
                for k in range(32):
                    mm = PE.matmul(bank[:], wr[slot][:, k, :], hT[:, k, tg * 512:(tg + 1) * 512], start=(k == 0), stop=(k == 31))
                tk = inc(mm, s_gm)
                if tg == 3:
                    unit_last_grp[u] = tk
                grp_rel[g] = epilogue(tg, bank, tk)

        with ExitStack() as es:
            es.enter_context(b.scope())
            def sb(name, shape, dt):
                return es.enter_context(nc.sbuf_tensor(name, shape, dt))
            qT = sb("qT", [128, T], BF16)
            kT = sb("kT", [128, T], BF16)
            vbf = sb("vbf", [128, NTB, 128], BF16)
            fst = [sb(f"fst{i}", [128, 512], F32) for i in range(2)]
            kvo = [sb(f"kvo{i}", [128, 4, 128], F32) for i in range(2)]
            biast = sb("biast", [128, 6 * 5 * 128], F32)
            PT = [sb(f"PT{i}", [128, 7 * 128], BF16) for i in range(2)]
            rinv = [sb(f"rinv{i}", [128, 128], F32) for i in range(2)]
            mixst = sb("mixst", [128, NTB, 128], BF16)
            kcs2 = [sb(f"kcs{i}", [128, 2, 128], BF16) for i in range(2)]
            kcT = sb("kcT", [128, 256], BF16)
            vcs2 = [sb(f"vcs{i}", [128, 2, 128], BF16) for i in range(2)]
            att_hist = {}
            s_q = b.sem("q"); s_kA = b.sem("kA"); s_kD = b.sem("kD"); s_tr = b.sem("tr"); s_ko = b.sem("ko")
            s_va = b.sem("va"); s_st = [b.sem("st"), b.sem("st")]; s_bias = b.sem("bias"); s_ctx = [b.semg("ctx"), b.semg("ctx")]; s_kct = b.sem("kct")
            s_qk = b.sem("qk"); s_e1 = b.sem("e1"); s_e2 = b.sem("e2"); s_ex = b.sem("ex"); s_pv = b.sem("pv")
            s_ri = b.sem("ri"); s_no = b.sem("no"); s_mx = b.sem("mx")
            fs_n = {"n": 0}
            tr_rel = {}
            fst_rel = {}
            kvo_rel = {}
            att = {"n": 0, "pv": {}, "ex": {}, "no": {}, "mx": None, "att_done": None}

            def ep_q(tg, bank, tk):
                wait(ACT, tk)
                if tg == 0:
                    wait(ACT, att["att_done"])
                return [inc(ACT.activation(out=qT[:, tg * 512:(tg + 1) * 512], in_=bank[:], func=AF.Copy), s_q)]

            def make_ep_kv(is_k, j):
                def ep(tg, bank, tk):
                    n = fs_n["n"]; fs_n["n"] += 1
                    f = fst[n % 2]
                    wait(DVE, tk)
                    if n >= 2:
                        for tk2 in fst_rel[n - 2]:
                            wait(DVE, tk2)
                    tf = inc(DVE.tensor_copy(out=f[:], in_=bank[:]), s_kD)
                    rel = [tf]
                    fst_rel[n] = []
                    if is_k:
                        wait(ACT, tf)
                        if tg == 0:
                            wait(ACT, att["att_done"])
                        fst_rel[n].append(inc(ACT.activation(out=kT[:, tg * 512:(tg + 1) * 512], in_=f[:], func=AF.Copy), s_kA))
                    wait(PE, tf)
                    if n >= 1:
                        for tk2 in tr_rel[n - 1]:
                            wait(PE, tk2)
                    for bb in range(4):
                        tr = PE.transpose(BK[2][:, bb * 128:(bb + 1) * 128], f[:, bb * 128:(bb + 1) * 128], idf[:])
                    ttr = inc(tr, s_tr)
                    fst_rel[n].append(ttr)
                    ko = kvo[n % 2]
                    wait(DVE, ttr)
                    if n >= 2:
                        for tk2 in kvo_rel[n - 2]:
                            wait(DVE, tk2)
                    tko = inc(DVE.tensor_copy(out=ko[:].rearrange("p b d -> p (b d)"), in_=BK[2][:]), s_ko)
                    tr_rel[n] = [tko]
                    kvo_rel[n] = []
                    if not is_k:
                        wait(ACT, tko)
                        if tg == 0:
                            wait(ACT, att["att_done"])
                        kvo_rel[n].append(inc(ACT.activation(out=vbf[:, tg * 4:(tg + 1) * 4, :].rearrange("p b d -> p (b d)"), in_=ko[:].rearrange("p b d -> p (b d)"), func=AF.Copy), s_va))
                    wait(SP, tko)
                    dst = (nk if is_k else nv)[tg * 512:(tg + 1) * 512, j * 128:(j + 1) * 128].rearrange("(b p) d -> p b d", p=128)
                    kvo_rel[n].append(dma("sync", dst, ko[:], s_st[n % 2]))
                    return rel
                return ep

            def tmap(i):
                return {0: 0, 1: 1, 14: 4, 15: 5}.get(i, 2 + (i % 2))

            for j in range(16):
                wait(SP, att["att_done"])
                t_bias = dma("sync", biast[:], bias_in[j], s_bias)
                kcs = kcs2[j % 2]; vcs = vcs2[j % 2]
                wait(POOL, att_hist.get(j - 2))
                t_kc = dma("gpsimd", kcs[:], kc_in[:, j * 128:(j + 1) * 128].rearrange("(b p) d -> p b d", p=128), s_ctx[j % 2])
                t_vc = dma("gpsimd", vcs[:], vc_in[:, j * 128:(j + 1) * 128].rearrange("(b p) d -> p b d", p=128), s_ctx[j % 2])
                t_kc = t_vc
                nprev = att["n"] - 1
                for nn in (nprev, nprev - 1):
                    if nn >= 0:
                        wait(PE, att["ex"][nn]); wait(PE, att["no"][nn])
                ws_unit(w_in, j * 128, ep_q)
                ws_unit(w_in, 2048 + j * 128, make_ep_kv(True, j))
                ws_unit(w_in, 4096 + j * 128, make_ep_kv(False, j))
                wait(PE, t_kc)
                for bb in range(2):
                    tr = PE.transpose(PB[0][:, bb * 128:(bb + 1) * 128], kcs[:, bb, :], idb[:])
                ttr = inc(tr, s_kct)
                wait(ACT, ttr)
                t_kcT = inc(ACT.activation(out=kcT[:], in_=PB[0][:, 0:256], func=AF.Copy), s_kct)
                last_g = state["g"] - 1
                pend = []
                for gg in range(last_g - 11, last_g + 1):
                    pend += grp_rel[gg]
                for nn in range(fs_n["n"] - 8, fs_n["n"]):
                    pend += tr_rel[nn] + [t_ for t_ in kvo_rel[nn] if t_[0] is s_va] + [t_ for t_ in fst_rel[nn] if t_[0] is s_kA]
                for tk in pend + [t_kcT, t_vc, t_bias]:
                    wait(PE, tk)
                wait(DVE, t_bias)
                wait(DVE, att["mx"])
                sets = [(BK[3], BK[4], BK[5]), (BK[0], BK[1], BK[2])]
                n0 = att["n"]; att["n"] += 16

                def qk_stage(i):
                    n = n0 + i
                    base = min(max(i - 2, 0), 11)
                    ty = tmap(i)
                    X, Y, Z = sets[n % 2]
                    if n >= 2:
                        wait(PE, att["ex"][n - 2])
                    for s_ in range(5):
                        kb = base + s_
                        dst = X[:, s_ * 128:(s_ + 1) * 128] if s_ < 4 else Y[:, 0:128]
                        PE.matmul(dst, kT[:, kb * 128:(kb + 1) * 128], qT[:, i * 128:(i + 1) * 128], start=True, stop=True)
                    for s_ in range(2):
                        mm = PE.matmul(Y[:, (1 + s_) * 128:(2 + s_) * 128], kcT[:, s_ * 128:(s_ + 1) * 128], qT[:, i * 128:(i + 1) * 128], start=True, stop=True)
                    tqk = inc(mm, s_qk)
                    wait(DVE, tqk)
                    bo = ty * 640
                    DVE.scalar_tensor_tensor(out=X[:], in0=X[:], scalar=SCALE, in1=biast[:, bo:bo + 512], op0=ALU.mult, op1=ALU.add)
                    te = inc(DVE.scalar_tensor_tensor(out=Y[:, 0:128], in0=Y[:, 0:128], scalar=SCALE, in1=biast[:, bo + 512:bo + 640], op0=ALU.mult, op1=ALU.add), s_e1)
                    pt = PT[n % 2]
                    wait(ACT, te)
                    if n >= 2:
                        wait(ACT, att["pv"][n - 2])
                    ACT.activation(out=pt[:, 0:512], in_=X[:], func=AF.Exp)
                    ACT.activation(out=pt[:, 512:640], in_=Y[:, 0:128], func=AF.Exp)
                    att["ex"][n] = inc(ACT.activation(out=pt[:, 640:896], in_=Y[:, 128:384], func=AF.Exp, bias=ctxb[:], scale=SCALE), s_ex)

                def pv_stage(i):
                    n = n0 + i
                    base = min(max(i - 2, 0), 11)
                    X, Y, Z = sets[n % 2]
                    pt = PT[n % 2]
                    wait(PE, att["ex"][n])
                    if n >= 2:
                        wait(PE, att["no"][n - 2])
                    for s_ in range(7):
                        lhs = vbf[:, base + s_, :] if s_ < 5 else vcs[:, s_ - 5, :]
                        PE.matmul(Z[:, 0:128], lhs, pt[:, s_ * 128:(s_ + 1) * 128], start=(s_ == 0), stop=(s_ == 6))
                    for s_ in range(7):
                        mm = PE.matmul(Z[:, 128:256], onesb[:], pt[:, s_ * 128:(s_ + 1) * 128], start=(s_ == 0), stop=(s_ == 6))
                    tpv_ = inc(mm, s_pv)
                    att["pv"][n] = tpv_
                    ri = rinv[n % 2]
                    wait(DVE, tpv_)
                    tri = inc(DVE.reciprocal(out=ri[:], in_=Z[:, 128:256]), s_ri)
                    wait(DVE, tri)
                    tno_ = inc(DVE.tensor_tensor(out=mixst[:, i, :], in0=Z[:, 0:128], in1=ri[:], op=ALU.mult), s_no)
                    att["no"][n] = tno_
                    return tpv_, tno_

                qk_stage(0)
                for i in range(16):
                    if i + 1 < 16:
                        qk_stage(i + 1)
                    tpv, tno = pv_stage(i)
                wait(SP, tno)
                att["mx"] = dma("sync", mix_scr.rearrange("tb p kc t -> p tb kc t")[:, :, j, :], mixst[:], s_mx)
                att["att_done"] = tpv
                att_hist[j] = tpv
            b.barrier(dummies)

        with ExitStack() as es:
            es.enter_context(b.scope())
            def sb(name, shape, dt):
                return es.enter_context(nc.sbuf_tensor(name, shape, dt))
            uT = [sb(f"uT{i}", [128, T], BF16) for i in range(4)]
            cst = sb("cst", [128, 4, 1024], BF16)
            abst = [sb(f"abst{i}", [128, 1024], BF16) for i in range(2)]
            s_u = b.sem("u"); s_cs = b.semg("cs"); s_cd = b.sem("cd"); s_ab = b.sem("ab"); s_abo = [b.sem("abo"), b.sem("abo")]
            t_cs = dma("gpsimd", cst[:], cs_in.rearrange("(c p) n -> p c n", p=128), s_cs)
            cd = {"n": 0, "ev": {}, "out": {}, "last_mm": None}

            def make_ep_u(c):
                def ep(tg, bank, tk):
                    wait(ACT, tk)
                    if tg == 0:
                        wait(ACT, cd["last_mm"])
                    return [inc(ACT.activation(out=uT[c][:, tg * 512:(tg + 1) * 512], in_=bank[:], func=AF.Copy), s_u)]
                return ep

            for g4 in range(4):
                for c in range(4):
                    ws_unit(w_in, 6144 + g4 * 512 + c * 128, make_ep_u(c))
                last_g = state["g"] - 1
                for gg in range(last_g - 15, last_g + 1):
                    for tk in grp_rel[gg]:
                        wait(PE, tk)
                wait(PE, t_cs)
                for tb in range(NTB):
                    n = cd["n"]; cd["n"] += 1
                    A_, B_ = BK[3], BK[4]
                    if n >= 1:
                        wait(PE, cd["ev"][n - 1])
                    for c in range(4):
                        PE.matmul(A_[:], uT[c][:, tb * 128:(tb + 1) * 128], cst[:, c, 0:512], start=(c == 0), stop=(c == 3))
                    for c in range(4):
                        mm = PE.matmul(B_[:], uT[c][:, tb * 128:(tb + 1) * 128], cst[:, c, 512:1024], start=(c == 0), stop=(c == 3))
                    tmm = inc(mm, s_cd)
                    cd["last_mm"] = tmm
                    ab = abst[n % 2]
                    wait(ACT, tmm); wait(DVE, tmm)
                    if n >= 2:
                        wait(ACT, cd["out"][n - 2]); wait(DVE, cd["out"][n - 2])
                    ta = inc(ACT.activation(out=ab[:, 0:512], in_=A_[:], func=AF.Copy), s_ab)
                    td = inc(DVE.tensor_copy(out=ab[:, 512:1024], in_=B_[:]), s_ab)
                    cd["ev"][n] = td
                    wait(SP, td)
                    cd["out"][n] = dma("sync", ab_scr[g4, tb], ab[:], s_abo[n % 2])
            b.barrier(dummies)

    if stop == 2:
        return nc
    with ExitStack() as es:
        es.enter_context(b.scope())
        def sb(name, shape, dt):
            return es.enter_context(nc.sbuf_tensor(name, shape, dt))
        ctt = sb("ctt", [128, NTB, T], BF16)
        stt = sb("stt", [128, NTB, T], BF16)
        abt = [sb(f"abt{i}", [128, NTB, 1024], BF16) for i in range(2)]
        mxf = [sb(f"mxf{i}", [128, T], BF16) for i in range(2)]
        s_c = b.semg("c"); s_ab = b.sem("ab"); s_mm = b.sem("mm"); s_ev = b.sem("ev"); s_o = [b.sem("o"), b.sem("o")]
        tc1 = dma("gpsimd", ctt[:], ct_in.rearrange("(tb p) n -> p tb n", p=128), s_c)
        tc2 = dma("gpsimd", stt[:], st_in.rearrange("(tb p) n -> p tb n", p=128), s_c)
        wait(PE, tc2)
        gcount = 0
        mm_tok = {}
        ev_tok = {}
        o_tok = {}
        ab_last = {}
        ci = 0
        for g4 in range(4):
            if g4 >= 2:
                wait(SP, ab_last[g4 - 2])
            tab = dma("sync", abt[g4 % 2][:], ab_scr[g4].rearrange("tb p n -> p tb n"), s_ab)
            wait(PE, tab)
            for c in range(4):
                mx = mxf[ci % 2]
                for tg in range(4):
                    bank = BK[gcount % 2]
                    if gcount >= 2:
                        wait(PE, ev_tok[gcount - 2])
                    for tb in range(NTB):
                        PE.matmul(bank[:], abt[g4 % 2][:, tb, c * 128:(c + 1) * 128], ctt[:, tb, tg * 512:(tg + 1) * 512], start=(tb == 0), stop=False)
                    for tb in range(NTB):
                        mm = PE.matmul(bank[:], abt[g4 % 2][:, tb, 512 + c * 128:512 + (c + 1) * 128], stt[:, tb, tg * 512:(tg + 1) * 512], start=False, stop=(tb == NTB - 1))
                    tmm = inc(mm, s_mm)
                    ab_last[g4] = tmm
                    wait(ACT, tmm)
                    if tg == 0 and ci >= 2:
                        wait(ACT, o_tok[ci - 2])
                    ev_tok[gcount] = inc(ACT.activation(out=mx[:, tg * 512:(tg + 1) * 512], in_=bank[:], func=AF.Copy), s_ev)
                    gcount += 1
                wait(SP, ev_tok[gcount - 1])
                o_tok[ci] = dma("sync", mix_scr.rearrange("tb p kc t -> p tb kc t")[:, :, 16 + g4 * 4 + c, :], mx[:].rearrange("p (tb t) -> p tb t", t=128), s_o[ci % 2])
                ci += 1
        b.barrier(dummies)

    def as_gemm(tag, KC, NB, a_scr, W, gate_off, res_src, res_dst, ssq=None, NAT=2):
        ncb = D // NB
        with ExitStack() as es:
            es.enter_context(b.scope())
            def sb(name, shape, dt):
                return es.enter_context(nc.sbuf_tensor(name + tag, shape, dt))
            gb = sb("gb", [128, D], F32)
            wb = [sb(f"wb{i}", [128, KC, NB], BF16) for i in range(2)]
            at = [sb(f"at{i}", [128, KC, 128], BF16) for i in range(NAT)]
            xi = [sb(f"xi{i}", [128, NB], F32) for i in range(NAT)]
            xo = [sb(f"xo{i}", [128, NB], F32) for i in range(2)]
            junk = sb("junk", [128, NB], BF16)
            s_g = b.sem("g"); s_w = [b.semg("w0"), b.semg("w1")]; s_a = [b.sem("a") for _ in range(NAT)]
            s_x = [b.sem("x") for _ in range(NAT)]; s_mm = b.sem("mm")
            s_m = b.sem("m"); s_ad = b.sem("ad"); s_o = [b.sem("o"), b.sem("o")]; s_sq = b.sem("sq")
            tg_ = dma("sync", gb[:], mod_row[:, gate_off:gate_off + D].partition_broadcast(128), s_g)
            wait(DVE, tg_)
            mm_tok = {}; m_tok = {}; ad_tok = {}; o_tok = {}; sq_tok = {}; cb_last = {}
            ta_tok = {}; tx_tok = {}; tw_tok = {}
            NI = ncb * NTB

            def loads(idx):
                cb, tb = divmod(idx, NTB)
                if idx >= NAT:
                    wait(SP, mm_tok[idx - NAT])
                ta_tok[idx] = dma("sync", at[idx % NAT][:], a_scr[tb], s_a[idx % NAT])
                if idx >= NAT:
                    wait(SP, ad_tok[idx - NAT])
                tx_tok[idx] = dma("sync", xi[idx % NAT][:], res_src[tb * 128:(tb + 1) * 128, cb * NB:(cb + 1) * NB], s_x[idx % NAT])

            def wloads(cb):
                if cb >= 2:
                    wait(POOL, cb_last[cb - 2])
                tw_tok[cb] = wload(wb[cb % 2][:], W, cb * NB, NB, s_w[cb % 2])

            wloads(0)
            PD = NAT - 1
            for i_ in range(PD):
                loads(i_)
            for idx in range(NI):
                cb, tb = divmod(idx, NTB)
                if tb == 0 and cb + 1 < ncb:
                    wloads(cb + 1)
                if idx + PD < NI:
                    loads(idx + PD)
                bank = BK[idx % 2]
                wait(PE, ta_tok[idx])
                if tb == 0:
                    wait(PE, tw_tok[cb])
                if idx >= 2:
                    wait(PE, m_tok[idx - 2])
                for k in range(KC):
                    mm = PE.matmul(bank[:, 0:NB], at[idx % NAT][:, k, :], wb[cb % 2][:, k, :], start=(k == 0), stop=(k == KC - 1))
                mm_tok[idx] = inc(mm, s_mm)
                cb_last[cb] = mm_tok[idx]
                wait(DVE, mm_tok[idx]); wait(DVE, tx_tok[idx])
                if idx >= 2:
                    wait(DVE, o_tok[idx - 2])
                    if ssq is not None:
                        wait(DVE, sq_tok[idx - 2])
                m_tok[idx] = inc(DVE.tensor_tensor(out=xo[idx % 2][:], in0=bank[:, 0:NB], in1=gb[:, cb * NB:(cb + 1) * NB], op=ALU.mult), s_m)
                wait(DVE, m_tok[idx])
                ad_tok[idx] = inc(DVE.tensor_tensor(out=xo[idx % 2][:], in0=xo[idx % 2][:], in1=xi[idx % NAT][:], op=ALU.add), s_ad)
                if ssq is not None:
                    wait(ACT, ad_tok[idx])
                    sq_tok[idx] = inc(ACT.activation(out=junk[:], in_=xo[idx % 2][:], func=AF.Square, accum_out=ssq[:, cb * NTB + tb:cb * NTB + tb + 1]), s_sq)
                wait(SP, ad_tok[idx])
                o_tok[idx] = dma("sync", res_dst[tb * 128:(tb + 1) * 128, cb * NB:(cb + 1) * NB], xo[idx % 2][:], s_o[idx % 2])
            b.barrier(dummies)

    if stop == 25:
        return nc
    as_gemm("op", 32, 512, mix_scr, w_out, 2 * D, x, x1_scr, NAT=4)
    if stop == 3:
        return nc

    with ExitStack() as es_h:
        h2T = es_h.enter_context(nc.sbuf_tensor("h2T", [128, 32, T], BF16))
        phase_norm(x1_scr, a2, b2, h2T, "n2")
        with ExitStack() as es:
            es.enter_context(b.scope())
            def sb(name, shape, dt):
                return es.enter_context(nc.sbuf_tensor(name, shape, dt))
            NS = 4
            wr = [sb(f"fwr{i}", [128, 32, 128], BF16) for i in range(NS)]
            sgt = [sb(f"sgt{i}", [128, 512], F32) for i in range(2)]
            actst = [sb(f"actst{i}", [128, T], BF16) for i in range(2)]
            s_wr = [b.semg(f"fwr{i}") for i in range(NS)]
            s_mm = b.sem("mm"); s_sg = b.sem("sg"); s_ac = b.sem("ac"); s_o = [b.sem("o"), b.sem("o")]
            f_last = {}
            mm_tok = {}
            sg_tok = {}
            ac_tok = {}
            o_tok = {}
            gidx = 0
            for f in range(NF):
                sl_g = (2 * f) % NS
                sl_u = (2 * f + 1) % NS
                if f >= 2:
                    wait(POOL, f_last[f - 2])
                twg = wload(wr[sl_g][:], w_gate, f * 128, 128, s_wr[sl_g])
                twu = wload(wr[sl_u][:], w_up, f * 128, 128, s_wr[sl_u])
                ast = actst[f % 2]
                for tg in range(4):
                    Bg = BK[(gidx % 2) * 2]
                    Bu = BK[(gidx % 2) * 2 + 1]
                    if tg == 0:
                        wait(PE, twg); wait(PE, twu)
                    if gidx >= 2:
                        wait(PE, ac_tok[gidx - 2])
                    for k in range(32):
                        PE.matmul(Bg[:], wr[sl_g][:, k, :], h2T[:, k, tg * 512:(tg + 1) * 512], start=(k == 0), stop=(k == 31))
                    for k in range(32):
                        mm = PE.matmul(Bu[:], wr[sl_u][:, k, :], h2T[:, k, tg * 512:(tg + 1) * 512], start=(k == 0), stop=(k == 31))
                    mm_tok[gidx] = inc(mm, s_mm)
                    f_last[f] = mm_tok[gidx]
                    sg = sgt[gidx % 2]
                    wait(ACT, mm_tok[gidx])
                    if gidx >= 2:
                        wait(ACT, ac_tok[gidx - 2])
                    sg_tok[gidx] = inc(ACT.activation(out=sg[:], in_=Bg[:], func=AF.Silu), s_sg)
                    wait(DVE, sg_tok[gidx])
                    if tg == 0 and f >= 2:
                        wait(DVE, o_tok[f - 2])
                    ac_tok[gidx] = inc(DVE.tensor_tensor(out=ast[:, tg * 512:(tg + 1) * 512], in0=Bu[:], in1=sg[:], op=ALU.mult), s_ac)
                    gidx += 1
                wait(SP, ac_tok[gidx - 1])
                o_tok[f] = dma("sync", act_scr.rearrange("tb p f t -> p tb f t")[:, :, f, :], ast[:].rearrange("p (tb t) -> p tb t", t=128), s_o[f % 2])
            b.barrier(dummies)

    if stop == 5:
        return nc
    NB6 = 256
    ssq = nc.alloc_sbuf_tensor("ssq", [128, (D // NB6) * NTB], F32)
    DVE.memset(ssq[:], 0.0)
    b.barrier(dummies)
    as_gemm("dn", NF, NB6, act_scr, w_down, 5 * D, x1_scr, x2_scr, ssq=ssq)

    with ExitStack() as es:
        es.enter_context(b.scope())
        def sb(name, shape, dt):
            return es.enter_context(nc.sbuf_tensor(name, shape, dt))
        fgb = sb("fgb", [128, D], F32)
        xt = [sb(f"fx{i}", [128, D], F32) for i in range(2)]
        yo = [sb(f"fy{i}", [128, D], F32) for i in range(2)]
        tot = sb("tot", [128, NTB], F32)
        sq = sb("fsq", [128, NTB], F32)
        rs = sb("frs", [128, NTB], F32)
        s_ld = [b.sem("ld"), b.sem("ld")]; s_t = b.sem("t"); s_y = b.sem("y"); s_o = [b.sem("o"), b.sem("o")]; s_fg = b.sem("fg")
        tf = dma("sync", fgb[:], fg.partition_broadcast(128), s_fg)
        ncb = D // NB6
        t = inc(DVE.tensor_reduce(out=tot[:], in_=ssq[:].rearrange("p (cb tb) -> p tb cb", tb=NTB), axis=mybir.AxisListType.X, op=ALU.add), s_t)
        wait(ACT, t)
        t = inc(ACT.activation(out=sq[:], in_=tot[:], func=AF.Sqrt, bias=epsc[:], scale=1.0 / D), s_t)
        wait(DVE, t)
        t = inc(DVE.reciprocal(out=rs[:], in_=sq[:]), s_t)
        wait(DVE, t); wait(DVE, tf)
        y_tok = {}
        o_tok = {}
        ld_tok = {}

        def fload(tb):
            if tb >= 2:
                wait(SP, y_tok[tb - 2])
            ld_tok[tb] = dma("sync", xt[tb % 2][:], x2_scr[tb * 128:(tb + 1) * 128, :], s_ld[tb % 2])

        fload(0)
        for tb in range(NTB):
            if tb + 1 < NTB:
                fload(tb + 1)
            wait(DVE, ld_tok[tb])
            if tb >= 2:
                wait(DVE, o_tok[tb - 2])
            y_tok[tb] = inc(DVE.scalar_tensor_tensor(out=yo[tb % 2][:], in0=xt[tb % 2][:], scalar=rs[:, tb:tb + 1], in1=fgb[:], op0=ALU.mult, op1=ALU.mult), s_y)
            wait(SP, y_tok[tb])
            o_tok[tb] = dma("sync", y[tb * 128:(tb + 1) * 128, :], yo[tb % 2][:], s_o[tb % 2])
        b.barrier(dummies)
    return nc


def _bias_tables(rpb):
    reps = [0, 1, 2, 3, 14, 15]
    sb_ = np.empty((16, 128, 6, 5, 128), np.float32)
    pb_ = np.empty((16, 128, 6, 5, 128), np.float32)
    ql = np.arange(128)
    kl = np.arange(128)
    for ti, i in enumerate(reps):
        base = min(max(i - 2, 0), 11)
        qr = 2 * i + ql // 64
        qc = ql % 64
        rs_ = np.clip(qr - 4, 0, 24)
        cs_ = np.clip(qc - 8, 0, 48)
        for s in range(5):
            kb = base + s
            kr = 2 * kb + kl // 64
            kc = kl % 64
            valid = ((kr[:, None] >= rs_[None, :]) & (kr[:, None] < rs_[None, :] + 8)
                     & (kc[:, None] >= cs_[None, :]) & (kc[:, None] < cs_[None, :] + 16))
            dr = np.clip(kr[:, None] - qr[None, :] + 7, 0, 14)
            dc = np.clip(kc[:, None] - qc[None, :] + 15, 0, 30)
            vals = rpb[:, dr, dc]
            sb_[:, :, ti, s, :] = np.where(valid[None], vals, np.float32(NEG))
            pb_[:, :, ti, s, :] = 0.0 if (kb // 2 == i // 2) else NEG
    return sb_.reshape(16, 128, -1), pb_.reshape(16, 128, -1)


def _dft_tables():
    def cs(n, scale):
        idx = np.arange(n)
        m = (idx[:, None] * idx[None, :]) % n
        ang = 2.0 * np.pi * m / n
        return (np.cos(ang) * scale), (np.sin(ang) * scale)
    c2048, s2048 = cs(2048, 1.0 / np.sqrt(2048 * 512.0))
    c256, s256 = cs(256, 1.0 / np.sqrt(256 * 512.0))
    ctp = np.zeros((2048, 2048)); stp = np.zeros((2048, 2048))
    for bq in range(8):
        ctp[bq * 256:(bq + 1) * 256, bq * 256:(bq + 1) * 256] = c256
        stp[bq * 256:(bq + 1) * 256, bq * 256:(bq + 1) * 256] = s256
    cc, sc = cs(512, 1.0)
    csm = np.concatenate([cc, -sc], axis=1)
    f = np.float32
    return c2048.astype(f), s2048.astype(f), ctp.astype(f), stp.astype(f), csm.astype(f)


def _prep(x_prompt, x_sample, cache_k, cache_v, c, c_ctx, w_ada, b_ada, norm1_g, w_in,
          rpb, w_out, norm2_g, w_gate, w_up, w_down, final_g):
    f = np.float32
    A = lambda a: np.ascontiguousarray(np.asarray(a), dtype=f)
    x_prompt = A(x_prompt); x_sample = A(x_sample)
    sbias, pbias = _bias_tables(A(rpb)[0])
    c2048, s2048, ctp, stp, csm = _dft_tables()
    common = {
        "w_ada": A(w_ada)[0], "b_ada": A(b_ada)[0].reshape(1, -1), "n1g": A(norm1_g)[0].reshape(32, 128),
        "n2g": A(norm2_g)[0].reshape(32, 128), "fg": A(final_g).reshape(1, -1), "w_in": A(w_in)[0],
        "w_out": A(w_out)[0], "w_gate": A(w_gate)[0], "w_up": A(w_up)[0], "w_down": A(w_down)[0],
        "cs": csm, "ident": np.eye(128, dtype=f),
    }
    zkv = np.zeros((256, 2048), f)
    in_maps = []
    for core in range(8):
        m = dict(common)
        if core < 4:
            m["x"] = x_prompt[core * 8:(core + 1) * 8].reshape(T, D)
            m["cvec"] = A(c_ctx).reshape(32, 128)
            m["kc"] = zkv; m["vc"] = zkv
            m["bias"] = pbias
            m["ctxb"] = np.full((128, 1), NEG, f)
            m["ct"] = ctp; m["st"] = stp
        else:
            bi = core - 4
            m["x"] = x_sample[bi]
            m["cvec"] = A(c)[bi].reshape(32, 128)
            m["kc"] = A(cache_k)[bi, 0].reshape(256, 2048)
            m["vc"] = A(cache_v)[bi, 0].reshape(256, 2048)
            m["bias"] = sbias
            m["ctxb"] = np.zeros((128, 1), f)
            m["ct"] = c2048; m["st"] = s2048
        in_maps.append(m)
    return in_maps


def kernel(**inputs):
    f = np.float32
    in_maps = _prep(**inputs)
    nc = build()
    res = run_bass_kernel_spmd(nc, in_maps, core_ids=list(range(8)))
    r = res.results
    y_prompt = np.stack([r[i]["y"] for i in range(4)]).reshape(32, 256, D)
    y_sample = np.stack([r[4 + i]["y"] for i in range(4)])
    nkk = np.stack([r[i]["nk"] for i in range(4)]).reshape(32, 1, 256, 16, 128)
    nvv = np.stack([r[i]["nv"] for i in range(4)]).reshape(32, 1, 256, 16, 128)
    return (y_prompt.astype(f), y_sample.astype(f), nkk.astype(f), nvv.astype(f))
```

```python
from contextlib import ExitStack, contextmanager
import numpy as np
import concourse.bass as bass
import concourse.mybir as mybir
from concourse.bass_utils import run_bass_kernel_spmd

F32 = mybir.dt.float32
BF16 = mybir.dt.bfloat16
AF = mybir.ActivationFunctionType
ALU = mybir.AluOpType

D = 4096
T = 2048
NTB = 16
DFF = 11008
NF = 86
EPS = 1e-6
SCALE = 128.0 ** -0.5
NEG = -30000.0
import os as _os
NOPOOL = bool(_os.environ.get('NOPOOL'))


class Sem:
    def __init__(self, nc, name):
        self.h = nc.alloc_semaphore(name)
        self.v = 0


class Lazy:
    def __init__(self, f):
        self._f = f
        self._v = None

    def get(self):
        if self._v is None:
            self._v = self._f()
        return self._v

    def __getitem__(self, k):
        return self.get()[k]

    def __getattr__(self, n):
        return getattr(self.get(), n)


def _un(a):
    return a.get() if isinstance(a, Lazy) else a


class B:
    def __init__(self):
        self.nc = bass.Bass("TRN2", target_bir_lowering=False)
        self.nsem = 0
        self.dma_sems = {"sync": set(), "gpsimd": set()}
        self.bar = Sem(self.nc, "bar")
        self.bar_n = 0
        self.pool = []
        self.scopes = []

    def sem(self, name):
        self.nsem += 1
        sm = self.pool.pop() if (self.pool and not NOPOOL) else Sem(self.nc, f"{name}_{self.nsem}")
        if self.scopes:
            self.scopes[-1].append(sm)
        return sm

    def semg(self, name):
        self.nsem += 1
        return Sem(self.nc, f"{name}_{self.nsem}")

    @contextmanager
    def scope(self):
        self.scopes.append([])
        yield
        self.pool.extend(self.scopes.pop())

    @staticmethod
    def inc(instr, sem, n=1):
        instr.then_inc(sem.h, n)
        sem.v += n
        return (sem, sem.v)

    @staticmethod
    def wait(eng, tok):
        if tok is not None:
            eng.wait_ge(tok[0].h, tok[1])

    def dma(self, q, out, in_, sem):
        eng = self.nc.sync if q == "sync" else self.nc.gpsimd
        ins = eng.dma_start(out=_un(out), in_=_un(in_))
        self.dma_sems[q].add(sem)
        return self.inc(ins, sem, 16)

    def barrier(self, dummies):
        nc = self.nc
        self.bar_n += 1
        dps, idb, dv, da, dg = dummies
        self.inc(nc.vector.memset(dv[:], 0.0), self.bar)
        self.inc(nc.scalar.activation(out=da[:, 0:1], in_=da[:, 1:2], func=AF.Copy), self.bar)
        for s in self.dma_sems["gpsimd"]:
            nc.gpsimd.wait_ge(s.h, s.v)
        self.inc(nc.gpsimd.memset(dg[:], 0.0), self.bar)
        for s in self.dma_sems["sync"]:
            nc.sync.wait_ge(s.h, s.v)
        nc.sync.sem_inc(self.bar.h, 1)
        self.bar.v += 1
        nc.tensor.wait_ge(self.bar.h, self.bar.v)
        self.inc(nc.tensor.matmul(dps[:, 511:512], idb[:], idb[:, 0:1], start=True, stop=True), self.bar)
        for e in (nc.tensor, nc.vector, nc.scalar, nc.gpsimd, nc.sync):
            e.wait_ge(self.bar.h, self.bar.v)


def build(stop=None, skip0=False):
    b = B()
    nc = b.nc
    inc, wait, dma = b.inc, b.wait, b.dma
    PE, DVE, ACT, POOL, SP = nc.tensor, nc.vector, nc.scalar, nc.gpsimd, nc.sync

    def din(name, shape):
        return Lazy(lambda: nc.dram_tensor(name, shape, F32, kind="ExternalInput").ap())

    def dout(name, shape):
        return Lazy(lambda: nc.dram_tensor(name, shape, F32, kind="ExternalOutput").ap())

    x = din("x", [T, D])
    cvec = din("cvec", [32, 128])
    w_ada = din("w_ada", [D, 6 * D])
    b_ada = din("b_ada", [1, 6 * D])
    n1g = din("n1g", [32, 128])
    n2g = din("n2g", [32, 128])
    fg = din("fg", [1, D])
    w_in = din("w_in", [D, 2 * D])
    w_out = din("w_out", [D, D])
    w_gate = din("w_gate", [D, DFF])
    w_up = din("w_up", [D, DFF])
    w_down = din("w_down", [DFF, D])
    kc_in = din("kc", [256, 2048])
    vc_in = din("vc", [256, 2048])
    bias_in = din("bias", [16, 128, 6 * 5 * 128])
    ctxb_in = din("ctxb", [128, 1])
    ct_in = din("ct", [T, T])
    st_in = din("st", [T, T])
    cs_in = din("cs", [512, 1024])
    ident_in = din("ident", [128, 128])
    y = dout("y", [T, D])
    nk = dout("nk", [T, 2048])
    nv = dout("nv", [T, 2048])
    mod_row = nc.dram_tensor("mod_row", [1, 6 * D], F32).ap()
    DBG = bool(_os.environ.get("DBG"))
    skind = "ExternalOutput" if DBG else "Internal"
    mix_scr = nc.dram_tensor("mix_scr", [NTB, 128, 32, 128], BF16, kind=skind).ap()
    ab_scr = nc.dram_tensor("ab_scr", [4, NTB, 128, 1024], BF16, kind=skind).ap()
    x1_scr = nc.dram_tensor("x1_scr", [T, D], F32, kind=skind).ap()
    act_scr = nc.dram_tensor("act_scr", [NTB, 128, NF, 128], BF16, kind=skind).ap()
    x2_scr = nc.dram_tensor("x2_scr", [T, D], F32, kind=skind).ap()
    x3_scr = nc.dram_tensor("x3_scr", [T, D], F32, kind=skind).ap()

    idf = nc.alloc_sbuf_tensor("idf", [128, 128], F32)
    idb = nc.alloc_sbuf_tensor("idb", [128, 128], BF16)
    onesb = nc.alloc_sbuf_tensor("onesb", [128, 128], BF16)
    modT = nc.alloc_sbuf_tensor("modT", [128, 192], F32)
    a1 = nc.alloc_sbuf_tensor("a1", [128, 32], F32)
    a2 = nc.alloc_sbuf_tensor("a2", [128, 32], F32)
    ctxb = nc.alloc_sbuf_tensor("ctxb_sb", [128, 1], F32)
    epsc = nc.alloc_sbuf_tensor("epsc", [128, 1], F32)
    dv = nc.alloc_sbuf_tensor("dv", [128, 1], F32)
    da = nc.alloc_sbuf_tensor("da", [128, 2], F32)
    dg = nc.alloc_sbuf_tensor("dg", [128, 1], F32)
    BK = [nc.alloc_psum_tensor(f"bk{i}", [128, 512], F32) for i in range(6)]
    PB = [nc.alloc_psum_tensor(f"pb{i}", [128, 1024], BF16) for i in range(2)]
    dummies = (BK[5], idb, dv, da, dg)

    s0 = b.sem("init")
    dma("sync", idf[:], ident_in, s0)
    s0g = b.semg("initg")
    dma("gpsimd", idb[:], ident_in, s0g)
    dma("sync", ctxb[:], ctxb_in, s0)
    for e in (PE, DVE, ACT, POOL):
        e.wait_ge(s0.h, s0.v)
        e.wait_ge(s0g.h, s0g.v)
    DVE.memset(onesb[:], 1.0)
    DVE.memset(epsc[:], EPS)
    s0c = b.sem("initc")
    t = inc(DVE.memset(da[:], 0.0), s0c)
    for e in (PE, ACT, POOL, SP):
        wait(e, t)
    b.barrier(dummies)

    if skip0:
        DVE.memset(modT[:], 0.5)
        DVE.memset(a1[:], 1.5)
        t = inc(DVE.memset(a2[:], 1.5), s0c)
        wait(SP, t)
        dma("sync", mod_row.rearrange("o (p j) -> (o p) j", p=128), modT[:], s0)
        for e in (PE, ACT, POOL, SP):
            wait(e, t)
        b.barrier(dummies)
    if not skip0:
        with ExitStack() as es:
            es.enter_context(b.scope())
            def sb(name, shape, dt):
                return es.enter_context(nc.sbuf_tensor(name, shape, dt))
            cv = sb("cv", [32, 128], F32)
            cvT = sb("cvT", [128, 32], BF16)
            ring = [sb(f"adar{i}", [128, 32, 512], BF16) for i in range(3)]
            brow = [sb(f"brow{i}", [1, 512], F32) for i in range(2)]
            rowsb = [sb(f"rowsb{i}", [1, 512], F32) for i in range(2)]
            mr = [sb(f"mr{i}", [96, 128], F32) for i in range(2)]
            gt = [sb(f"gt{i}", [32, 128], F32) for i in range(2)]
            s_ld = b.sem("ld"); s_w = [b.semg("w") for _ in range(3)]; s_b = [b.sem("b") for _ in range(2)]; s_mm = b.sem("mm"); s_row = b.sem("row")
            s_st = [b.sem("st") for _ in range(2)]; s_a = b.sem("a")
            t = dma("sync", cv[:], cvec, s_ld)
            wait(PE, t)
            t = inc(PE.transpose(BK[0][:, 0:32], cv[:], idf[0:32, 0:32]), s_a)
            wait(ACT, t)
            t_cvT = inc(ACT.activation(out=cvT[:], in_=BK[0][:, 0:32], func=AF.Silu), s_a)
            wait(PE, t_cvT)
            NU = 48
            mm_tok = [None] * NU
            row_tok = [None] * NU
            st_tok = [None] * NU
            for u in range(NU):
                if u >= 3:
                    wait(POOL, mm_tok[u - 3])
                tw = dma("gpsimd", ring[u % 3][:], w_ada[:, u * 512:(u + 1) * 512].rearrange("(kc p) n -> p kc n", p=128), s_w[u % 3])
                if u >= 2:
                    wait(SP, row_tok[u - 2])
                tb_ = dma("sync", brow[u % 2][:], b_ada[:, u * 512:(u + 1) * 512], s_b[u % 2])
                wait(PE, tw)
                if u >= 2:
                    wait(PE, row_tok[u - 2])
                for k in range(32):
                    mm = PE.matmul(BK[u % 2][0:1, :], cvT[:, k:k + 1], ring[u % 3][:, k, :], start=(k == 0), stop=(k == 31))
                mm_tok[u] = inc(mm, s_mm)
                wait(DVE, mm_tok[u]); wait(DVE, tb_)
                if u >= 2:
                    wait(DVE, st_tok[u - 2])
                row_tok[u] = inc(DVE.tensor_tensor(out=rowsb[u % 2][:], in0=BK[u % 2][0:1, :], in1=brow[u % 2][:], op=ALU.add), s_row)
                wait(SP, row_tok[u])
                st_tok[u] = dma("sync", mod_row[:, u * 512:(u + 1) * 512], rowsb[u % 2][:], s_st[u % 2])
            wait(SP, st_tok[NU - 1]); wait(SP, st_tok[NU - 2])
            mrv = mod_row.rearrange("o (j p) -> (o j) p", p=128)
            t0_ = dma("sync", mr[0][:], mrv[0:96, :], s_ld)
            t1_ = dma("sync", mr[1][:], mrv[96:192, :], s_ld)
            t2_ = dma("sync", gt[0][:], n1g, s_ld)
            t3_ = dma("sync", gt[1][:], n2g, s_ld)
            wait(PE, t3_)
            PE.transpose(BK[2][:, 0:96], mr[0][:], idf[0:96, 0:96])
            PE.transpose(BK[2][:, 96:192], mr[1][:], idf[0:96, 0:96])
            PE.transpose(BK[2][:, 192:224], gt[0][:], idf[0:32, 0:32])
            t = inc(PE.transpose(BK[2][:, 224:256], gt[1][:], idf[0:32, 0:32]), s_a)
            wait(DVE, t)
            t = inc(DVE.tensor_copy(out=modT[:], in_=BK[2][:, 0:192]), s_a)
            wait(DVE, t)
            DVE.scalar_tensor_tensor(out=a1[:], in0=modT[:, 32:64], scalar=1.0, in1=BK[2][:, 192:224], op0=ALU.add, op1=ALU.mult)
            DVE.scalar_tensor_tensor(out=a2[:], in0=modT[:, 128:160], scalar=1.0, in1=BK[2][:, 224:256], op0=ALU.add, op1=ALU.mult)
            b.barrier(dummies)
    b1 = modT[:, 0:32]
    b2 = modT[:, 96:128]

    def dbg_out(name, src_ap, shape):
        o = nc.dram_tensor(name, shape, F32, kind="ExternalOutput").ap()
        sd = b.sem("dbg")
        tk = dma("sync", o, src_ap, sd)
        wait(SP, tk)

    if stop == 0:
        dbg_out("d_modT", modT[:], [128, 192])
        dbg_out("d_a1", a1[:], [128, 32])
        return nc

    def phase_norm(src, av, bv, hT, tag):
        with ExitStack() as es:
            es.enter_context(b.scope())
            def sb(name, shape, dt):
                return es.enter_context(nc.sbuf_tensor(name + tag, shape, dt))
            xt = [sb(f"xt{i}", [128, D], F32) for i in range(2)]
            xh = [sb(f"xh{i}", [128, D], BF16) for i in range(2)]
            junk = sb("junk", [128, D], BF16)
            ss = sb("ss", [128, NTB], F32)
            sq = sb("sq", [128, NTB], F32)
            rs = sb("rs", [128, NTB], F32)
            s_ld = [b.sem("ld"), b.sem("ld")]; s_sq = b.sem("sq"); s_rt = b.sem("rt"); s_r = b.sem("r"); s_xh = b.sem("xh")
            s_tr = b.sem("tr"); s_ea = b.sem("ea"); s_ed = b.sem("ed")
            xh_tok = [None] * NTB
            tr_last = [None] * NTB
            ev_tok = {}
            t = inc(DVE.memset(ss[:], 0.0), s_r)
            wait(ACT, t)
            gi = 0
            LVL = int(_os.environ.get("LVL", "9"))
            sq_tok = [None] * NTB
            for tb in range(NTB):
                if tb >= 2:
                    wait(SP, xh_tok[tb - 2] if LVL >= 2 else sq_tok[tb - 2])
                tl = dma("sync", xt[tb % 2][:], src[tb * 128:(tb + 1) * 128, :], s_ld[tb % 2])
                wait(ACT, tl)
                t = inc(ACT.activation(out=junk[:], in_=xt[tb % 2][:], func=AF.Square, accum_out=ss[:, tb:tb + 1]), s_sq)
                sq_tok[tb] = t
                if LVL < 2:
                    continue
                wait(ACT, t)
                t = inc(ACT.activation(out=sq[:, tb:tb + 1], in_=ss[:, tb:tb + 1], func=AF.Sqrt, bias=epsc[:], scale=1.0 / D), s_rt)
                wait(DVE, t)
                t = inc(DVE.reciprocal(out=rs[:, tb:tb + 1], in_=sq[:, tb:tb + 1]), s_r)
                wait(DVE, t)
                if tb >= 2 and LVL >= 3:
                    wait(DVE, tr_last[tb - 2])
                xh_tok[tb] = inc(DVE.tensor_scalar(out=xh[tb % 2][:], in0=xt[tb % 2][:], scalar1=rs[:, tb:tb + 1], scalar2=None, op0=ALU.mult), s_xh)
                if LVL < 3:
                    continue
                wait(PE, xh_tok[tb])
                for g in range(4):
                    pb = PB[gi % 2]
                    if gi >= 2 and LVL >= 4:
                        wait(PE, ev_tok[gi - 2])
                    for j in range(8):
                        kc = g * 8 + j
                        tr = PE.transpose(pb[:, j * 128:(j + 1) * 128], xh[tb % 2][:, kc * 128:(kc + 1) * 128], idb[:])
                    ttr = inc(tr, s_tr)
                    if g == 3:
                        tr_last[tb] = ttr
                    if LVL < 4:
                        gi += 1
                        continue
                    use_act = (gi % 2 == 0)
                    wait(ACT if use_act else DVE, ttr)
                    for j in range(8):
                        kc = g * 8 + j
                        o = hT[:, kc, tb * 128:(tb + 1) * 128]
                        i_ = pb[:, j * 128:(j + 1) * 128]
                        if use_act:
                            iv = ACT.activation(out=o, in_=i_, func=AF.Identity, bias=bv[:, kc:kc + 1], scale=av[:, kc:kc + 1])
                        else:
                            iv = DVE.tensor_scalar(out=o, in0=i_, scalar1=av[:, kc:kc + 1], scalar2=bv[:, kc:kc + 1], op0=ALU.mult, op1=ALU.add)
                    ev_tok[gi] = inc(iv, s_ea if use_act else s_ed)
                    gi += 1
            b.barrier(dummies)

    def wload(dst, W, c0, n, sem):
        return dma("gpsimd", dst, W[:, c0:c0 + n].rearrange("(kc p) n -> p kc n", p=128), sem)

    with ExitStack() as es_h:
        hT = es_h.enter_context(nc.sbuf_tensor("hT", [128, 32, T], BF16))
        phase_norm(x, a1, b1, hT, "n1")
        if stop == 1:
            with nc.sbuf_tensor("dbgh", [128, 32, 128], F32) as dbgh:
                if int(_os.environ.get('LVL', '9')) < 4:
                    DVE.memset(hT[:, :, 0:128], 1.0)
                DVE.tensor_copy(out=dbgh[:], in_=hT[:, :, 0:128])
                b.barrier(dummies)
                dbg_out("d_hT", dbgh[:], [128, 32, 128])
            return nc

        NSLOT = 3
        wr = [es_h.enter_context(nc.sbuf_tensor(f"wr{i}", [128, 32, 128], BF16)) for i in range(NSLOT)]
        s_wr = [b.semg(f"wr{i}") for i in range(NSLOT)]
        s_gm = b.sem("gm")
        state = {"u": 0, "g": 0}
        unit_last_grp = {}
        grp_rel = {}

        def ws_unit(W, c0, epilogue):
            u = state["u"]; state["u"] += 1
            slot = u % NSLOT
            if u >= NSLOT:
                wait(POOL, unit_last_grp[u - NSLOT])
            tw = wload(wr[slot][:], W, c0, 128, s_wr[slot])
            wait(PE, tw)
            for tg in range(4):
                g = state["g"]; state["g"] += 1
                bank = BK[g % 2]
                if g >= 2:
                    for tk in grp_rel[g - 2]:
                        wait(PE, tk)
                for k in range(32):
                    mm = PE.matmul(bank[:], wr[slot][:, k, :], hT[:, k, tg * 512:(tg + 1) * 512], start=(k == 0), stop=(k == 31))
                tk = inc(mm, s_gm)
                if tg == 3:
                    unit_last_grp[u] = tk
                grp_rel[g] = epilogue(tg, bank, tk)

        with ExitStack() as es:
            es.enter_context(b.scope())
            def sb(name, shape, dt):
                return es.enter_context(nc.sbuf_tensor(name, shape, dt))
            qT = sb("qT", [128, T], BF16)
            kT = sb("kT", [128, T], BF16)
            vbf = sb("vbf", [128, NTB, 128], BF16)
            fst = [sb(f"fst{i}", [128, 512], F32) for i in range(2)]
            kvo = [sb(f"kvo{i}", [128, 4, 128], F32) for i in range(2)]
            biast = sb("biast", [128, 6 * 5 * 128], F32)
            PT = [sb(f"PT{i}", [128, 7 * 128], BF16) for i in range(2)]
            rinv = [sb(f"rinv{i}", [128, 128], F32) for i in range(2)]
            mixst = sb("mixst", [128, NTB, 128], BF16)
            kcs2 = [sb(f"kcs{i}", [128, 2, 128], BF16) for i in range(2)]
            kcT = sb("kcT", [128, 256], BF16)
            vcs2 = [sb(f"vcs{i}", [128, 2, 128], BF16) for i in range(2)]
            att_hist = {}
            s_q = b.sem("q"); s_kA = b.sem("kA"); s_kD = b.sem("kD"); s_tr = b.sem("tr"); s_ko = b.sem("ko")
            s_va = b.sem("va"); s_st = [b.sem("st"), b.sem("st")]; s_bias = b.sem("bias"); s_ctx = [b.semg("ctx"), b.semg("ctx")]; s_kct = b.sem("kct")
            s_qk = b.sem("qk"); s_e1 = b.sem("e1"); s_e2 = b.sem("e2"); s_ex = b.sem("ex"); s_pv = b.sem("pv")
            s_ri = b.sem("ri"); s_no = b.sem("no"); s_mx = b.sem("mx")
            fs_n = {"n": 0}
            tr_rel = {}
            fst_rel = {}
            kvo_rel = {}
            att = {"n": 0, "pv": {}, "ex": {}, "no": {}, "mx": None, "att_done": None}

            def ep_q(tg, bank, tk):
                wait(ACT, tk)
                if tg == 0:
                    wait(ACT, att["att_done"])
                return [inc(ACT.activation(out=qT[:, tg * 512:(tg + 1) * 512], in_=bank[:], func=AF.Copy), s_q)]

            def make_ep_kv(is_k, j):
                def ep(tg, bank, tk):
                    n = fs_n["n"]; fs_n["n"] += 1
                    f = fst[n % 2]
                    wait(DVE, tk)
                    if n >= 2:
                        for tk2 in fst_rel[n - 2]:
                            wait(DVE, tk2)
                    tf = inc(DVE.tensor_copy(out=f[:], in_=bank[:]), s_kD)
                    rel = [tf]
                    fst_rel[n] = []
                    if is_k:
                        wait(ACT, tf)
                        if tg == 0:
                            wait(ACT, att["att_done"])
                        fst_rel[n].append(inc(ACT.activation(out=kT[:, tg * 512:(tg + 1) * 512], in_=f[:], func=AF.Copy), s_kA))
                    wait(PE, tf)
                    if n >= 1:
                        for tk2 in tr_rel[n - 1]:
                            wait(PE, tk2)
                    for bb in range(4):
                        tr = PE.transpose(BK[2][:, bb * 128:(bb + 1) * 128], f[:, bb * 128:(bb + 1) * 128], idf[:])
                    ttr = inc(tr, s_tr)
                    fst_rel[n].append(ttr)
                    ko = kvo[n % 2]
                    wait(DVE, ttr)
                    if n >= 2:
                        for tk2 in kvo_rel[n - 2]:
                            wait(DVE, tk2)
                    tko = inc(DVE.tensor_copy(out=ko[:].rearrange("p b d -> p (b d)"), in_=BK[2][:]), s_ko)
                    tr_rel[n] = [tko]
                    kvo_rel[n] = []
                    if not is_k:
                        wait(ACT, tko)
                        if tg == 0:
                            wait(ACT, att["att_done"])
                        kvo_rel[n].append(inc(ACT.activation(out=vbf[:, tg * 4:(tg + 1) * 4, :].rearrange("p b d -> p (b d)"), in_=ko[:].rearrange("p b d -> p (b d)"), func=AF.Copy), s_va))
                    wait(SP, tko)
                    dst = (nk if is_k else nv)[tg * 512:(tg + 1) * 512, j * 128:(j + 1) * 128].rearrange("(b p) d -> p b d", p=128)
                    kvo_rel[n].append(dma("sync", dst, ko[:], s_st[n % 2]))
                    return rel
                return ep

            def tmap(i):
                return {0: 0, 1: 1, 14: 4, 15: 5}.get(i, 2 + (i % 2))

            for j in range(16):
                wait(SP, att["att_done"])
                t_bias = dma("sync", biast[:], bias_in[j], s_bias)
                kcs = kcs2[j % 2]; vcs = vcs2[j % 2]
                wait(POOL, att_hist.get(j - 2))
                t_kc = dma("gpsimd", kcs[:], kc_in[:, j * 128:(j + 1) * 128].rearrange("(b p) d -> p b d", p=128), s_ctx[j % 2])
                t_vc = dma("gpsimd", vcs[:], vc_in[:, j * 128:(j + 1) * 128].rearrange("(b p) d -> p b d", p=128), s_ctx[j % 2])
                t_kc = t_vc
                nprev = att["n"] - 1
                for nn in (nprev, nprev - 1):
                    if nn >= 0:
                        wait(PE, att["ex"][nn]); wait(PE, att["no"][nn])
                ws_unit(w_in, j * 128, ep_q)
                ws_unit(w_in, 2048 + j * 128, make_ep_kv(True, j))
                ws_unit(w_in, 4096 + j * 128, make_ep_kv(False, j))
                wait(PE, t_kc)
                for bb in range(2):
                    tr = PE.transpose(PB[0][:, bb * 128:(bb + 1) * 128], kcs[:, bb, :], idb[:])
                ttr = inc(tr, s_kct)
                wait(ACT, ttr)
                t_kcT = inc(ACT.activation(out=kcT[:], in_=PB[0][:, 0:256], func=AF.Copy), s_kct)
                last_g = state["g"] - 1
                pend = []
                for gg in range(last_g - 11, last_g + 1):
                    pend += grp_rel[gg]
                for nn in range(fs_n["n"] - 8, fs_n["n"]):
                    pend += tr_rel[nn] + [t_ for t_ in kvo_rel[nn] if t_[0] is s_va] + [t_ for t_ in fst_rel[nn] if t_[0] is s_kA]
                for tk in pend + [t_kcT, t_vc, t_bias]:
                    wait(PE, tk)
                wait(DVE, t_bias)
                wait(DVE, att["mx"])
                sets = [(BK[3], BK[4], BK[5]), (BK[0], BK[1], BK[2])]
                n0 = att["n"]; att["n"] += 16

                def qk_stage(i):
                    n = n0 + i
                    base = min(max(i - 2, 0), 11)
                    ty = tmap(i)
                    X, Y, Z = sets[n % 2]
                    if n >= 2:
                        wait(PE, att["ex"][n - 2])
                    for s_ in range(5):
                        kb = base + s_
                        dst = X[:, s_ * 128:(s_ + 1) * 128] if s_ < 4 else Y[:, 0:128]
                        PE.matmul(dst, kT[:, kb * 128:(kb + 1) * 128], qT[:, i * 128:(i + 1) * 128], start=True, stop=True)
                    for s_ in range(2):
                        mm = PE.matmul(Y[:, (1 + s_) * 128:(2 + s_) * 128], kcT[:, s_ * 128:(s_ + 1) * 128], qT[:, i * 128:(i + 1) * 128], start=True, stop=True)
                    tqk = inc(mm, s_qk)
                    wait(DVE, tqk)
                    bo = ty * 640
                    DVE.scalar_tensor_tensor(out=X[:], in0=X[:], scalar=SCALE, in1=biast[:, bo:bo + 512], op0=ALU.mult, op1=ALU.add)
                    te = inc(DVE.scalar_tensor_tensor(out=Y[:, 0:128], in0=Y[:, 0:128], scalar=SCALE, in1=biast[:, bo + 512:bo + 640], op0=ALU.mult, op1=ALU.add), s_e1)
                    pt = PT[n % 2]
                    wait(ACT, te)
                    if n >= 2:
                        wait(ACT, att["pv"][n - 2])
                    ACT.activation(out=pt[:, 0:512], in_=X[:], func=AF.Exp)
                    ACT.activation(out=pt[:, 512:640], in_=Y[:, 0:128], func=AF.Exp)
                    att["ex"][n] = inc(ACT.activation(out=pt[:, 640:896], in_=Y[:, 128:384], func=AF.Exp, bias=ctxb[:], scale=SCALE), s_ex)

                def pv_stage(i):
                    n = n0 + i
                    base = min(max(i - 2, 0), 11)
                    X, Y, Z = sets[n % 2]
                    pt = PT[n % 2]
                    wait(PE, att["ex"][n])
                    if n >= 2:
                        wait(PE, att["no"][n - 2])
                    for s_ in range(7):
                        lhs = vbf[:, base + s_, :] if s_ < 5 else vcs[:, s_ - 5, :]
                        PE.matmul(Z[:, 0:128], lhs, pt[:, s_ * 128:(s_ + 1) * 128], start=(s_ == 0), stop=(s_ == 6))
                    for s_ in range(7):
                        mm = PE.matmul(Z[:, 128:256], onesb[:], pt[:, s_ * 128:(s_ + 1) * 128], start=(s_ == 0), stop=(s_ == 6))
                    tpv_ = inc(mm, s_pv)
                    att["pv"][n] = tpv_
                    ri = rinv[n % 2]
                    wait(DVE, tpv_)
                    tri = inc(DVE.reciprocal(out=ri[:], in_=Z[:, 128:256]), s_ri)
                    wait(DVE, tri)
                    tno_ = inc(DVE.tensor_tensor(out=mixst[:, i, :], in0=Z[:, 0:128], in1=ri[:], op=ALU.mult), s_no)
                    att["no"][n] = tno_
                    return tpv_, tno_

                qk_stage(0)
                for i in range(16):
                    if i + 1 < 16:
                        qk_stage(i + 1)
                    tpv, tno = pv_stage(i)
                wait(SP, tno)
                att["mx"] = dma("sync", mix_scr.rearrange("tb p kc t -> p tb kc t")[:, :, j, :], mixst[:], s_mx)
                att["att_done"] = tpv
                att_hist[j] = tpv
            b.barrier(dummies)

        with ExitStack() as es:
            es.enter_context(b.scope())
            def sb(name, shape, dt):
                return es.enter_context(nc.sbuf_tensor(name, shape, dt))
            uT = [sb(f"uT{i}", [128, T], BF16) for i in range(4)]
            cst = sb("cst", [128, 4, 1024], BF16)
            abst = [sb(f"abst{i}", [128, 1024], BF16) for i in range(2)]
            s_u = b.sem("u"); s_cs = b.semg("cs"); s_cd = b.sem("cd"); s_ab = b.sem("ab"); s_abo = [b.sem("abo"), b.sem("abo")]
            t_cs = dma("gpsimd", cst[:], cs_in.rearrange("(c p) n -> p c n", p=128), s_cs)
            cd = {"n": 0, "ev": {}, "out": {}, "last_mm": None}

            def make_ep_u(c):
                def ep(tg, bank, tk):
                    wait(ACT, tk)
                    if tg == 0:
                        wait(ACT, cd["last_mm"])
                    return [inc(ACT.activation(out=uT[c][:, tg * 512:(tg + 1) * 512], in_=bank[:], func=AF.Copy), s_u)]
                return ep

            for g4 in range(4):
                for c in range(4):
                    ws_unit(w_in, 6144 + g4 * 512 + c * 128, make_ep_u(c))
                last_g = state["g"] - 1
                for gg in range(last_g - 15, last_g + 1):
                    for tk in grp_rel[gg]:
                        wait(PE, tk)
                wait(PE, t_cs)
                for tb in range(NTB):
                    n = cd["n"]; cd["n"] += 1
                    A_, B_ = BK[3], BK[4]
                    if n >= 1:
                        wait(PE, cd["ev"][n - 1])
                    for c in range(4):
                        PE.matmul(A_[:], uT[c][:, tb * 128:(tb + 1) * 128], cst[:, c, 0:512], start=(c == 0), stop=(c == 3))
                    for c in range(4):
                        mm = PE.matmul(B_[:], uT[c][:, tb * 128:(tb + 1) * 128], cst[:, c, 512:1024], start=(c == 0), stop=(c == 3))
                    tmm = inc(mm, s_cd)
                    cd["last_mm"] = tmm
                    ab = abst[n % 2]
                    wait(ACT, tmm); wait(DVE, tmm)
                    if n >= 2:
                        wait(ACT, cd["out"][n - 2]); wait(DVE, cd["out"][n - 2])
                    ta = inc(ACT.activation(out=ab[:, 0:512], in_=A_[:], func=AF.Copy), s_ab)
                    td = inc(DVE.tensor_copy(out=ab[:, 512:1024], in_=B_[:]), s_ab)
                    cd["ev"][n] = td
                    wait(SP, td)
                    cd["out"][n] = dma("sync", ab_scr[g4, tb], ab[:], s_abo[n % 2])
            b.barrier(dummies)

    if stop == 2:
        return nc
    with ExitStack() as es:
        es.enter_context(b.scope())
        def sb(name, shape, dt):
            return es.enter_context(nc.sbuf_tensor(name, shape, dt))
        ctt = sb("ctt", [128, NTB, T], BF16)
        stt = sb("stt", [128, NTB, T], BF16)
        abt = [sb(f"abt{i}", [128, NTB, 1024], BF16) for i in range(2)]
        mxf = [sb(f"mxf{i}", [128, T], BF16) for i in range(2)]
        s_c = b.semg("c"); s_ab = b.sem("ab"); s_mm = b.sem("mm"); s_ev = b.sem("ev"); s_o = [b.sem("o"), b.sem("o")]
        tc1 = dma("gpsimd", ctt[:], ct_in.rearrange("(tb p) n -> p tb n", p=128), s_c)
        tc2 = dma("gpsimd", stt[:], st_in.rearrange("(tb p) n -> p tb n", p=128), s_c)
        wait(PE, tc2)
        gcount = 0
        mm_tok = {}
        ev_tok = {}
        o_tok = {}
        ab_last = {}
        ci = 0
        for g4 in range(4):
            if g4 >= 2:
                wait(SP, ab_last[g4 - 2])
            tab = dma("sync", abt[g4 % 2][:], ab_scr[g4].rearrange("tb p n -> p tb n"), s_ab)
            wait(PE, tab)
            for c in range(4):
                mx = mxf[ci % 2]
                for tg in range(4):
                    bank = BK[gcount % 2]
                    if gcount >= 2:
                        wait(PE, ev_tok[gcount - 2])
                    for tb in range(NTB):
                        PE.matmul(bank[:], abt[g4 % 2][:, tb, c * 128:(c + 1) * 128], ctt[:, tb, tg * 512:(tg + 1) * 512], start=(tb == 0), stop=False)
                    for tb in range(NTB):
                        mm = PE.matmul(bank[:], abt[g4 % 2][:, tb, 512 + c * 128:512 + (c + 1) * 128], stt[:, tb, tg * 512:(tg + 1) * 512], start=False, stop=(tb == NTB - 1))
                    tmm = inc(mm, s_mm)
                    ab_last[g4] = tmm
                    wait(ACT, tmm)
                    if tg == 0 and ci >= 2:
                        wait(ACT, o_tok[ci - 2])
                    ev_tok[gcount] = inc(ACT.activation(out=mx[:, tg * 512:(tg + 1) * 512], in_=bank[:], func=AF.Copy), s_ev)
                    gcount += 1
                wait(SP, ev_tok[gcount - 1])
                o_tok[ci] = dma("sync", mix_scr.rearrange("tb p kc t -> p tb kc t")[:, :, 16 + g4 * 4 + c, :], mx[:].rearrange("p (tb t) -> p tb t", t=128), s_o[ci % 2])
                ci += 1
        b.barrier(dummies)

    def as_gemm(tag, KC, NB, a_scr, W, gate_off, res_src, res_dst, ssq=None, NAT=2, kc0=0):
        ncb = D // NB
        with ExitStack() as es:
            es.enter_context(b.scope())
            def sb(name, shape, dt):
                return es.enter_context(nc.sbuf_tensor(name + tag, shape, dt))
            gb = sb("gb", [128, D], F32)
            wb = [sb(f"wb{i}", [128, KC, NB], BF16) for i in range(2)]
            at = [sb(f"at{i}", [128, KC, 128], BF16) for i in range(NAT)]
            xi = [sb(f"xi{i}", [128, NB], F32) for i in range(NAT)]
            xo = [sb(f"xo{i}", [128, NB], F32) for i in range(2)]
            junk = sb("junk", [128, NB], BF16)
            s_g = b.sem("g"); s_w = [b.semg("w0"), b.semg("w1")]; s_a = [b.sem("a") for _ in range(NAT)]
            s_x = [b.sem("x") for _ in range(NAT)]; s_mm = b.sem("mm")
            s_m = b.sem("m"); s_ad = b.sem("ad"); s_o = [b.sem("o"), b.sem("o")]; s_sq = b.sem("sq")
            tg_ = dma("sync", gb[:], mod_row[:, gate_off:gate_off + D].partition_broadcast(128), s_g)
            wait(DVE, tg_)
            mm_tok = {}; m_tok = {}; ad_tok = {}; o_tok = {}; sq_tok = {}; cb_last = {}
            ta_tok = {}; tx_tok = {}; tw_tok = {}
            NI = ncb * NTB

            def loads(idx):
                cb, tb = divmod(idx, NTB)
                if idx >= NAT:
                    wait(SP, mm_tok[idx - NAT])
                ta_tok[idx] = dma("sync", at[idx % NAT][:], a_scr[tb][:, kc0:kc0 + KC, :], s_a[idx % NAT])
                if idx >= NAT:
                    wait(SP, ad_tok[idx - NAT])
                tx_tok[idx] = dma("sync", xi[idx % NAT][:], res_src[tb * 128:(tb + 1) * 128, cb * NB:(cb + 1) * NB], s_x[idx % NAT])

            def wloads(cb):
                if cb >= 2:
                    wait(POOL, cb_last[cb - 2])
                tw_tok[cb] = wload(wb[cb % 2][:], W[kc0 * 128:(kc0 + KC) * 128, :], cb * NB, NB, s_w[cb % 2])

            wloads(0)
            PD = NAT - 1
            for i_ in range(PD):
                loads(i_)
            for idx in range(NI):
                cb, tb = divmod(idx, NTB)
                if tb == 0 and cb + 1 < ncb:
                    wloads(cb + 1)
                if idx + PD < NI:
                    loads(idx + PD)
                bank = BK[idx % 2]
                wait(PE, ta_tok[idx])
                if tb == 0:
                    wait(PE, tw_tok[cb])
                if idx >= 2:
                    wait(PE, m_tok[idx - 2])
                for k in range(KC):
                    mm = PE.matmul(bank[:, 0:NB], at[idx % NAT][:, k, :], wb[cb % 2][:, k, :], start=(k == 0), stop=(k == KC - 1))
                mm_tok[idx] = inc(mm, s_mm)
                cb_last[cb] = mm_tok[idx]
                wait(DVE, mm_tok[idx]); wait(DVE, tx_tok[idx])
                if idx >= 2:
                    wait(DVE, o_tok[idx - 2])
                    if ssq is not None:
                        wait(DVE, sq_tok[idx - 2])
                m_tok[idx] = inc(DVE.tensor_tensor(out=xo[idx % 2][:], in0=bank[:, 0:NB], in1=gb[:, cb * NB:(cb + 1) * NB], op=ALU.mult), s_m)
                wait(DVE, m_tok[idx])
                ad_tok[idx] = inc(DVE.tensor_tensor(out=xo[idx % 2][:], in0=xo[idx % 2][:], in1=xi[idx % NAT][:], op=ALU.add), s_ad)
                if ssq is not None:
                    wait(ACT, ad_tok[idx])
                    sq_tok[idx] = inc(ACT.activation(out=junk[:], in_=xo[idx % 2][:], func=AF.Square, accum_out=ssq[:, cb * NTB + tb:cb * NTB + tb + 1]), s_sq)
                wait(SP, ad_tok[idx])
                o_tok[idx] = dma("sync", res_dst[tb * 128:(tb + 1) * 128, cb * NB:(cb + 1) * NB], xo[idx % 2][:], s_o[idx % 2])
            b.barrier(dummies)

    if stop == 25:
        return nc
    as_gemm("op", 32, 512, mix_scr, w_out, 2 * D, x, x1_scr, NAT=4)
    if stop == 3:
        return nc

    with ExitStack() as es_h:
        h2T = es_h.enter_context(nc.sbuf_tensor("h2T", [128, 32, T], BF16))
        phase_norm(x1_scr, a2, b2, h2T, "n2")
        with ExitStack() as es:
            es.enter_context(b.scope())
            def sb(name, shape, dt):
                return es.enter_context(nc.sbuf_tensor(name, shape, dt))
            NS = 4
            wr = [sb(f"fwr{i}", [128, 32, 128], BF16) for i in range(NS)]
            sgt = [sb(f"sgt{i}", [128, 512], F32) for i in range(2)]
            actst = [sb(f"actst{i}", [128, T], BF16) for i in range(2)]
            s_wr = [b.semg(f"fwr{i}") for i in range(NS)]
            s_mm = b.sem("mm"); s_sg = b.sem("sg"); s_ac = b.sem("ac"); s_o = [b.sem("o"), b.sem("o")]
            f_last = {}
            mm_tok = {}
            sg_tok = {}
            ac_tok = {}
            o_tok = {}
            gidx = 0
            for f in range(NF):
                sl_g = (2 * f) % NS
                sl_u = (2 * f + 1) % NS
                if f >= 2:
                    wait(POOL, f_last[f - 2])
                twg = wload(wr[sl_g][:], w_gate, f * 128, 128, s_wr[sl_g])
                twu = wload(wr[sl_u][:], w_up, f * 128, 128, s_wr[sl_u])
                ast = actst[f % 2]
                for tg in range(4):
                    Bg = BK[(gidx % 2) * 2]
                    Bu = BK[(gidx % 2) * 2 + 1]
                    if tg == 0:
                        wait(PE, twg); wait(PE, twu)
                    if gidx >= 2:
                        wait(PE, ac_tok[gidx - 2])
                    for k in range(32):
                        PE.matmul(Bg[:], wr[sl_g][:, k, :], h2T[:, k, tg * 512:(tg + 1) * 512], start=(k == 0), stop=(k == 31))
                    for k in range(32):
                        mm = PE.matmul(Bu[:], wr[sl_u][:, k, :], h2T[:, k, tg * 512:(tg + 1) * 512], start=(k == 0), stop=(k == 31))
                    mm_tok[gidx] = inc(mm, s_mm)
                    f_last[f] = mm_tok[gidx]
                    sg = sgt[gidx % 2]
                    wait(ACT, mm_tok[gidx])
                    if gidx >= 2:
                        wait(ACT, ac_tok[gidx - 2])
                    sg_tok[gidx] = inc(ACT.activation(out=sg[:], in_=Bg[:], func=AF.Silu), s_sg)
                    wait(DVE, sg_tok[gidx])
                    if tg == 0 and f >= 2:
                        wait(DVE, o_tok[f - 2])
                    ac_tok[gidx] = inc(DVE.tensor_tensor(out=ast[:, tg * 512:(tg + 1) * 512], in0=Bu[:], in1=sg[:], op=ALU.mult), s_ac)
                    gidx += 1
                wait(SP, ac_tok[gidx - 1])
                o_tok[f] = dma("sync", act_scr.rearrange("tb p f t -> p tb f t")[:, :, f, :], ast[:].rearrange("p (tb t) -> p tb t", t=128), s_o[f % 2])
            b.barrier(dummies)

    if stop == 5:
        return nc
    NB6 = 512
    ssq = nc.alloc_sbuf_tensor("ssq", [128, (D // NB6) * NTB], F32)
    DVE.memset(ssq[:], 0.0)
    b.barrier(dummies)
    as_gemm("dn1", 43, NB6, act_scr, w_down, 5 * D, x1_scr, x2_scr, NAT=3, kc0=0)
    as_gemm("dn2", 43, NB6, act_scr, w_down, 5 * D, x2_scr, x3_scr, ssq=ssq, NAT=3, kc0=43)

    with ExitStack() as es:
        es.enter_context(b.scope())
        def sb(name, shape, dt):
            return es.enter_context(nc.sbuf_tensor(name, shape, dt))
        fgb = sb("fgb", [128, D], F32)
        xt = [sb(f"fx{i}", [128, D], F32) for i in range(2)]
        yo = [sb(f"fy{i}", [128, D], F32) for i in range(2)]
        tot = sb("tot", [128, NTB], F32)
        sq = sb("fsq", [128, NTB], F32)
        rs = sb("frs", [128, NTB], F32)
        s_ld = [b.sem("ld"), b.sem("ld")]; s_t = b.sem("t"); s_y = b.sem("y"); s_o = [b.sem("o"), b.sem("o")]; s_fg = b.sem("fg")
        tf = dma("sync", fgb[:], fg.partition_broadcast(128), s_fg)
        ncb = D // NB6
        t = inc(DVE.tensor_reduce(out=tot[:], in_=ssq[:].rearrange("p (cb tb) -> p tb cb", tb=NTB), axis=mybir.AxisListType.X, op=ALU.add), s_t)
        wait(ACT, t)
        t = inc(ACT.activation(out=sq[:], in_=tot[:], func=AF.Sqrt, bias=epsc[:], scale=1.0 / D), s_t)
        wait(DVE, t)
        t = inc(DVE.reciprocal(out=rs[:], in_=sq[:]), s_t)
        wait(DVE, t); wait(DVE, tf)
        y_tok = {}
        o_tok = {}
        ld_tok = {}

        def fload(tb):
            if tb >= 2:
                wait(SP, y_tok[tb - 2])
            ld_tok[tb] = dma("sync", xt[tb % 2][:], x3_scr[tb * 128:(tb + 1) * 128, :], s_ld[tb % 2])

        fload(0)
        for tb in range(NTB):
            if tb + 1 < NTB:
                fload(tb + 1)
            wait(DVE, ld_tok[tb])
            if tb >= 2:
                wait(DVE, o_tok[tb - 2])
            y_tok[tb] = inc(DVE.scalar_tensor_tensor(out=yo[tb % 2][:], in0=xt[tb % 2][:], scalar=rs[:, tb:tb + 1], in1=fgb[:], op0=ALU.mult, op1=ALU.mult), s_y)
            wait(SP, y_tok[tb])
            o_tok[tb] = dma("sync", y[tb * 128:(tb + 1) * 128, :], yo[tb % 2][:], s_o[tb % 2])
        b.barrier(dummies)
    return nc


def _bias_tables(rpb):
    reps = [0, 1, 2, 3, 14, 15]
    sb_ = np.empty((16, 128, 6, 5, 128), np.float32)
    pb_ = np.empty((16, 128, 6, 5, 128), np.float32)
    ql = np.arange(128)
    kl = np.arange(128)
    for ti, i in enumerate(reps):
        base = min(max(i - 2, 0), 11)
        qr = 2 * i + ql // 64
        qc = ql % 64
        rs_ = np.clip(qr - 4, 0, 24)
        cs_ = np.clip(qc - 8, 0, 48)
        for s in range(5):
            kb = base + s
            kr = 2 * kb + kl // 64
            kc = kl % 64
            valid = ((kr[:, None] >= rs_[None, :]) & (kr[:, None] < rs_[None, :] + 8)
                     & (kc[:, None] >= cs_[None, :]) & (kc[:, None] < cs_[None, :] + 16))
            dr = np.clip(kr[:, None] - qr[None, :] + 7, 0, 14)
            dc = np.clip(kc[:, None] - qc[None, :] + 15, 0, 30)
            vals = rpb[:, dr, dc]
            sb_[:, :, ti, s, :] = np.where(valid[None], vals, np.float32(NEG))
            pb_[:, :, ti, s, :] = 0.0 if (kb // 2 == i // 2) else NEG
    return sb_.reshape(16, 128, -1), pb_.reshape(16, 128, -1)


def _dft_tables():
    def cs(n, scale):
        idx = np.arange(n)
        m = (idx[:, None] * idx[None, :]) % n
        ang = 2.0 * np.pi * m / n
        return (np.cos(ang) * scale), (np.sin(ang) * scale)
    c2048, s2048 = cs(2048, 1.0 / np.sqrt(2048 * 512.0))
    c256, s256 = cs(256, 1.0 / np.sqrt(256 * 512.0))
    ctp = np.zeros((2048, 2048)); stp = np.zeros((2048, 2048))
    for bq in range(8):
        ctp[bq * 256:(bq + 1) * 256, bq * 256:(bq + 1) * 256] = c256
        stp[bq * 256:(bq + 1) * 256, bq * 256:(bq + 1) * 256] = s256
    cc, sc = cs(512, 1.0)
    csm = np.concatenate([cc, -sc], axis=1)
    f = np.float32
    return c2048.astype(f), s2048.astype(f), ctp.astype(f), stp.astype(f), csm.astype(f)


def _prep(x_prompt, x_sample, cache_k, cache_v, c, c_ctx, w_ada, b_ada, norm1_g, w_in,
          rpb, w_out, norm2_g, w_gate, w_up, w_down, final_g):
    f = np.float32
    A = lambda a: np.ascontiguousarray(np.asarray(a), dtype=f)
    x_prompt = A(x_prompt); x_sample = A(x_sample)
    sbias, pbias = _bias_tables(A(rpb)[0])
    c2048, s2048, ctp, stp, csm = _dft_tables()
    common = {
        "w_ada": A(w_ada)[0], "b_ada": A(b_ada)[0].reshape(1, -1), "n1g": A(norm1_g)[0].reshape(32, 128),
        "n2g": A(norm2_g)[0].reshape(32, 128), "fg": A(final_g).reshape(1, -1), "w_in": A(w_in)[0],
        "w_out": A(w_out)[0], "w_gate": A(w_gate)[0], "w_up": A(w_up)[0], "w_down": A(w_down)[0],
        "cs": csm, "ident": np.eye(128, dtype=f),
    }
    zkv = np.zeros((256, 2048), f)
    in_maps = []
    for core in range(8):
        m = dict(common)
        if core < 4:
            m["x"] = x_prompt[core * 8:(core + 1) * 8].reshape(T, D)
            m["cvec"] = A(c_ctx).reshape(32, 128)
            m["kc"] = zkv; m["vc"] = zkv
            m["bias"] = pbias
            m["ctxb"] = np.full((128, 1), NEG, f)
            m["ct"] = ctp; m["st"] = stp
        else:
            bi = core - 4
            m["x"] = x_sample[bi]
            m["cvec"] = A(c)[bi].reshape(32, 128)
            m["kc"] = A(cache_k)[bi, 0].reshape(256, 2048)
            m["vc"] = A(cache_v)[bi, 0].reshape(256, 2048)
            m["bias"] = sbias
            m["ctxb"] = np.zeros((128, 1), f)
            m["ct"] = c2048; m["st"] = s2048
        in_maps.append(m)
    return in_maps


def kernel(**inputs):
    f = np.float32
    in_maps = _prep(**inputs)
    nc = build()
    res = run_bass_kernel_spmd(nc, in_maps, core_ids=list(range(8)))
    r = res.results
    y_prompt = np.stack([r[i]["y"] for i in range(4)]).reshape(32, 256, D)
    y_sample = np.stack([r[4 + i]["y"] for i in range(4)])
    nkk = np.stack([r[i]["nk"] for i in range(4)]).reshape(32, 1, 256, 16, 128)
    nvv = np.stack([r[i]["nv"] for i in range(4)]).reshape(32, 1, 256, 16, 128)
    return (y_prompt.astype(f), y_sample.astype(f), nkk.astype(f), nvv.astype(f))
```

```python
from contextlib import ExitStack, contextmanager
import numpy as np
import concourse.bass as bass
import concourse.mybir as mybir
from concourse.bass_utils import run_bass_kernel_spmd

F32 = mybir.dt.float32
BF16 = mybir.dt.bfloat16
AF = mybir.ActivationFunctionType
ALU = mybir.AluOpType

D = 4096
T = 2048
NTB = 16
DFF = 11008
NF = 86
EPS = 1e-6
SCALE = 128.0 ** -0.5
NEG = -30000.0
import os as _os
NOPOOL = bool(_os.environ.get('NOPOOL'))


class Sem:
    def __init__(self, nc, name):
        self.h = nc.alloc_semaphore(name)
        self.v = 0


class Lazy:
    def __init__(self, f):
        self._f = f
        self._v = None

    def get(self):
        if self._v is None:
            self._v = self._f()
        return self._v

    def __getitem__(self, k):
        return self.get()[k]

    def __getattr__(self, n):
        return getattr(self.get(), n)


def _un(a):
    return a.get() if isinstance(a, Lazy) else a


class B:
    def __init__(self):
        self.nc = bass.Bass("TRN2", target_bir_lowering=False)
        self.nsem = 0
        self.dma_sems = {"sync": set(), "gpsimd": set()}
        self.bar = Sem(self.nc, "bar")
        self.bar_n = 0
        self.pool = []
        self.scopes = []

    def sem(self, name):
        self.nsem += 1
        sm = self.pool.pop() if (self.pool and not NOPOOL) else Sem(self.nc, f"{name}_{self.nsem}")
        if self.scopes:
            self.scopes[-1].append(sm)
        return sm

    def semg(self, name):
        self.nsem += 1
        return Sem(self.nc, f"{name}_{self.nsem}")

    @contextmanager
    def scope(self):
        self.scopes.append([])
        yield
        self.pool.extend(self.scopes.pop())

    @staticmethod
    def inc(instr, sem, n=1):
        instr.then_inc(sem.h, n)
        sem.v += n
        return (sem, sem.v)

    @staticmethod
    def wait(eng, tok):
        if tok is not None:
            eng.wait_ge(tok[0].h, tok[1])

    def dma(self, q, out, in_, sem):
        eng = self.nc.sync if q == "sync" else self.nc.gpsimd
        ins = eng.dma_start(out=_un(out), in_=_un(in_))
        self.dma_sems[q].add(sem)
        return self.inc(ins, sem, 16)

    def barrier(self, dummies):
        nc = self.nc
        self.bar_n += 1
        dps, idb, dv, da, dg = dummies
        self.inc(nc.vector.memset(dv[:], 0.0), self.bar)
        self.inc(nc.scalar.activation(out=da[:, 0:1], in_=da[:, 1:2], func=AF.Copy), self.bar)
        for s in self.dma_sems["gpsimd"]:
            nc.gpsimd.wait_ge(s.h, s.v)
        self.inc(nc.gpsimd.memset(dg[:], 0.0), self.bar)
        for s in self.dma_sems["sync"]:
            nc.sync.wait_ge(s.h, s.v)
        nc.sync.sem_inc(self.bar.h, 1)
        self.bar.v += 1
        nc.tensor.wait_ge(self.bar.h, self.bar.v)
        self.inc(nc.tensor.matmul(dps[:, 511:512], idb[:], idb[:, 0:1], start=True, stop=True), self.bar)
        for e in (nc.tensor, nc.vector, nc.scalar, nc.gpsimd, nc.sync):
            e.wait_ge(self.bar.h, self.bar.v)


def build(stop=None, skip0=False):
    b = B()
    nc = b.nc
    inc, wait, dma = b.inc, b.wait, b.dma
    PE, DVE, ACT, POOL, SP = nc.tensor, nc.vector, nc.scalar, nc.gpsimd, nc.sync

    def din(name, shape):
        return Lazy(lambda: nc.dram_tensor(name, shape, F32, kind="ExternalInput").ap())

    def dout(name, shape):
        return Lazy(lambda: nc.dram_tensor(name, shape, F32, kind="ExternalOutput").ap())

    x = din("x", [T, D])
    cvec = din("cvec", [32, 128])
    w_ada = din("w_ada", [D, 6 * D])
    b_ada = din("b_ada", [1, 6 * D])
    n1g = din("n1g", [32, 128])
    n2g = din("n2g", [32, 128])
    fg = din("fg", [1, D])
    w_in = din("w_in", [D, 2 * D])
    w_out = din("w_out", [D, D])
    w_gate = din("w_gate", [D, DFF])
    w_up = din("w_up", [D, DFF])
    w_down = din("w_down", [DFF, D])
    kc_in = din("kc", [256, 2048])
    vc_in = din("vc", [256, 2048])
    bias_in = din("bias", [16, 128, 6 * 5 * 128])
    ctxb_in = din("ctxb", [128, 1])
    ct_in = din("ct", [T, T])
    st_in = din("st", [T, T])
    cs_in = din("cs", [512, 1024])
    ident_in = din("ident", [128, 128])
    y = dout("y", [T, D])
    nk = dout("nk", [T, 2048])
    nv = dout("nv", [T, 2048])
    mod_row = nc.dram_tensor("mod_row", [1, 6 * D], F32).ap()
    DBG = bool(_os.environ.get("DBG"))
    skind = "ExternalOutput" if DBG else "Internal"
    mix_scr = nc.dram_tensor("mix_scr", [NTB, 128, 32, 128], BF16, kind=skind).ap()
    ab_scr = nc.dram_tensor("ab_scr", [4, NTB, 128, 1024], BF16, kind=skind).ap()
    x1_scr = nc.dram_tensor("x1_scr", [T, D], F32, kind=skind).ap()
    act_scr = nc.dram_tensor("act_scr", [NTB, 128, NF, 128], BF16, kind=skind).ap()
    x2_scr = nc.dram_tensor("x2_scr", [T, D], F32, kind=skind).ap()
    x3_scr = nc.dram_tensor("x3_scr", [T, D], F32, kind=skind).ap()

    idf = nc.alloc_sbuf_tensor("idf", [128, 128], F32)
    idb = nc.alloc_sbuf_tensor("idb", [128, 128], BF16)
    onesb = nc.alloc_sbuf_tensor("onesb", [128, 128], BF16)
    modT = nc.alloc_sbuf_tensor("modT", [128, 192], F32)
    a1 = nc.alloc_sbuf_tensor("a1", [128, 32], F32)
    a2 = nc.alloc_sbuf_tensor("a2", [128, 32], F32)
    ctxb = nc.alloc_sbuf_tensor("ctxb_sb", [128, 1], F32)
    epsc = nc.alloc_sbuf_tensor("epsc", [128, 1], F32)
    dv = nc.alloc_sbuf_tensor("dv", [128, 1], F32)
    da = nc.alloc_sbuf_tensor("da", [128, 2], F32)
    dg = nc.alloc_sbuf_tensor("dg", [128, 1], F32)
    BK = [nc.alloc_psum_tensor(f"bk{i}", [128, 512], F32) for i in range(6)]
    PB = [nc.alloc_psum_tensor(f"pb{i}", [128, 1024], BF16) for i in range(2)]
    dummies = (BK[5], idb, dv, da, dg)

    s0 = b.sem("init")
    dma("sync", idf[:], ident_in, s0)
    s0g = b.semg("initg")
    dma("gpsimd", idb[:], ident_in, s0g)
    dma("sync", ctxb[:], ctxb_in, s0)
    for e in (PE, DVE, ACT, POOL):
        e.wait_ge(s0.h, s0.v)
        e.wait_ge(s0g.h, s0g.v)
    DVE.memset(onesb[:], 1.0)
    DVE.memset(epsc[:], EPS)
    s0c = b.sem("initc")
    t = inc(DVE.memset(da[:], 0.0), s0c)
    for e in (PE, ACT, POOL, SP):
        wait(e, t)
    b.barrier(dummies)

    if skip0:
        DVE.memset(modT[:], 0.5)
        DVE.memset(a1[:], 1.5)
        t = inc(DVE.memset(a2[:], 1.5), s0c)
        wait(SP, t)
        dma("sync", mod_row.rearrange("o (p j) -> (o p) j", p=128), modT[:], s0)
        for e in (PE, ACT, POOL, SP):
            wait(e, t)
        b.barrier(dummies)
    if not skip0:
        with ExitStack() as es:
            es.enter_context(b.scope())
            def sb(name, shape, dt):
                return es.enter_context(nc.sbuf_tensor(name, shape, dt))
            cv = sb("cv", [32, 128], F32)
            cvT = sb("cvT", [128, 32], BF16)
            ring = [sb(f"adar{i}", [128, 32, 512], BF16) for i in range(3)]
            brow = [sb(f"brow{i}", [1, 512], F32) for i in range(2)]
            rowsb = [sb(f"rowsb{i}", [1, 512], F32) for i in range(2)]
            mr = [sb(f"mr{i}", [96, 128], F32) for i in range(2)]
            gt = [sb(f"gt{i}", [32, 128], F32) for i in range(2)]
            s_ld = b.sem("ld"); s_w = [b.semg("w") for _ in range(3)]; s_b = [b.sem("b") for _ in range(2)]; s_mm = b.sem("mm"); s_row = b.sem("row")
            s_st = [b.sem("st") for _ in range(2)]; s_a = b.sem("a")
            t = dma("sync", cv[:], cvec, s_ld)
            wait(PE, t)
            t = inc(PE.transpose(BK[0][:, 0:32], cv[:], idf[0:32, 0:32]), s_a)
            wait(ACT, t)
            t_cvT = inc(ACT.activation(out=cvT[:], in_=BK[0][:, 0:32], func=AF.Silu), s_a)
            wait(PE, t_cvT)
            NU = 48
            mm_tok = [None] * NU
            row_tok = [None] * NU
            st_tok = [None] * NU
            for u in range(NU):
                if u >= 3:
                    wait(POOL, mm_tok[u - 3])
                tw = dma("gpsimd", ring[u % 3][:], w_ada[:, u * 512:(u + 1) * 512].rearrange("(kc p) n -> p kc n", p=128), s_w[u % 3])
                if u >= 2:
                    wait(SP, row_tok[u - 2])
                tb_ = dma("sync", brow[u % 2][:], b_ada[:, u * 512:(u + 1) * 512], s_b[u % 2])
                wait(PE, tw)
                if u >= 2:
                    wait(PE, row_tok[u - 2])
                for k in range(32):
                    mm = PE.matmul(BK[u % 2][0:1, :], cvT[:, k:k + 1], ring[u % 3][:, k, :], start=(k == 0), stop=(k == 31))
                mm_tok[u] = inc(mm, s_mm)
                wait(DVE, mm_tok[u]); wait(DVE, tb_)
                if u >= 2:
                    wait(DVE, st_tok[u - 2])
                row_tok[u] = inc(DVE.tensor_tensor(out=rowsb[u % 2][:], in0=BK[u % 2][0:1, :], in1=brow[u % 2][:], op=ALU.add), s_row)
                wait(SP, row_tok[u])
                st_tok[u] = dma("sync", mod_row[:, u * 512:(u + 1) * 512], rowsb[u % 2][:], s_st[u % 2])
            wait(SP, st_tok[NU - 1]); wait(SP, st_tok[NU - 2])
            mrv = mod_row.rearrange("o (j p) -> (o j) p", p=128)
            t0_ = dma("sync", mr[0][:], mrv[0:96, :], s_ld)
            t1_ = dma("sync", mr[1][:], mrv[96:192, :], s_ld)
            t2_ = dma("sync", gt[0][:], n1g, s_ld)
            t3_ = dma("sync", gt[1][:], n2g, s_ld)
            wait(PE, t3_)
            PE.transpose(BK[2][:, 0:96], mr[0][:], idf[0:96, 0:96])
            PE.transpose(BK[2][:, 96:192], mr[1][:], idf[0:96, 0:96])
            PE.transpose(BK[2][:, 192:224], gt[0][:], idf[0:32, 0:32])
            t = inc(PE.transpose(BK[2][:, 224:256], gt[1][:], idf[0:32, 0:32]), s_a)
            wait(DVE, t)
            t = inc(DVE.tensor_copy(out=modT[:], in_=BK[2][:, 0:192]), s_a)
            wait(DVE, t)
            DVE.scalar_tensor_tensor(out=a1[:], in0=modT[:, 32:64], scalar=1.0, in1=BK[2][:, 192:224], op0=ALU.add, op1=ALU.mult)
            DVE.scalar_tensor_tensor(out=a2[:], in0=modT[:, 128:160], scalar=1.0, in1=BK[2][:, 224:256], op0=ALU.add, op1=ALU.mult)
            b.barrier(dummies)
    b1 = modT[:, 0:32]
    b2 = modT[:, 96:128]

    def dbg_out(name, src_ap, shape):
        o = nc.dram_tensor(name, shape, F32, kind="ExternalOutput").ap()
        sd = b.sem("dbg")
        tk = dma("sync", o, src_ap, sd)
        wait(SP, tk)

    if stop == 0:
        dbg_out("d_modT", modT[:], [128, 192])
        dbg_out("d_a1", a1[:], [128, 32])
        return nc

    def phase_norm(src, av, bv, hT, tag):
        with ExitStack() as es:
            es.enter_context(b.scope())
            def sb(name, shape, dt):
                return es.enter_context(nc.sbuf_tensor(name + tag, shape, dt))
            xt = [sb(f"xt{i}", [128, D], F32) for i in range(2)]
            xh = [sb(f"xh{i}", [128, D], BF16) for i in range(2)]
            junk = sb("junk", [128, D], BF16)
            ss = sb("ss", [128, NTB], F32)
            sq = sb("sq", [128, NTB], F32)
            rs = sb("rs", [128, NTB], F32)
            s_ld = [b.sem("ld"), b.sem("ld")]; s_sq = b.sem("sq"); s_rt = b.sem("rt"); s_r = b.sem("r"); s_xh = b.sem("xh")
            s_tr = b.sem("tr"); s_ea = b.sem("ea"); s_ed = b.sem("ed")
            xh_tok = [None] * NTB
            tr_last = [None] * NTB
            ev_tok = {}
            t = inc(DVE.memset(ss[:], 0.0), s_r)
            wait(ACT, t)
            gi = 0
            LVL = int(_os.environ.get("LVL", "9"))
            sq_tok = [None] * NTB
            for tb in range(NTB):
                if tb >= 2:
                    wait(SP, xh_tok[tb - 2] if LVL >= 2 else sq_tok[tb - 2])
                tl = dma("sync", xt[tb % 2][:], src[tb * 128:(tb + 1) * 128, :], s_ld[tb % 2])
                wait(ACT, tl)
                t = inc(ACT.activation(out=junk[:], in_=xt[tb % 2][:], func=AF.Square, accum_out=ss[:, tb:tb + 1]), s_sq)
                sq_tok[tb] = t
                if LVL < 2:
                    continue
                wait(ACT, t)
                t = inc(ACT.activation(out=sq[:, tb:tb + 1], in_=ss[:, tb:tb + 1], func=AF.Sqrt, bias=epsc[:], scale=1.0 / D), s_rt)
                wait(DVE, t)
                t = inc(DVE.reciprocal(out=rs[:, tb:tb + 1], in_=sq[:, tb:tb + 1]), s_r)
                wait(DVE, t)
                if tb >= 2 and LVL >= 3:
                    wait(DVE, tr_last[tb - 2])
                xh_tok[tb] = inc(DVE.tensor_scalar(out=xh[tb % 2][:], in0=xt[tb % 2][:], scalar1=rs[:, tb:tb + 1], scalar2=None, op0=ALU.mult), s_xh)
                if LVL < 3:
                    continue
                wait(PE, xh_tok[tb])
                for g in range(4):
                    pb = PB[gi % 2]
                    if gi >= 2 and LVL >= 4:
                        wait(PE, ev_tok[gi - 2])
                    for j in range(8):
                        kc = g * 8 + j
                        tr = PE.transpose(pb[:, j * 128:(j + 1) * 128], xh[tb % 2][:, kc * 128:(kc + 1) * 128], idb[:])
                    ttr = inc(tr, s_tr)
                    if g == 3:
                        tr_last[tb] = ttr
                    if LVL < 4:
                        gi += 1
                        continue
                    use_act = (gi % 2 == 0)
                    wait(ACT if use_act else DVE, ttr)
                    for j in range(8):
                        kc = g * 8 + j
                        o = hT[:, kc, tb * 128:(tb + 1) * 128]
                        i_ = pb[:, j * 128:(j + 1) * 128]
                        if use_act:
                            iv = ACT.activation(out=o, in_=i_, func=AF.Identity, bias=bv[:, kc:kc + 1], scale=av[:, kc:kc + 1])
                        else:
                            iv = DVE.tensor_scalar(out=o, in0=i_, scalar1=av[:, kc:kc + 1], scalar2=bv[:, kc:kc + 1], op0=ALU.mult, op1=ALU.add)
                    ev_tok[gi] = inc(iv, s_ea if use_act else s_ed)
                    gi += 1
            b.barrier(dummies)

    def wload(dst, W, c0, n, sem):
        return dma("gpsimd", dst, W[:, c0:c0 + n].rearrange("(kc p) n -> p kc n", p=128), sem)

    with ExitStack() as es_h:
        hT = es_h.enter_context(nc.sbuf_tensor("hT", [128, 32, T], BF16))
        phase_norm(x, a1, b1, hT, "n1")
        if stop == 1:
            with nc.sbuf_tensor("dbgh", [128, 32, 128], F32) as dbgh:
                if int(_os.environ.get('LVL', '9')) < 4:
                    DVE.memset(hT[:, :, 0:128], 1.0)
                DVE.tensor_copy(out=dbgh[:], in_=hT[:, :, 0:128])
                b.barrier(dummies)
                dbg_out("d_hT", dbgh[:], [128, 32, 128])
            return nc

        NSLOT = 3
        wr = [es_h.enter_context(nc.sbuf_tensor(f"wr{i}", [128, 32, 128], BF16)) for i in range(NSLOT)]
        s_wr = [b.semg(f"wr{i}") for i in range(NSLOT)]
        s_gm = b.sem("gm")
        state = {"u": 0, "g": 0}
        unit_last_grp = {}
        grp_rel = {}

        deferred = []

        def ws_unit_gen(W, c0, epilogue):
            u = state["u"]; state["u"] += 1
            slot = u % NSLOT
            if u >= NSLOT:
                wait(POOL, unit_last_grp[u - NSLOT])
            tw = wload(wr[slot][:], W, c0, 128, s_wr[slot])
            for tg in range(4):
                g = state["g"]; state["g"] += 1
                bank = BK[g % 2]
                for k in range(32):
                    if k == 0:
                        if tg == 0:
                            wait(PE, tw)
                        if g >= 2:
                            for tk in grp_rel[g - 2]:
                                wait(PE, tk)
                    mm = PE.matmul(bank[:], wr[slot][:, k, :], hT[:, k, tg * 512:(tg + 1) * 512], start=(k == 0), stop=(k == 31))
                    if k % 8 == 7 and k != 31:
                        yield
                tk = inc(mm, s_gm)
                if tg == 3:
                    unit_last_grp[u] = tk
                while deferred:
                    deferred.pop(0)()
                grp_rel[g] = epilogue(tg, bank, tk)
                yield

        def ws_unit(W, c0, epilogue):
            for _ in ws_unit_gen(W, c0, epilogue):
                pass

        with ExitStack() as es:
            es.enter_context(b.scope())
            def sb(name, shape, dt):
                return es.enter_context(nc.sbuf_tensor(name, shape, dt))
            qTs = [sb(f"qT{i}", [128, T], BF16) for i in range(2)]
            kTs = [sb(f"kT{i}", [128, T], BF16) for i in range(2)]
            vbfs = [sb(f"vbf{i}", [128, NTB, 128], BF16) for i in range(2)]
            fst = sb("fst", [128, 512], F32)
            kvo = sb("kvo", [128, 4, 128], F32)
            biast = sb("biast", [128, 6 * 5 * 128], F32)
            pt = sb("PT", [128, 7 * 128], BF16)
            ri = sb("rinv", [128, 128], F32)
            mixst = sb("mixst", [128, NTB, 128], BF16)
            kcs2 = [sb(f"kcs{i}", [128, 2, 128], BF16) for i in range(2)]
            kcT = sb("kcT", [128, 256], BF16)
            vcs2 = [sb(f"vcs{i}", [128, 2, 128], BF16) for i in range(2)]
            att_hist = {}
            s_q = b.sem("q"); s_kA = b.sem("kA"); s_kD = b.sem("kD"); s_tr = b.sem("tr"); s_ko = b.sem("ko")
            s_va = b.sem("va"); s_st = b.sem("st"); s_bias = b.sem("bias"); s_ctx = [b.semg("ctx"), b.semg("ctx")]; s_kct = b.sem("kct")
            s_qk = b.sem("qk"); s_e1 = b.sem("e1"); s_ex = b.sem("ex"); s_pv = b.sem("pv")
            s_ri = b.sem("ri"); s_no = b.sem("no"); s_mx = b.sem("mx")
            fs_n = {"n": 0}
            tr_rel = {}
            fst_rel = {}
            kvo_rel = {}
            head_pend = {}
            head_ctx = {}
            att = {"n": 0, "pv": {}, "ex": {}, "no": {}, "mx": None}

            def make_ep_q(j):
                def ep_q(tg, bank, tk):
                    wait(ACT, tk)
                    if tg == 0:
                        wait(ACT, att_hist.get(j - 2))
                    t_ = inc(ACT.activation(out=qTs[j % 2][:, tg * 512:(tg + 1) * 512], in_=bank[:], func=AF.Copy), s_q)
                    head_pend[j].append(t_)
                    return [t_]
                return ep_q

            def make_ep_kv(is_k, j):
                def ep(tg, bank, tk):
                    n = fs_n["n"]; fs_n["n"] += 1
                    f = fst
                    wait(DVE, tk)
                    if n >= 1:
                        for tk2 in fst_rel[n - 1]:
                            wait(DVE, tk2)
                    tf = inc(DVE.tensor_copy(out=f[:], in_=bank[:]), s_kD)
                    rel = [tf]
                    fst_rel[n] = []
                    if is_k:
                        wait(ACT, tf)
                        if tg == 0:
                            wait(ACT, att_hist.get(j - 2))
                        t_ = inc(ACT.activation(out=kTs[j % 2][:, tg * 512:(tg + 1) * 512], in_=f[:], func=AF.Copy), s_kA)
                        fst_rel[n].append(t_)
                        head_pend[j].append(t_)
                    def part_b():
                        wait(PE, tf)
                        if n >= 1:
                            for tk2 in tr_rel[n - 1]:
                                wait(PE, tk2)
                        for bb in range(4):
                            tr = PE.transpose(BK[2][:, bb * 128:(bb + 1) * 128], f[:, bb * 128:(bb + 1) * 128], idf[:])
                        ttr = inc(tr, s_tr)
                        fst_rel[n].append(ttr)
                        ko = kvo
                        wait(DVE, ttr)
                        if n >= 1:
                            for tk2 in kvo_rel[n - 1]:
                                wait(DVE, tk2)
                        tko = inc(DVE.tensor_copy(out=ko[:].rearrange("p b d -> p (b d)"), in_=BK[2][:]), s_ko)
                        tr_rel[n] = [tko]
                        kvo_rel[n] = []
                        if not is_k:
                            wait(ACT, tko)
                            if tg == 0:
                                wait(ACT, att_hist.get(j - 2))
                            t_ = inc(ACT.activation(out=vbfs[j % 2][:, tg * 4:(tg + 1) * 4, :].rearrange("p b d -> p (b d)"), in_=ko[:].rearrange("p b d -> p (b d)"), func=AF.Copy), s_va)
                            kvo_rel[n].append(t_)
                            head_pend[j].append(t_)
                        wait(SP, tko)
                        dst = (nk if is_k else nv)[tg * 512:(tg + 1) * 512, j * 128:(j + 1) * 128].rearrange("(b p) d -> p b d", p=128)
                        kvo_rel[n].append(dma("sync", dst, ko[:], s_st))
                    deferred.append(part_b)
                    return rel
                return ep

            def head_gen(j):
                head_pend[j] = []
                kcs = kcs2[j % 2]; vcs = vcs2[j % 2]
                wait(POOL, att_hist.get(j - 2))
                dma("gpsimd", kcs[:], kc_in[:, j * 128:(j + 1) * 128].rearrange("(b p) d -> p b d", p=128), s_ctx[j % 2])
                t_vc = dma("gpsimd", vcs[:], vc_in[:, j * 128:(j + 1) * 128].rearrange("(b p) d -> p b d", p=128), s_ctx[j % 2])
                head_ctx[j] = (kcs, vcs, t_vc)
                yield from ws_unit_gen(w_in, j * 128, make_ep_q(j))
                yield from ws_unit_gen(w_in, 2048 + j * 128, make_ep_kv(True, j))
                yield from ws_unit_gen(w_in, 4096 + j * 128, make_ep_kv(False, j))

            def tmap(i):
                return {0: 0, 1: 1, 14: 4, 15: 5}.get(i, 2 + (i % 2))

            for _ in head_gen(0):
                pass
            while deferred:
                deferred.pop(0)()
            for j in range(16):
                gnext = [head_gen(j + 1)] if j + 1 < 16 else [None]

                def pull(cnt):
                    for _ in range(cnt):
                        if gnext[0] is not None:
                            try:
                                next(gnext[0])
                            except StopIteration:
                                gnext[0] = None

                qT = qTs[j % 2]; kT = kTs[j % 2]; vbf = vbfs[j % 2]
                kcs, vcs, t_vc = head_ctx[j]
                wait(SP, att_hist.get(j - 1))
                t_bias = dma("sync", biast[:], bias_in[j], s_bias)
                wait(PE, t_vc)
                for bb in range(2):
                    tr = PE.transpose(PB[0][:, bb * 128:(bb + 1) * 128], kcs[:, bb, :], idb[:])
                ttr = inc(tr, s_kct)
                wait(ACT, ttr)
                wait(ACT, att_hist.get(j - 1))
                t_kcT = inc(ACT.activation(out=kcT[:], in_=PB[0][:, 0:256], func=AF.Copy), s_kct)
                pull(4)
                for tk in head_pend[j] + [t_kcT]:
                    wait(PE, tk)
                wait(DVE, t_bias)
                wait(DVE, att["mx"])
                n0 = att["n"]; att["n"] += 16
                X, Y, Z = BK[3], BK[4], BK[5]

                def qk_stage(i):
                    n = n0 + i
                    base = min(max(i - 2, 0), 11)
                    ty = tmap(i)
                    if n >= 1:
                        wait(PE, att["ex"][n - 1])
                    for s_ in range(5):
                        kb = base + s_
                        dst = X[:, s_ * 128:(s_ + 1) * 128] if s_ < 4 else Y[:, 0:128]
                        PE.matmul(dst, kT[:, kb * 128:(kb + 1) * 128], qT[:, i * 128:(i + 1) * 128], start=True, stop=True)
                    for s_ in range(2):
                        mm = PE.matmul(Y[:, (1 + s_) * 128:(2 + s_) * 128], kcT[:, s_ * 128:(s_ + 1) * 128], qT[:, i * 128:(i + 1) * 128], start=True, stop=True)
                    tqk = inc(mm, s_qk)
                    wait(DVE, tqk)
                    bo = ty * 640
                    DVE.scalar_tensor_tensor(out=X[:], in0=X[:], scalar=SCALE, in1=biast[:, bo:bo + 512], op0=ALU.mult, op1=ALU.add)
                    te = inc(DVE.scalar_tensor_tensor(out=Y[:, 0:128], in0=Y[:, 0:128], scalar=SCALE, in1=biast[:, bo + 512:bo + 640], op0=ALU.mult, op1=ALU.add), s_e1)
                    wait(ACT, te)
                    if n >= 1:
                        wait(ACT, att["pv"][n - 1])
                    ACT.activation(out=pt[:, 0:512], in_=X[:], func=AF.Exp)
                    ACT.activation(out=pt[:, 512:640], in_=Y[:, 0:128], func=AF.Exp)
                    att["ex"][n] = inc(ACT.activation(out=pt[:, 640:896], in_=Y[:, 128:384], func=AF.Exp, bias=ctxb[:], scale=SCALE), s_ex)

                def pv_stage(i):
                    n = n0 + i
                    base = min(max(i - 2, 0), 11)
                    wait(PE, att["ex"][n])
                    if n >= 1:
                        wait(PE, att["no"][n - 1])
                    for s_ in range(7):
                        lhs = vbf[:, base + s_, :] if s_ < 5 else vcs[:, s_ - 5, :]
                        PE.matmul(Z[:, 0:128], lhs, pt[:, s_ * 128:(s_ + 1) * 128], start=(s_ == 0), stop=(s_ == 6))
                    for s_ in range(7):
                        mm = PE.matmul(Z[:, 128:256], onesb[:], pt[:, s_ * 128:(s_ + 1) * 128], start=(s_ == 0), stop=(s_ == 6))
                    tpv_ = inc(mm, s_pv)
                    att["pv"][n] = tpv_
                    wait(DVE, tpv_)
                    tri = inc(DVE.reciprocal(out=ri[:], in_=Z[:, 128:256]), s_ri)
                    wait(DVE, tri)
                    tno_ = inc(DVE.tensor_tensor(out=mixst[:, i, :], in0=Z[:, 0:128], in1=ri[:], op=ALU.mult), s_no)
                    att["no"][n] = tno_
                    return tpv_, tno_

                for i in range(16):
                    qk_stage(i)
                    pull(3)
                    tpv, tno = pv_stage(i)
                pull(1000)
                while deferred:
                    deferred.pop(0)()
                wait(SP, tno)
                att["mx"] = dma("sync", mix_scr.rearrange("tb p kc t -> p tb kc t")[:, :, j, :], mixst[:], s_mx)
                att_hist[j] = tpv
            b.barrier(dummies)

        with ExitStack() as es:
            es.enter_context(b.scope())
            def sb(name, shape, dt):
                return es.enter_context(nc.sbuf_tensor(name, shape, dt))
            uT = [sb(f"uT{i}", [128, T], BF16) for i in range(4)]
            cst = sb("cst", [128, 4, 1024], BF16)
            abst = [sb(f"abst{i}", [128, 1024], BF16) for i in range(2)]
            s_u = b.sem("u"); s_cs = b.semg("cs"); s_cd = b.sem("cd"); s_ab = b.sem("ab"); s_abo = [b.sem("abo"), b.sem("abo")]
            t_cs = dma("gpsimd", cst[:], cs_in.rearrange("(c p) n -> p c n", p=128), s_cs)
            cd = {"n": 0, "ev": {}, "out": {}, "last_mm": None}

            def make_ep_u(c):
                def ep(tg, bank, tk):
                    wait(ACT, tk)
                    if tg == 0:
                        wait(ACT, cd["last_mm"])
                    return [inc(ACT.activation(out=uT[c][:, tg * 512:(tg + 1) * 512], in_=bank[:], func=AF.Copy), s_u)]
                return ep

            for g4 in range(4):
                for c in range(4):
                    ws_unit(w_in, 6144 + g4 * 512 + c * 128, make_ep_u(c))
                last_g = state["g"] - 1
                for gg in range(last_g - 15, last_g + 1):
                    for tk in grp_rel[gg]:
                        wait(PE, tk)
                wait(PE, t_cs)
                for tb in range(NTB):
                    n = cd["n"]; cd["n"] += 1
                    A_, B_ = BK[3], BK[4]
                    if n >= 1:
                        wait(PE, cd["ev"][n - 1])
                    for c in range(4):
                        PE.matmul(A_[:], uT[c][:, tb * 128:(tb + 1) * 128], cst[:, c, 0:512], start=(c == 0), stop=(c == 3))
                    for c in range(4):
                        mm = PE.matmul(B_[:], uT[c][:, tb * 128:(tb + 1) * 128], cst[:, c, 512:1024], start=(c == 0), stop=(c == 3))
                    tmm = inc(mm, s_cd)
                    cd["last_mm"] = tmm
                    ab = abst[n % 2]
                    wait(ACT, tmm); wait(DVE, tmm)
                    if n >= 2:
                        wait(ACT, cd["out"][n - 2]); wait(DVE, cd["out"][n - 2])
                    ta = inc(ACT.activation(out=ab[:, 0:512], in_=A_[:], func=AF.Copy), s_ab)
                    td = inc(DVE.tensor_copy(out=ab[:, 512:1024], in_=B_[:]), s_ab)
                    cd["ev"][n] = td
                    wait(SP, td)
                    cd["out"][n] = dma("sync", ab_scr[g4, tb], ab[:], s_abo[n % 2])
            b.barrier(dummies)

    if stop == 2:
        return nc
    with ExitStack() as es:
        es.enter_context(b.scope())
        def sb(name, shape, dt):
            return es.enter_context(nc.sbuf_tensor(name, shape, dt))
        ctt = sb("ctt", [128, NTB, T], BF16)
        stt = sb("stt", [128, NTB, T], BF16)
        abt = [sb(f"abt{i}", [128, NTB, 1024], BF16) for i in range(2)]
        mxf = [sb(f"mxf{i}", [128, T], BF16) for i in range(2)]
        s_c = b.semg("c"); s_ab = b.sem("ab"); s_mm = b.sem("mm"); s_ev = b.sem("ev"); s_o = [b.sem("o"), b.sem("o")]
        tc1 = dma("gpsimd", ctt[:], ct_in.rearrange("(tb p) n -> p tb n", p=128), s_c)
        tc2 = dma("gpsimd", stt[:], st_in.rearrange("(tb p) n -> p tb n", p=128), s_c)
        wait(PE, tc2)
        gcount = 0
        mm_tok = {}
        ev_tok = {}
        o_tok = {}
        ab_last = {}
        ci = 0
        for g4 in range(4):
            if g4 >= 2:
                wait(SP, ab_last[g4 - 2])
            tab = dma("sync", abt[g4 % 2][:], ab_scr[g4].rearrange("tb p n -> p tb n"), s_ab)
            wait(PE, tab)
            for c in range(4):
                mx = mxf[ci % 2]
                for tg in range(4):
                    bank = BK[gcount % 2]
                    if gcount >= 2:
                        wait(PE, ev_tok[gcount - 2])
                    for tb in range(NTB):
                        PE.matmul(bank[:], abt[g4 % 2][:, tb, c * 128:(c + 1) * 128], ctt[:, tb, tg * 512:(tg + 1) * 512], start=(tb == 0), stop=False)
                    for tb in range(NTB):
                        mm = PE.matmul(bank[:], abt[g4 % 2][:, tb, 512 + c * 128:512 + (c + 1) * 128], stt[:, tb, tg * 512:(tg + 1) * 512], start=False, stop=(tb == NTB - 1))
                    tmm = inc(mm, s_mm)
                    ab_last[g4] = tmm
                    wait(ACT, tmm)
                    if tg == 0 and ci >= 2:
                        wait(ACT, o_tok[ci - 2])
                    ev_tok[gcount] = inc(ACT.activation(out=mx[:, tg * 512:(tg + 1) * 512], in_=bank[:], func=AF.Copy), s_ev)
                    gcount += 1
                wait(SP, ev_tok[gcount - 1])
                o_tok[ci] = dma("sync", mix_scr.rearrange("tb p kc t -> p tb kc t")[:, :, 16 + g4 * 4 + c, :], mx[:].rearrange("p (tb t) -> p tb t", t=128), s_o[ci % 2])
                ci += 1
        b.barrier(dummies)

    def as_gemm(tag, KC, NB, a_scr, W, gate_off, res_src, res_dst, ssq=None, NAT=2, kc0=0):
        ncb = D // NB
        with ExitStack() as es:
            es.enter_context(b.scope())
            def sb(name, shape, dt):
                return es.enter_context(nc.sbuf_tensor(name + tag, shape, dt))
            gb = sb("gb", [128, D], F32)
            wb = [sb(f"wb{i}", [128, KC, NB], BF16) for i in range(2)]
            at = [sb(f"at{i}", [128, KC, 128], BF16) for i in range(NAT)]
            xi = [sb(f"xi{i}", [128, NB], F32) for i in range(NAT)]
            xo = [sb(f"xo{i}", [128, NB], F32) for i in range(2)]
            junk = sb("junk", [128, NB], BF16)
            s_g = b.sem("g"); s_w = [b.semg("w0"), b.semg("w1")]; s_a = [b.sem("a") for _ in range(NAT)]
            s_x = [b.sem("x") for _ in range(NAT)]; s_mm = b.sem("mm")
            s_m = b.sem("m"); s_ad = b.sem("ad"); s_o = [b.sem("o"), b.sem("o")]; s_sq = b.sem("sq")
            tg_ = dma("sync", gb[:], mod_row[:, gate_off:gate_off + D].partition_broadcast(128), s_g)
            wait(DVE, tg_)
            mm_tok = {}; m_tok = {}; ad_tok = {}; o_tok = {}; sq_tok = {}; cb_last = {}
            ta_tok = {}; tx_tok = {}; tw_tok = {}
            NI = ncb * NTB

            def loads(idx):
                cb, tb = divmod(idx, NTB)
                if idx >= NAT:
                    wait(SP, mm_tok[idx - NAT])
                ta_tok[idx] = dma("sync", at[idx % NAT][:], a_scr[tb][:, kc0:kc0 + KC, :], s_a[idx % NAT])
                if idx >= NAT:
                    wait(SP, ad_tok[idx - NAT])
                tx_tok[idx] = dma("sync", xi[idx % NAT][:], res_src[tb * 128:(tb + 1) * 128, cb * NB:(cb + 1) * NB], s_x[idx % NAT])

            def wloads(cb):
                if cb >= 2:
                    wait(POOL, cb_last[cb - 2])
                tw_tok[cb] = wload(wb[cb % 2][:], W[kc0 * 128:(kc0 + KC) * 128, :], cb * NB, NB, s_w[cb % 2])

            wloads(0)
            PD = NAT - 1
            for i_ in range(PD):
                loads(i_)
            for idx in range(NI):
                cb, tb = divmod(idx, NTB)
                if tb == 0 and cb + 1 < ncb:
                    wloads(cb + 1)
                if idx + PD < NI:
                    loads(idx + PD)
                bank = BK[idx % 2]
                wait(PE, ta_tok[idx])
                if tb == 0:
                    wait(PE, tw_tok[cb])
                if idx >= 2:
                    wait(PE, m_tok[idx - 2])
                for k in range(KC):
                    mm = PE.matmul(bank[:, 0:NB], at[idx % NAT][:, k, :], wb[cb % 2][:, k, :], start=(k == 0), stop=(k == KC - 1))
                mm_tok[idx] = inc(mm, s_mm)
                cb_last[cb] = mm_tok[idx]
                wait(DVE, mm_tok[idx]); wait(DVE, tx_tok[idx])
                if idx >= 2:
                    wait(DVE, o_tok[idx - 2])
                    if ssq is not None:
                        wait(DVE, sq_tok[idx - 2])
                m_tok[idx] = inc(DVE.tensor_tensor(out=xo[idx % 2][:], in0=bank[:, 0:NB], in1=gb[:, cb * NB:(cb + 1) * NB], op=ALU.mult), s_m)
                wait(DVE, m_tok[idx])
                ad_tok[idx] = inc(DVE.tensor_tensor(out=xo[idx % 2][:], in0=xo[idx % 2][:], in1=xi[idx % NAT][:], op=ALU.add), s_ad)
                if ssq is not None:
                    wait(ACT, ad_tok[idx])
                    sq_tok[idx] = inc(ACT.activation(out=junk[:], in_=xo[idx % 2][:], func=AF.Square, accum_out=ssq[:, cb * NTB + tb:cb * NTB + tb + 1]), s_sq)
                wait(SP, ad_tok[idx])
                o_tok[idx] = dma("sync", res_dst[tb * 128:(tb + 1) * 128, cb * NB:(cb + 1) * NB], xo[idx % 2][:], s_o[idx % 2])
            b.barrier(dummies)

    if stop == 25:
        return nc
    as_gemm("op", 32, 512, mix_scr, w_out, 2 * D, x, x1_scr, NAT=4)
    if stop == 3:
        return nc

    with ExitStack() as es_h:
        h2T = es_h.enter_context(nc.sbuf_tensor("h2T", [128, 32, T], BF16))
        phase_norm(x1_scr, a2, b2, h2T, "n2")
        with ExitStack() as es:
            es.enter_context(b.scope())
            def sb(name, shape, dt):
                return es.enter_context(nc.sbuf_tensor(name, shape, dt))
            NS = 4
            wr = [sb(f"fwr{i}", [128, 32, 128], BF16) for i in range(NS)]
            sgt = [sb(f"sgt{i}", [128, 512], F32) for i in range(2)]
            actst = [sb(f"actst{i}", [128, T], BF16) for i in range(2)]
            s_wr = [b.semg(f"fwr{i}") for i in range(NS)]
            s_mm = b.sem("mm"); s_sg = b.sem("sg"); s_ac = b.sem("ac"); s_o = [b.sem("o"), b.sem("o")]
            f_last = {}
            mm_tok = {}
            sg_tok = {}
            ac_tok = {}
            o_tok = {}
            gidx = 0
            for f in range(NF):
                sl_g = (2 * f) % NS
                sl_u = (2 * f + 1) % NS
                if f >= 2:
                    wait(POOL, f_last[f - 2])
                twg = wload(wr[sl_g][:], w_gate, f * 128, 128, s_wr[sl_g])
                twu = wload(wr[sl_u][:], w_up, f * 128, 128, s_wr[sl_u])
                ast = actst[f % 2]
                for tg in range(4):
                    Bg = BK[(gidx % 2) * 2]
                    Bu = BK[(gidx % 2) * 2 + 1]
                    if tg == 0:
                        wait(PE, twg); wait(PE, twu)
                    if gidx >= 2:
                        wait(PE, ac_tok[gidx - 2])
                    for k in range(32):
                        PE.matmul(Bg[:], wr[sl_g][:, k, :], h2T[:, k, tg * 512:(tg + 1) * 512], start=(k == 0), stop=(k == 31))
                    for k in range(32):
                        mm = PE.matmul(Bu[:], wr[sl_u][:, k, :], h2T[:, k, tg * 512:(tg + 1) * 512], start=(k == 0), stop=(k == 31))
                    mm_tok[gidx] = inc(mm, s_mm)
                    f_last[f] = mm_tok[gidx]
                    sg = sgt[gidx % 2]
                    wait(ACT, mm_tok[gidx])
                    if gidx >= 2:
                        wait(ACT, ac_tok[gidx - 2])
                    sg_tok[gidx] = inc(ACT.activation(out=sg[:], in_=Bg[:], func=AF.Silu), s_sg)
                    wait(DVE, sg_tok[gidx])
                    if tg == 0 and f >= 2:
                        wait(DVE, o_tok[f - 2])
                    ac_tok[gidx] = inc(DVE.tensor_tensor(out=ast[:, tg * 512:(tg + 1) * 512], in0=Bu[:], in1=sg[:], op=ALU.mult), s_ac)
                    gidx += 1
                wait(SP, ac_tok[gidx - 1])
                o_tok[f] = dma("sync", act_scr.rearrange("tb p f t -> p tb f t")[:, :, f, :], ast[:].rearrange("p (tb t) -> p tb t", t=128), s_o[f % 2])
            b.barrier(dummies)

    if stop == 5:
        return nc
    NB6 = 512
    ssq = nc.alloc_sbuf_tensor("ssq", [128, (D // NB6) * NTB], F32)
    DVE.memset(ssq[:], 0.0)
    b.barrier(dummies)
    as_gemm("dn1", 43, NB6, act_scr, w_down, 5 * D, x1_scr, x2_scr, NAT=3, kc0=0)
    as_gemm("dn2", 43, NB6, act_scr, w_down, 5 * D, x2_scr, x3_scr, ssq=ssq, NAT=3, kc0=43)

    with ExitStack() as es:
        es.enter_context(b.scope())
        def sb(name, shape, dt):
            return es.enter_context(nc.sbuf_tensor(name, shape, dt))
        fgb = sb("fgb", [128, D], F32)
        xt = [sb(f"fx{i}", [128, D], F32) for i in range(2)]
        yo = [sb(f"fy{i}", [128, D], F32) for i in range(2)]
        tot = sb("tot", [128, NTB], F32)
        sq = sb("fsq", [128, NTB], F32)
        rs = sb("frs", [128, NTB], F32)
        s_ld = [b.sem("ld"), b.sem("ld")]; s_t = b.sem("t"); s_y = b.sem("y"); s_o = [b.sem("o"), b.sem("o")]; s_fg = b.sem("fg")
        tf = dma("sync", fgb[:], fg.partition_broadcast(128), s_fg)
        ncb = D // NB6
        t = inc(DVE.tensor_reduce(out=tot[:], in_=ssq[:].rearrange("p (cb tb) -> p tb cb", tb=NTB), axis=mybir.AxisListType.X, op=ALU.add), s_t)
        wait(ACT, t)
        t = inc(ACT.activation(out=sq[:], in_=tot[:], func=AF.Sqrt, bias=epsc[:], scale=1.0 / D), s_t)
        wait(DVE, t)
        t = inc(DVE.reciprocal(out=rs[:], in_=sq[:]), s_t)
        wait(DVE, t); wait(DVE, tf)
        y_tok = {}
        o_tok = {}
        ld_tok = {}

        def fload(tb):
            if tb >= 2:
                wait(SP, y_tok[tb - 2])
            ld_tok[tb] = dma("sync", xt[tb % 2][:], x3_scr[tb * 128:(tb + 1) * 128, :], s_ld[tb % 2])

        fload(0)
        for tb in range(NTB):
            if tb + 1 < NTB:
                fload(tb + 1)
            wait(DVE, ld_tok[tb])
            if tb >= 2:
                wait(DVE, o_tok[tb - 2])
            y_tok[tb] = inc(DVE.scalar_tensor_tensor(out=yo[tb % 2][:], in0=xt[tb % 2][:], scalar=rs[:, tb:tb + 1], in1=fgb[:], op0=ALU.mult, op1=ALU.mult), s_y)
            wait(SP, y_tok[tb])
            o_tok[tb] = dma("sync", y[tb * 128:(tb + 1) * 128, :], yo[tb % 2][:], s_o[tb % 2])
        b.barrier(dummies)
    return nc


def _bias_tables(rpb):
    reps = [0, 1, 2, 3, 14, 15]
    sb_ = np.empty((16, 128, 6, 5, 128), np.float32)
    pb_ = np.empty((16, 128, 6, 5, 128), np.float32)
    ql = np.arange(128)
    kl = np.arange(128)
    for ti, i in enumerate(reps):
        base = min(max(i - 2, 0), 11)
        qr = 2 * i + ql // 64
        qc = ql % 64
        rs_ = np.clip(qr - 4, 0, 24)
        cs_ = np.clip(qc - 8, 0, 48)
        for s in range(5):
            kb = base + s
            kr = 2 * kb + kl // 64
            kc = kl % 64
            valid = ((kr[:, None] >= rs_[None, :]) & (kr[:, None] < rs_[None, :] + 8)
                     & (kc[:, None] >= cs_[None, :]) & (kc[:, None] < cs_[None, :] + 16))
            dr = np.clip(kr[:, None] - qr[None, :] + 7, 0, 14)
            dc = np.clip(kc[:, None] - qc[None, :] + 15, 0, 30)
            vals = rpb[:, dr, dc]
            sb_[:, :, ti, s, :] = np.where(valid[None], vals, np.float32(NEG))
            pb_[:, :, ti, s, :] = 0.0 if (kb // 2 == i // 2) else NEG
    return sb_.reshape(16, 128, -1), pb_.reshape(16, 128, -1)


def _dft_tables():
    def cs(n, scale):
        idx = np.arange(n)
        m = (idx[:, None] * idx[None, :]) % n
        ang = 2.0 * np.pi * m / n
        return (np.cos(ang) * scale), (np.sin(ang) * scale)
    c2048, s2048 = cs(2048, 1.0 / np.sqrt(2048 * 512.0))
    c256, s256 = cs(256, 1.0 / np.sqrt(256 * 512.0))
    ctp = np.zeros((2048, 2048)); stp = np.zeros((2048, 2048))
    for bq in range(8):
        ctp[bq * 256:(bq + 1) * 256, bq * 256:(bq + 1) * 256] = c256
        stp[bq * 256:(bq + 1) * 256, bq * 256:(bq + 1) * 256] = s256
    cc, sc = cs(512, 1.0)
    csm = np.concatenate([cc, -sc], axis=1)
    f = np.float32
    return c2048.astype(f), s2048.astype(f), ctp.astype(f), stp.astype(f), csm.astype(f)


def _prep(x_prompt, x_sample, cache_k, cache_v, c, c_ctx, w_ada, b_ada, norm1_g, w_in,
          rpb, w_out, norm2_g, w_gate, w_up, w_down, final_g):
    f = np.float32
    A = lambda a: np.ascontiguousarray(np.asarray(a), dtype=f)
    x_prompt = A(x_prompt); x_sample = A(x_sample)
    sbias, pbias = _bias_tables(A(rpb)[0])
    c2048, s2048, ctp, stp, csm = _dft_tables()
    common = {
        "w_ada": A(w_ada)[0], "b_ada": A(b_ada)[0].reshape(1, -1), "n1g": A(norm1_g)[0].reshape(32, 128),
        "n2g": A(norm2_g)[0].reshape(32, 128), "fg": A(final_g).reshape(1, -1), "w_in": A(w_in)[0],
        "w_out": A(w_out)[0], "w_gate": A(w_gate)[0], "w_up": A(w_up)[0], "w_down": A(w_down)[0],
        "cs": csm, "ident": np.eye(128, dtype=f),
    }
    zkv = np.zeros((256, 2048), f)
    in_maps = []
    for core in range(8):
        m = dict(common)
        if core < 4:
            m["x"] = x_prompt[core * 8:(core + 1) * 8].reshape(T, D)
            m["cvec"] = A(c_ctx).reshape(32, 128)
            m["kc"] = zkv; m["vc"] = zkv
            m["bias"] = pbias
            m["ctxb"] = np.full((128, 1), NEG, f)
            m["ct"] = ctp; m["st"] = stp
        else:
            bi = core - 4
            m["x"] = x_sample[bi]
            m["cvec"] = A(c)[bi].reshape(32, 128)
            m["kc"] = A(cache_k)[bi, 0].reshape(256, 2048)
            m["vc"] = A(cache_v)[bi, 0].reshape(256, 2048)
            m["bias"] = sbias
            m["ctxb"] = np.zeros((128, 1), f)
            m["ct"] = c2048; m["st"] = s2048
        in_maps.append(m)
    return in_maps


def kernel(**inputs):
    f = np.float32
    in_maps = _prep(**inputs)
    nc = build()
    res = run_bass_kernel_spmd(nc, in_maps, core_ids=list(range(8)))
    r = res.results
    y_prompt = np.stack([r[i]["y"] for i in range(4)]).reshape(32, 256, D)
    y_sample = np.stack([r[4 + i]["y"] for i in range(4)])
    nkk = np.stack([r[i]["nk"] for i in range(4)]).reshape(32, 1, 256, 16, 128)
    nvv = np.stack([r[i]["nv"] for i in range(4)]).reshape(32, 1, 256, 16, 128)
    return (y_prompt.astype(f), y_sample.astype(f), nkk.astype(f), nvv.astype(f))
```

```python
from contextlib import ExitStack, contextmanager
import numpy as np
import concourse.bass as bass
import concourse.mybir as mybir
from concourse.bass_utils import run_bass_kernel_spmd

F32 = mybir.dt.float32
BF16 = mybir.dt.bfloat16
AF = mybir.ActivationFunctionType
ALU = mybir.AluOpType

D = 4096
T = 2048
NTB = 16
DFF = 11008
NF = 86
EPS = 1e-6
SCALE = 128.0 ** -0.5
NEG = -30000.0
import os as _os
NOPOOL = bool(_os.environ.get('NOPOOL'))


class Sem:
    def __init__(self, nc, name):
        self.h = nc.alloc_semaphore(name)
        self.v = 0


class Lazy:
    def __init__(self, f):
        self._f = f
        self._v = None

    def get(self):
        if self._v is None:
            self._v = self._f()
        return self._v

    def __getitem__(self, k):
        return self.get()[k]

    def __getattr__(self, n):
        return getattr(self.get(), n)


def _un(a):
    return a.get() if isinstance(a, Lazy) else a


class B:
    def __init__(self):
        self.nc = bass.Bass("TRN2", target_bir_lowering=False)
        self.nsem = 0
        self.dma_sems = {"sync": set(), "gpsimd": set()}
        self.bar = Sem(self.nc, "bar")
        self.bar_n = 0
        self.pool = []
        self.scopes = []

    def sem(self, name):
        self.nsem += 1
        sm = self.pool.pop() if (self.pool and not NOPOOL) else Sem(self.nc, f"{name}_{self.nsem}")
        if self.scopes:
            self.scopes[-1].append(sm)
        return sm

    def semg(self, name):
        self.nsem += 1
        return Sem(self.nc, f"{name}_{self.nsem}")

    @contextmanager
    def scope(self):
        self.scopes.append([])
        yield
        self.pool.extend(self.scopes.pop())

    @staticmethod
    def inc(instr, sem, n=1):
        instr.then_inc(sem.h, n)
        sem.v += n
        return (sem, sem.v)

    @staticmethod
    def wait(eng, tok):
        if tok is not None:
            eng.wait_ge(tok[0].h, tok[1])

    def dma(self, q, out, in_, sem):
        eng = self.nc.sync if q == "sync" else self.nc.gpsimd
        ins = eng.dma_start(out=_un(out), in_=_un(in_))
        self.dma_sems[q].add(sem)
        return self.inc(ins, sem, 16)

    def barrier(self, dummies):
        nc = self.nc
        self.bar_n += 1
        dps, idb, dv, da, dg = dummies
        self.inc(nc.vector.memset(dv[:], 0.0), self.bar)
        self.inc(nc.scalar.activation(out=da[:, 0:1], in_=da[:, 1:2], func=AF.Copy), self.bar)
        for s in self.dma_sems["gpsimd"]:
            nc.gpsimd.wait_ge(s.h, s.v)
        self.inc(nc.gpsimd.memset(dg[:], 0.0), self.bar)
        for s in self.dma_sems["sync"]:
            nc.sync.wait_ge(s.h, s.v)
        nc.sync.sem_inc(self.bar.h, 1)
        self.bar.v += 1
        nc.tensor.wait_ge(self.bar.h, self.bar.v)
        self.inc(nc.tensor.matmul(dps[:, 511:512], idb[:], idb[:, 0:1], start=True, stop=True), self.bar)
        for e in (nc.tensor, nc.vector, nc.scalar, nc.gpsimd, nc.sync):
            e.wait_ge(self.bar.h, self.bar.v)


def build(stop=None, skip0=False):
    b = B()
    nc = b.nc
    inc, wait, dma = b.inc, b.wait, b.dma
    PE, DVE, ACT, POOL, SP = nc.tensor, nc.vector, nc.scalar, nc.gpsimd, nc.sync

    def din(name, shape):
        return Lazy(lambda: nc.dram_tensor(name, shape, F32, kind="ExternalInput").ap())

    def dout(name, shape):
        return Lazy(lambda: nc.dram_tensor(name, shape, F32, kind="ExternalOutput").ap())

    x = din("x", [T, D])
    cvec = din("cvec", [32, 128])
    w_ada = din("w_ada", [D, 6 * D])
    b_ada = din("b_ada", [1, 6 * D])
    n1g = din("n1g", [32, 128])
    n2g = din("n2g", [32, 128])
    fg = din("fg", [1, D])
    w_in = din("w_in", [D, 2 * D])
    w_out = din("w_out", [D, D])
    w_gate = din("w_gate", [D, DFF])
    w_up = din("w_up", [D, DFF])
    w_down = din("w_down", [DFF, D])
    kc_in = din("kc", [256, 2048])
    vc_in = din("vc", [256, 2048])
    bias_in = din("bias", [16, 128, 6 * 5 * 128])
    ctxb_in = din("ctxb", [128, 1])
    ct_in = din("ct", [T, T])
    st_in = din("st", [T, T])
    cs_in = din("cs", [512, 1024])
    ident_in = din("ident", [128, 128])
    y = dout("y", [T, D])
    nk = dout("nk", [T, 2048])
    nv = dout("nv", [T, 2048])
    mod_row = nc.dram_tensor("mod_row", [1, 6 * D], F32).ap()
    DBG = bool(_os.environ.get("DBG"))
    skind = "ExternalOutput" if DBG else "Internal"
    mix_scr = nc.dram_tensor("mix_scr", [NTB, 128, 32, 128], BF16, kind=skind).ap()
    ab_scr = nc.dram_tensor("ab_scr", [4, NTB, 128, 1024], BF16, kind=skind).ap()
    x1_scr = nc.dram_tensor("x1_scr", [T, D], F32, kind=skind).ap()
    act_scr = nc.dram_tensor("act_scr", [NTB, 128, NF, 128], BF16, kind=skind).ap()
    x2_scr = nc.dram_tensor("x2_scr", [T, D], F32, kind=skind).ap()
    x3_scr = nc.dram_tensor("x3_scr", [T, D], F32, kind=skind).ap()

    idf = nc.alloc_sbuf_tensor("idf", [128, 128], F32)
    idb = nc.alloc_sbuf_tensor("idb", [128, 128], BF16)
    onesb = nc.alloc_sbuf_tensor("onesb", [128, 128], BF16)
    modT = nc.alloc_sbuf_tensor("modT", [128, 192], F32)
    a1 = nc.alloc_sbuf_tensor("a1", [128, 32], F32)
    a2 = nc.alloc_sbuf_tensor("a2", [128, 32], F32)
    ctxb = nc.alloc_sbuf_tensor("ctxb_sb", [128, 1], F32)
    epsc = nc.alloc_sbuf_tensor("epsc", [128, 1], F32)
    dv = nc.alloc_sbuf_tensor("dv", [128, 1], F32)
    da = nc.alloc_sbuf_tensor("da", [128, 2], F32)
    dg = nc.alloc_sbuf_tensor("dg", [128, 1], F32)
    BK = [nc.alloc_psum_tensor(f"bk{i}", [128, 512], F32) for i in range(6)]
    PB = [nc.alloc_psum_tensor(f"pb{i}", [128, 1024], BF16) for i in range(2)]
    dummies = (BK[5], idb, dv, da, dg)

    s0 = b.sem("init")
    dma("sync", idf[:], ident_in, s0)
    s0g = b.semg("initg")
    dma("gpsimd", idb[:], ident_in, s0g)
    dma("sync", ctxb[:], ctxb_in, s0)
    for e in (PE, DVE, ACT, POOL):
        e.wait_ge(s0.h, s0.v)
        e.wait_ge(s0g.h, s0g.v)
    DVE.memset(onesb[:], 1.0)
    DVE.memset(epsc[:], EPS)
    s0c = b.sem("initc")
    t = inc(DVE.memset(da[:], 0.0), s0c)
    for e in (PE, ACT, POOL, SP):
        wait(e, t)
    b.barrier(dummies)

    if skip0:
        DVE.memset(modT[:], 0.5)
        DVE.memset(a1[:], 1.5)
        t = inc(DVE.memset(a2[:], 1.5), s0c)
        wait(SP, t)
        dma("sync", mod_row.rearrange("o (p j) -> (o p) j", p=128), modT[:], s0)
        for e in (PE, ACT, POOL, SP):
            wait(e, t)
        b.barrier(dummies)
    if not skip0:
        with ExitStack() as es:
            es.enter_context(b.scope())
            def sb(name, shape, dt):
                return es.enter_context(nc.sbuf_tensor(name, shape, dt))
            cv = sb("cv", [32, 128], F32)
            cvT = sb("cvT", [128, 32], BF16)
            ring = [sb(f"adar{i}", [128, 32, 512], BF16) for i in range(3)]
            brow = [sb(f"brow{i}", [1, 512], F32) for i in range(2)]
            rowsb = [sb(f"rowsb{i}", [1, 512], F32) for i in range(2)]
            mr = [sb(f"mr{i}", [96, 128], F32) for i in range(2)]
            gt = [sb(f"gt{i}", [32, 128], F32) for i in range(2)]
            s_ld = b.sem("ld"); s_w = [b.semg("w") for _ in range(3)]; s_b = [b.sem("b") for _ in range(2)]; s_mm = b.sem("mm"); s_row = b.sem("row")
            s_st = [b.sem("st") for _ in range(2)]; s_a = b.sem("a")
            t = dma("sync", cv[:], cvec, s_ld)
            wait(PE, t)
            t = inc(PE.transpose(BK[0][:, 0:32], cv[:], idf[0:32, 0:32]), s_a)
            wait(ACT, t)
            t_cvT = inc(ACT.activation(out=cvT[:], in_=BK[0][:, 0:32], func=AF.Silu), s_a)
            wait(PE, t_cvT)
            NU = 48
            mm_tok = [None] * NU
            row_tok = [None] * NU
            st_tok = [None] * NU
            for u in range(NU):
                if u >= 3:
                    wait(POOL, mm_tok[u - 3])
                tw = dma("gpsimd", ring[u % 3][:], w_ada[:, u * 512:(u + 1) * 512].rearrange("(kc p) n -> p kc n", p=128), s_w[u % 3])
                if u >= 2:
                    wait(SP, row_tok[u - 2])
                tb_ = dma("sync", brow[u % 2][:], b_ada[:, u * 512:(u + 1) * 512], s_b[u % 2])
                wait(PE, tw)
                if u >= 2:
                    wait(PE, row_tok[u - 2])
                for k in range(32):
                    mm = PE.matmul(BK[u % 2][0:1, :], cvT[:, k:k + 1], ring[u % 3][:, k, :], start=(k == 0), stop=(k == 31))
                mm_tok[u] = inc(mm, s_mm)
                wait(DVE, mm_tok[u]); wait(DVE, tb_)
                if u >= 2:
                    wait(DVE, st_tok[u - 2])
                row_tok[u] = inc(DVE.tensor_tensor(out=rowsb[u % 2][:], in0=BK[u % 2][0:1, :], in1=brow[u % 2][:], op=ALU.add), s_row)
                wait(SP, row_tok[u])
                st_tok[u] = dma("sync", mod_row[:, u * 512:(u + 1) * 512], rowsb[u % 2][:], s_st[u % 2])
            wait(SP, st_tok[NU - 1]); wait(SP, st_tok[NU - 2])
            mrv = mod_row.rearrange("o (j p) -> (o j) p", p=128)
            t0_ = dma("sync", mr[0][:], mrv[0:96, :], s_ld)
            t1_ = dma("sync", mr[1][:], mrv[96:192, :], s_ld)
            t2_ = dma("sync", gt[0][:], n1g, s_ld)
            t3_ = dma("sync", gt[1][:], n2g, s_ld)
            wait(PE, t3_)
            PE.transpose(BK[2][:, 0:96], mr[0][:], idf[0:96, 0:96])
            PE.transpose(BK[2][:, 96:192], mr[1][:], idf[0:96, 0:96])
            PE.transpose(BK[2][:, 192:224], gt[0][:], idf[0:32, 0:32])
            t = inc(PE.transpose(BK[2][:, 224:256], gt[1][:], idf[0:32, 0:32]), s_a)
            wait(DVE, t)
            t = inc(DVE.tensor_copy(out=modT[:], in_=BK[2][:, 0:192]), s_a)
            wait(DVE, t)
            DVE.scalar_tensor_tensor(out=a1[:], in0=modT[:, 32:64], scalar=1.0, in1=BK[2][:, 192:224], op0=ALU.add, op1=ALU.mult)
            DVE.scalar_tensor_tensor(out=a2[:], in0=modT[:, 128:160], scalar=1.0, in1=BK[2][:, 224:256], op0=ALU.add, op1=ALU.mult)
            b.barrier(dummies)
    b1 = modT[:, 0:32]
    b2 = modT[:, 96:128]

    def dbg_out(name, src_ap, shape):
        o = nc.dram_tensor(name, shape, F32, kind="ExternalOutput").ap()
        sd = b.sem("dbg")
        tk = dma("sync", o, src_ap, sd)
        wait(SP, tk)

    if stop == 0:
        dbg_out("d_modT", modT[:], [128, 192])
        dbg_out("d_a1", a1[:], [128, 32])
        return nc

    def phase_norm(src, av, bv, hT, tag):
        with ExitStack() as es:
            es.enter_context(b.scope())
            def sb(name, shape, dt):
                return es.enter_context(nc.sbuf_tensor(name + tag, shape, dt))
            xt = [sb(f"xt{i}", [128, D], F32) for i in range(2)]
            xh = [sb(f"xh{i}", [128, D], BF16) for i in range(2)]
            junk = sb("junk", [128, D], BF16)
            ss = sb("ss", [128, NTB], F32)
            sq = sb("sq", [128, NTB], F32)
            rs = sb("rs", [128, NTB], F32)
            s_ld = [b.sem("ld"), b.sem("ld")]; s_sq = b.sem("sq"); s_rt = b.sem("rt"); s_r = b.sem("r"); s_xh = b.sem("xh")
            s_tr = b.sem("tr"); s_ea = b.sem("ea"); s_ed = b.sem("ed")
            xh_tok = [None] * NTB
            tr_last = [None] * NTB
            ev_tok = {}
            t = inc(DVE.memset(ss[:], 0.0), s_r)
            wait(ACT, t)
            gi = 0
            LVL = int(_os.environ.get("LVL", "9"))
            sq_tok = [None] * NTB
            for tb in range(NTB):
                if tb >= 2:
                    wait(SP, xh_tok[tb - 2] if LVL >= 2 else sq_tok[tb - 2])
                tl = dma("sync", xt[tb % 2][:], src[tb * 128:(tb + 1) * 128, :], s_ld[tb % 2])
                wait(ACT, tl)
                t = inc(ACT.activation(out=junk[:], in_=xt[tb % 2][:], func=AF.Square, accum_out=ss[:, tb:tb + 1]), s_sq)
                sq_tok[tb] = t
                if LVL < 2:
                    continue
                wait(ACT, t)
                t = inc(ACT.activation(out=sq[:, tb:tb + 1], in_=ss[:, tb:tb + 1], func=AF.Sqrt, bias=epsc[:], scale=1.0 / D), s_rt)
                wait(DVE, t)
                t = inc(DVE.reciprocal(out=rs[:, tb:tb + 1], in_=sq[:, tb:tb + 1]), s_r)
                wait(DVE, t)
                if tb >= 2 and LVL >= 3:
                    wait(DVE, tr_last[tb - 2])
                xh_tok[tb] = inc(DVE.tensor_scalar(out=xh[tb % 2][:], in0=xt[tb % 2][:], scalar1=rs[:, tb:tb + 1], scalar2=None, op0=ALU.mult), s_xh)
                if LVL < 3:
                    continue
                wait(PE, xh_tok[tb])
                for g in range(4):
                    pb = PB[gi % 2]
                    if gi >= 2 and LVL >= 4:
                        wait(PE, ev_tok[gi - 2])
                    for j in range(8):
                        kc = g * 8 + j
                        tr = PE.transpose(pb[:, j * 128:(j + 1) * 128], xh[tb % 2][:, kc * 128:(kc + 1) * 128], idb[:])
                    ttr = inc(tr, s_tr)
                    if g == 3:
                        tr_last[tb] = ttr
                    if LVL < 4:
                        gi += 1
                        continue
                    use_act = (gi % 2 == 0)
                    wait(ACT if use_act else DVE, ttr)
                    for j in range(8):
                        kc = g * 8 + j
                        o = hT[:, kc, tb * 128:(tb + 1) * 128]
                        i_ = pb[:, j * 128:(j + 1) * 128]
                        if use_act:
                            iv = ACT.activation(out=o, in_=i_, func=AF.Identity, bias=bv[:, kc:kc + 1], scale=av[:, kc:kc + 1])
                        else:
                            iv = DVE.tensor_scalar(out=o, in0=i_, scalar1=av[:, kc:kc + 1], scalar2=bv[:, kc:kc + 1], op0=ALU.mult, op1=ALU.add)
                    ev_tok[gi] = inc(iv, s_ea if use_act else s_ed)
                    gi += 1
            b.barrier(dummies)

    def wload(dst, W, c0, n, sem):
        return dma("gpsimd", dst, W[:, c0:c0 + n].rearrange("(kc p) n -> p kc n", p=128), sem)

    with ExitStack() as es_h:
        hT = es_h.enter_context(nc.sbuf_tensor("hT", [128, 32, T], BF16))
        phase_norm(x, a1, b1, hT, "n1")
        if stop == 1:
            with nc.sbuf_tensor("dbgh", [128, 32, 128], F32) as dbgh:
                if int(_os.environ.get('LVL', '9')) < 4:
                    DVE.memset(hT[:, :, 0:128], 1.0)
                DVE.tensor_copy(out=dbgh[:], in_=hT[:, :, 0:128])
                b.barrier(dummies)
                dbg_out("d_hT", dbgh[:], [128, 32, 128])
            return nc

        NSLOT = 3
        wr = [es_h.enter_context(nc.sbuf_tensor(f"wr{i}", [128, 32, 128], BF16)) for i in range(NSLOT)]
        s_wr = [b.semg(f"wr{i}") for i in range(NSLOT)]
        s_gm = b.sem("gm")
        state = {"u": 0, "g": 0}
        unit_last_grp = {}
        grp_rel = {}

        deferred = []

        def ws_unit_gen(W, c0, epilogue):
            u = state["u"]; state["u"] += 1
            slot = u % NSLOT
            if u >= NSLOT:
                wait(POOL, unit_last_grp[u - NSLOT])
            tw = wload(wr[slot][:], W, c0, 128, s_wr[slot])
            for tg in range(4):
                g = state["g"]; state["g"] += 1
                bank = BK[g % 2]
                for k in range(32):
                    if k == 0:
                        if tg == 0:
                            wait(PE, tw)
                        if g >= 2:
                            for tk in grp_rel[g - 2]:
                                wait(PE, tk)
                    mm = PE.matmul(bank[:], wr[slot][:, k, :], hT[:, k, tg * 512:(tg + 1) * 512], start=(k == 0), stop=(k == 31))
                    if k % 8 == 7 and k != 31:
                        yield
                tk = inc(mm, s_gm)
                if tg == 3:
                    unit_last_grp[u] = tk
                while deferred:
                    deferred.pop(0)()
                grp_rel[g] = epilogue(tg, bank, tk)
                yield

        def ws_unit(W, c0, epilogue):
            for _ in ws_unit_gen(W, c0, epilogue):
                pass

        with ExitStack() as es:
            es.enter_context(b.scope())
            def sb(name, shape, dt):
                return es.enter_context(nc.sbuf_tensor(name, shape, dt))
            qTs = [sb(f"qT{i}", [128, T], BF16) for i in range(2)]
            kTs = [sb(f"kT{i}", [128, T], BF16) for i in range(2)]
            vbfs = [sb(f"vbf{i}", [128, NTB, 128], BF16) for i in range(2)]
            fst = sb("fst", [128, 512], F32)
            kvo = sb("kvo", [128, 4, 128], F32)
            biast = sb("biast", [128, 6 * 5 * 128], F32)
            pt = sb("PT", [128, 7 * 128], BF16)
            ri = sb("rinv", [128, 128], F32)
            mixst = sb("mixst", [128, NTB, 128], BF16)
            kcs2 = [sb(f"kcs{i}", [128, 2, 128], BF16) for i in range(2)]
            kcT = sb("kcT", [128, 256], BF16)
            vcs2 = [sb(f"vcs{i}", [128, 2, 128], BF16) for i in range(2)]
            att_hist = {}
            s_q = b.sem("q"); s_kA = b.sem("kA"); s_kD = b.sem("kD"); s_tr = b.sem("tr"); s_ko = b.sem("ko")
            s_va = b.sem("va"); s_st = b.sem("st"); s_bias = b.sem("bias"); s_ctx = [b.semg("ctx"), b.semg("ctx")]; s_kct = b.sem("kct")
            s_qk = b.sem("qk"); s_e1 = b.sem("e1"); s_ex = b.sem("ex"); s_pv = b.sem("pv")
            s_ri = b.sem("ri"); s_no = b.sem("no"); s_mx = b.sem("mx")
            fs_n = {"n": 0}
            tr_rel = {}
            fst_rel = {}
            kvo_rel = {}
            head_pend = {}
            head_ctx = {}
            att = {"n": 0, "pv": {}, "ex": {}, "no": {}, "mx": None}

            def make_ep_q(j):
                def ep_q(tg, bank, tk):
                    wait(ACT, tk)
                    if tg == 0:
                        wait(ACT, att_hist.get(j - 2))
                    t_ = inc(ACT.activation(out=qTs[j % 2][:, tg * 512:(tg + 1) * 512], in_=bank[:], func=AF.Copy), s_q)
                    head_pend[j].append(t_)
                    return [t_]
                return ep_q

            def make_ep_kv(is_k, j):
                def ep(tg, bank, tk):
                    n = fs_n["n"]; fs_n["n"] += 1
                    f = fst
                    wait(DVE, tk)
                    if n >= 1:
                        for tk2 in fst_rel[n - 1]:
                            wait(DVE, tk2)
                    tf = inc(DVE.tensor_copy(out=f[:], in_=bank[:]), s_kD)
                    rel = [tf]
                    fst_rel[n] = []
                    if is_k:
                        wait(ACT, tf)
                        if tg == 0:
                            wait(ACT, att_hist.get(j - 2))
                        t_ = inc(ACT.activation(out=kTs[j % 2][:, tg * 512:(tg + 1) * 512], in_=f[:], func=AF.Copy), s_kA)
                        fst_rel[n].append(t_)
                        head_pend[j].append(t_)
                    def part_b():
                        wait(PE, tf)
                        if n >= 1:
                            for tk2 in tr_rel[n - 1]:
                                wait(PE, tk2)
                        for bb in range(4):
                            tr = PE.transpose(BK[2][:, bb * 128:(bb + 1) * 128], f[:, bb * 128:(bb + 1) * 128], idf[:])
                        ttr = inc(tr, s_tr)
                        fst_rel[n].append(ttr)
                        ko = kvo
                        wait(DVE, ttr)
                        if n >= 1:
                            for tk2 in kvo_rel[n - 1]:
                                wait(DVE, tk2)
                        tko = inc(DVE.tensor_copy(out=ko[:].rearrange("p b d -> p (b d)"), in_=BK[2][:]), s_ko)
                        tr_rel[n] = [tko]
                        kvo_rel[n] = []
                        if not is_k:
                            wait(ACT, tko)
                            if tg == 0:
                                wait(ACT, att_hist.get(j - 2))
                            t_ = inc(ACT.activation(out=vbfs[j % 2][:, tg * 4:(tg + 1) * 4, :].rearrange("p b d -> p (b d)"), in_=ko[:].rearrange("p b d -> p (b d)"), func=AF.Copy), s_va)
                            kvo_rel[n].append(t_)
                            head_pend[j].append(t_)
                        wait(SP, tko)
                        dst = (nk if is_k else nv)[tg * 512:(tg + 1) * 512, j * 128:(j + 1) * 128].rearrange("(b p) d -> p b d", p=128)
                        kvo_rel[n].append(dma("sync", dst, ko[:], s_st))
                    deferred.append(part_b)
                    return rel
                return ep

            def head_gen(j):
                head_pend[j] = []
                kcs = kcs2[j % 2]; vcs = vcs2[j % 2]
                wait(POOL, att_hist.get(j - 2))
                dma("gpsimd", kcs[:], kc_in[:, j * 128:(j + 1) * 128].rearrange("(b p) d -> p b d", p=128), s_ctx[j % 2])
                t_vc = dma("gpsimd", vcs[:], vc_in[:, j * 128:(j + 1) * 128].rearrange("(b p) d -> p b d", p=128), s_ctx[j % 2])
                head_ctx[j] = (kcs, vcs, t_vc)
                yield from ws_unit_gen(w_in, j * 128, make_ep_q(j))
                yield from ws_unit_gen(w_in, 2048 + j * 128, make_ep_kv(True, j))
                yield from ws_unit_gen(w_in, 4096 + j * 128, make_ep_kv(False, j))

            def tmap(i):
                return {0: 0, 1: 1, 14: 4, 15: 5}.get(i, 2 + (i % 2))

            for _ in head_gen(0):
                pass
            while deferred:
                deferred.pop(0)()
            for j in range(16):
                gnext = [head_gen(j + 1)] if j + 1 < 16 else [None]

                def pull(cnt):
                    for _ in range(cnt):
                        if gnext[0] is not None:
                            try:
                                next(gnext[0])
                            except StopIteration:
                                gnext[0] = None

                qT = qTs[j % 2]; kT = kTs[j % 2]; vbf = vbfs[j % 2]
                kcs, vcs, t_vc = head_ctx[j]
                wait(SP, att_hist.get(j - 1))
                t_bias = dma("sync", biast[:], bias_in[j], s_bias)
                wait(PE, t_vc)
                for bb in range(2):
                    tr = PE.transpose(PB[0][:, bb * 128:(bb + 1) * 128], kcs[:, bb, :], idb[:])
                ttr = inc(tr, s_kct)
                wait(ACT, ttr)
                wait(ACT, att_hist.get(j - 1))
                t_kcT = inc(ACT.activation(out=kcT[:], in_=PB[0][:, 0:256], func=AF.Copy), s_kct)
                pull(4)
                for tk in head_pend[j] + [t_kcT]:
                    wait(PE, tk)
                wait(DVE, t_bias)
                wait(DVE, att["mx"])
                n0 = att["n"]; att["n"] += 16
                X, Y, Z = BK[3], BK[4], BK[5]

                def qk_stage(i):
                    n = n0 + i
                    base = min(max(i - 2, 0), 11)
                    ty = tmap(i)
                    if n >= 1:
                        wait(PE, att["ex"][n - 1])
                    for s_ in range(5):
                        kb = base + s_
                        dst = X[:, s_ * 128:(s_ + 1) * 128] if s_ < 4 else Y[:, 0:128]
                        PE.matmul(dst, kT[:, kb * 128:(kb + 1) * 128], qT[:, i * 128:(i + 1) * 128], start=True, stop=True)
                    for s_ in range(2):
                        mm = PE.matmul(Y[:, (1 + s_) * 128:(2 + s_) * 128], kcT[:, s_ * 128:(s_ + 1) * 128], qT[:, i * 128:(i + 1) * 128], start=True, stop=True)
                    tqk = inc(mm, s_qk)
                    wait(DVE, tqk)
                    bo = ty * 640
                    DVE.scalar_tensor_tensor(out=X[:], in0=X[:], scalar=SCALE, in1=biast[:, bo:bo + 512], op0=ALU.mult, op1=ALU.add)
                    te = inc(DVE.scalar_tensor_tensor(out=Y[:, 0:128], in0=Y[:, 0:128], scalar=SCALE, in1=biast[:, bo + 512:bo + 640], op0=ALU.mult, op1=ALU.add), s_e1)
                    wait(ACT, te)
                    if n >= 1:
                        wait(ACT, att["pv"][n - 1])
                    ACT.activation(out=pt[:, 0:512], in_=X[:], func=AF.Exp)
                    ACT.activation(out=pt[:, 512:640], in_=Y[:, 0:128], func=AF.Exp)
                    att["ex"][n] = inc(ACT.activation(out=pt[:, 640:896], in_=Y[:, 128:384], func=AF.Exp, bias=ctxb[:], scale=SCALE), s_ex)

                def pv_stage(i):
                    n = n0 + i
                    base = min(max(i - 2, 0), 11)
                    wait(PE, att["ex"][n])
                    if n >= 1:
                        wait(PE, att["no"][n - 1])
                    for s_ in range(7):
                        lhs = vbf[:, base + s_, :] if s_ < 5 else vcs[:, s_ - 5, :]
                        PE.matmul(Z[:, 0:128], lhs, pt[:, s_ * 128:(s_ + 1) * 128], start=(s_ == 0), stop=(s_ == 6))
                    for s_ in range(7):
                        mm = PE.matmul(Z[:, 128:256], onesb[:], pt[:, s_ * 128:(s_ + 1) * 128], start=(s_ == 0), stop=(s_ == 6))
                    tpv_ = inc(mm, s_pv)
                    att["pv"][n] = tpv_
                    wait(DVE, tpv_)
                    tri = inc(DVE.reciprocal(out=ri[:], in_=Z[:, 128:256]), s_ri)
                    wait(DVE, tri)
                    tno_ = inc(DVE.tensor_tensor(out=mixst[:, i, :], in0=Z[:, 0:128], in1=ri[:], op=ALU.mult), s_no)
                    att["no"][n] = tno_
                    return tpv_, tno_

                for i in range(16):
                    qk_stage(i)
                    pull(3)
                    tpv, tno = pv_stage(i)
                pull(1000)
                while deferred:
                    deferred.pop(0)()
                wait(SP, tno)
                att["mx"] = dma("sync", mix_scr.rearrange("tb p kc t -> p tb kc t")[:, :, j, :], mixst[:], s_mx)
                att_hist[j] = tpv
            b.barrier(dummies)

        with ExitStack() as es:
            es.enter_context(b.scope())
            def sb(name, shape, dt):
                return es.enter_context(nc.sbuf_tensor(name, shape, dt))
            uT = [sb(f"uT{i}", [128, T], BF16) for i in range(4)]
            cst = sb("cst", [128, 4, 1024], BF16)
            abst = [sb(f"abst{i}", [128, 1024], BF16) for i in range(2)]
            s_u = b.sem("u"); s_cs = b.semg("cs"); s_cd = b.sem("cd"); s_ab = b.sem("ab"); s_ab2 = b.sem("ab2"); s_abo = [b.sem("abo"), b.sem("abo")]
            t_cs = dma("gpsimd", cst[:], cs_in.rearrange("(c p) n -> p c n", p=128), s_cs)
            cd = {"n": 0, "ev": {}, "out": {}, "last_mm": None}

            def make_ep_u(c):
                def ep(tg, bank, tk):
                    wait(ACT, tk)
                    if tg == 0:
                        wait(ACT, cd["last_mm"])
                    return [inc(ACT.activation(out=uT[c][:, tg * 512:(tg + 1) * 512], in_=bank[:], func=AF.Copy), s_u)]
                return ep

            for g4 in range(4):
                for c in range(4):
                    ws_unit(w_in, 6144 + g4 * 512 + c * 128, make_ep_u(c))
                last_g = state["g"] - 1
                for gg in range(last_g - 15, last_g + 1):
                    for tk in grp_rel[gg]:
                        wait(PE, tk)
                wait(PE, t_cs)
                for tb in range(NTB):
                    n = cd["n"]; cd["n"] += 1
                    A_, B_ = (BK[3], BK[4]) if n % 2 == 0 else (BK[5], BK[2])
                    if n >= 2:
                        wait(PE, cd["ev"][n - 2][0]); wait(PE, cd["ev"][n - 2][1])
                    for c in range(4):
                        PE.matmul(A_[:], uT[c][:, tb * 128:(tb + 1) * 128], cst[:, c, 0:512], start=(c == 0), stop=(c == 3))
                    for c in range(4):
                        mm = PE.matmul(B_[:], uT[c][:, tb * 128:(tb + 1) * 128], cst[:, c, 512:1024], start=(c == 0), stop=(c == 3))
                    tmm = inc(mm, s_cd)
                    cd["last_mm"] = tmm
                    ab = abst[n % 2]
                    wait(ACT, tmm); wait(DVE, tmm)
                    if n >= 2:
                        wait(ACT, cd["out"][n - 2]); wait(DVE, cd["out"][n - 2])
                    ta = inc(ACT.activation(out=ab[:, 0:512], in_=A_[:], func=AF.Copy), s_ab)
                    td = inc(DVE.tensor_copy(out=ab[:, 512:1024], in_=B_[:]), s_ab2)
                    cd["ev"][n] = (ta, td)
                    wait(SP, ta); wait(SP, td)
                    cd["out"][n] = dma("sync", ab_scr[g4, tb], ab[:], s_abo[n % 2])
            b.barrier(dummies)

    if stop == 2:
        return nc
    with ExitStack() as es:
        es.enter_context(b.scope())
        def sb(name, shape, dt):
            return es.enter_context(nc.sbuf_tensor(name, shape, dt))
        ctt = sb("ctt", [128, NTB, T], BF16)
        stt = sb("stt", [128, NTB, T], BF16)
        abt = [sb(f"abt{i}", [128, NTB, 1024], BF16) for i in range(2)]
        mxf = [sb(f"mxf{i}", [128, T], BF16) for i in range(2)]
        s_c = b.semg("c"); s_ab = b.sem("ab"); s_mm = b.sem("mm"); s_ev = b.sem("ev"); s_o = [b.sem("o"), b.sem("o")]
        s_c2 = b.semg("c2")
        tc1 = dma("gpsimd", ctt[:], ct_in.rearrange("(tb p) n -> p tb n", p=128), s_c)
        tc2 = dma("gpsimd", stt[:], st_in.rearrange("(tb p) n -> p tb n", p=128), s_c2)
        wait(PE, tc1)
        st_waited = [False]
        gcount = 0
        mm_tok = {}
        ev_tok = {}
        o_tok = {}
        ab_last = {}
        ci = 0
        for g4 in range(4):
            if g4 >= 2:
                wait(SP, ab_last[g4 - 2])
            tab = dma("sync", abt[g4 % 2][:], ab_scr[g4].rearrange("tb p n -> p tb n"), s_ab)
            wait(PE, tab)
            for c in range(4):
                mx = mxf[ci % 2]
                for tg in range(4):
                    bank = BK[gcount % 2]
                    if gcount >= 2:
                        wait(PE, ev_tok[gcount - 2])
                    for tb in range(NTB):
                        PE.matmul(bank[:], abt[g4 % 2][:, tb, c * 128:(c + 1) * 128], ctt[:, tb, tg * 512:(tg + 1) * 512], start=(tb == 0), stop=False)
                    if not st_waited[0]:
                        wait(PE, tc2)
                        st_waited[0] = True
                    for tb in range(NTB):
                        mm = PE.matmul(bank[:], abt[g4 % 2][:, tb, 512 + c * 128:512 + (c + 1) * 128], stt[:, tb, tg * 512:(tg + 1) * 512], start=False, stop=(tb == NTB - 1))
                    tmm = inc(mm, s_mm)
                    ab_last[g4] = tmm
                    wait(ACT, tmm)
                    if tg == 0 and ci >= 2:
                        wait(ACT, o_tok[ci - 2])
                    ev_tok[gcount] = inc(ACT.activation(out=mx[:, tg * 512:(tg + 1) * 512], in_=bank[:], func=AF.Copy), s_ev)
                    gcount += 1
                wait(SP, ev_tok[gcount - 1])
                o_tok[ci] = dma("sync", mix_scr.rearrange("tb p kc t -> p tb kc t")[:, :, 16 + g4 * 4 + c, :], mx[:].rearrange("p (tb t) -> p tb t", t=128), s_o[ci % 2])
                ci += 1
        b.barrier(dummies)

    def as_gemm(tag, KC, NB, a_scr, W, gate_off, res_src, res_dst, ssq=None, NAT=2, kc0=0):
        ncb = D // NB
        with ExitStack() as es:
            es.enter_context(b.scope())
            def sb(name, shape, dt):
                return es.enter_context(nc.sbuf_tensor(name + tag, shape, dt))
            gb = sb("gb", [128, D], F32)
            wb = [sb(f"wb{i}", [128, KC, NB], BF16) for i in range(2)]
            at = [sb(f"at{i}", [128, KC, 128], BF16) for i in range(NAT)]
            xi = [sb(f"xi{i}", [128, NB], F32) for i in range(NAT)]
            xo = [sb(f"xo{i}", [128, NB], F32) for i in range(2)]
            junk = sb("junk", [128, NB], BF16)
            s_g = b.sem("g"); s_w = [b.semg("w0"), b.semg("w1")]; s_a = [b.sem("a") for _ in range(NAT)]
            s_x = [b.sem("x") for _ in range(NAT)]; s_mm = b.sem("mm")
            s_m = b.sem("m"); s_ad = b.sem("ad"); s_o = [b.sem("o"), b.sem("o")]; s_sq = b.sem("sq")
            tg_ = dma("sync", gb[:], mod_row[:, gate_off:gate_off + D].partition_broadcast(128), s_g)
            wait(DVE, tg_)
            mm_tok = {}; m_tok = {}; ad_tok = {}; o_tok = {}; sq_tok = {}; cb_last = {}
            ta_tok = {}; tx_tok = {}; tw_tok = {}
            NI = ncb * NTB

            def loads(idx):
                cb, tb = divmod(idx, NTB)
                if idx >= NAT:
                    wait(SP, mm_tok[idx - NAT])
                ta_tok[idx] = dma("sync", at[idx % NAT][:], a_scr[tb][:, kc0:kc0 + KC, :], s_a[idx % NAT])
                if idx >= NAT:
                    wait(SP, ad_tok[idx - NAT])
                tx_tok[idx] = dma("sync", xi[idx % NAT][:], res_src[tb * 128:(tb + 1) * 128, cb * NB:(cb + 1) * NB], s_x[idx % NAT])

            def wloads(cb):
                if cb >= 2:
                    wait(POOL, cb_last[cb - 2])
                tw_tok[cb] = wload(wb[cb % 2][:], W[kc0 * 128:(kc0 + KC) * 128, :], cb * NB, NB, s_w[cb % 2])

            wloads(0)
            PD = NAT - 1
            for i_ in range(PD):
                loads(i_)
            for idx in range(NI):
                cb, tb = divmod(idx, NTB)
                if tb == 0 and cb + 1 < ncb:
                    wloads(cb + 1)
                if idx + PD < NI:
                    loads(idx + PD)
                bank = BK[idx % 2]
                wait(PE, ta_tok[idx])
                if tb == 0:
                    wait(PE, tw_tok[cb])
                if idx >= 2:
                    wait(PE, m_tok[idx - 2])
                for k in range(KC):
                    mm = PE.matmul(bank[:, 0:NB], at[idx % NAT][:, k, :], wb[cb % 2][:, k, :], start=(k == 0), stop=(k == KC - 1))
                mm_tok[idx] = inc(mm, s_mm)
                cb_last[cb] = mm_tok[idx]
                wait(DVE, mm_tok[idx]); wait(DVE, tx_tok[idx])
                if idx >= 2:
                    wait(DVE, o_tok[idx - 2])
                    if ssq is not None:
                        wait(DVE, sq_tok[idx - 2])
                m_tok[idx] = inc(DVE.tensor_tensor(out=xo[idx % 2][:], in0=bank[:, 0:NB], in1=gb[:, cb * NB:(cb + 1) * NB], op=ALU.mult), s_m)
                wait(DVE, m_tok[idx])
                ad_tok[idx] = inc(DVE.tensor_tensor(out=xo[idx % 2][:], in0=xo[idx % 2][:], in1=xi[idx % NAT][:], op=ALU.add), s_ad)
                if ssq is not None:
                    wait(ACT, ad_tok[idx])
                    sq_tok[idx] = inc(ACT.activation(out=junk[:], in_=xo[idx % 2][:], func=AF.Square, accum_out=ssq[:, cb * NTB + tb:cb * NTB + tb + 1]), s_sq)
                wait(SP, ad_tok[idx])
                o_tok[idx] = dma("sync", res_dst[tb * 128:(tb + 1) * 128, cb * NB:(cb + 1) * NB], xo[idx % 2][:], s_o[idx % 2])
            b.barrier(dummies)

    if stop == 25:
        return nc
    as_gemm("op", 32, 512, mix_scr, w_out, 2 * D, x, x1_scr, NAT=4)
    if stop == 3:
        return nc

    with ExitStack() as es_h:
        h2T = es_h.enter_context(nc.sbuf_tensor("h2T", [128, 32, T], BF16))
        phase_norm(x1_scr, a2, b2, h2T, "n2")
        with ExitStack() as es:
            es.enter_context(b.scope())
            def sb(name, shape, dt):
                return es.enter_context(nc.sbuf_tensor(name, shape, dt))
            NS = 4
            wr = [sb(f"fwr{i}", [128, 32, 128], BF16) for i in range(NS)]
            sgt = [sb(f"sgt{i}", [128, 512], F32) for i in range(2)]
            actst = [sb(f"actst{i}", [128, T], BF16) for i in range(2)]
            s_wr = [b.semg(f"fwr{i}") for i in range(NS)]
            s_mm = b.sem("mm"); s_sg = b.sem("sg"); s_ac = b.sem("ac"); s_o = [b.sem("o"), b.sem("o")]
            f_last = {}
            mm_tok = {}
            sg_tok = {}
            ac_tok = {}
            o_tok = {}
            gidx = 0
            for f in range(NF):
                sl_g = (2 * f) % NS
                sl_u = (2 * f + 1) % NS
                if f >= 2:
                    wait(POOL, f_last[f - 2])
                twg = wload(wr[sl_g][:], w_gate, f * 128, 128, s_wr[sl_g])
                twu = wload(wr[sl_u][:], w_up, f * 128, 128, s_wr[sl_u])
                ast = actst[f % 2]
                for tg in range(4):
                    Bg = BK[(gidx % 2) * 2]
                    Bu = BK[(gidx % 2) * 2 + 1]
                    if tg == 0:
                        wait(PE, twg); wait(PE, twu)
                    if gidx >= 2:
                        wait(PE, ac_tok[gidx - 2])
                    for k in range(32):
                        PE.matmul(Bg[:], wr[sl_g][:, k, :], h2T[:, k, tg * 512:(tg + 1) * 512], start=(k == 0), stop=(k == 31))
                    for k in range(32):
                        mm = PE.matmul(Bu[:], wr[sl_u][:, k, :], h2T[:, k, tg * 512:(tg + 1) * 512], start=(k == 0), stop=(k == 31))
                    mm_tok[gidx] = inc(mm, s_mm)
                    f_last[f] = mm_tok[gidx]
                    sg = sgt[gidx % 2]
                    wait(ACT, mm_tok[gidx])
                    if gidx >= 2:
                        wait(ACT, ac_tok[gidx - 2])
                    sg_tok[gidx] = inc(ACT.activation(out=sg[:], in_=Bg[:], func=AF.Silu), s_sg)
                    wait(DVE, sg_tok[gidx])
                    if tg == 0 and f >= 2:
                        wait(DVE, o_tok[f - 2])
                    ac_tok[gidx] = inc(DVE.tensor_tensor(out=ast[:, tg * 512:(tg + 1) * 512], in0=Bu[:], in1=sg[:], op=ALU.mult), s_ac)
                    gidx += 1
                wait(SP, ac_tok[gidx - 1])
                o_tok[f] = dma("sync", act_scr.rearrange("tb p f t -> p tb f t")[:, :, f, :], ast[:].rearrange("p (tb t) -> p tb t", t=128), s_o[f % 2])
            b.barrier(dummies)

    if stop == 5:
        return nc
    NB6 = 512
    ssq = nc.alloc_sbuf_tensor("ssq", [128, (D // NB6) * NTB], F32)
    DVE.memset(ssq[:], 0.0)
    b.barrier(dummies)
    as_gemm("dn1", 43, NB6, act_scr, w_down, 5 * D, x1_scr, x2_scr, NAT=3, kc0=0)
    as_gemm("dn2", 43, NB6, act_scr, w_down, 5 * D, x2_scr, x3_scr, ssq=ssq, NAT=3, kc0=43)

    with ExitStack() as es:
        es.enter_context(b.scope())
        def sb(name, shape, dt):
            return es.enter_context(nc.sbuf_tensor(name, shape, dt))
        fgb = sb("fgb", [128, D], F32)
        xt = [sb(f"fx{i}", [128, D], F32) for i in range(2)]
        yo = [sb(f"fy{i}", [128, D], F32) for i in range(2)]
        tot = sb("tot", [128, NTB], F32)
        sq = sb("fsq", [128, NTB], F32)
        rs = sb("frs", [128, NTB], F32)
        s_ld = [b.sem("ld"), b.sem("ld")]; s_t = b.sem("t"); s_y = b.sem("y"); s_o = [b.sem("o"), b.sem("o")]; s_fg = b.sem("fg")
        tf = dma("sync", fgb[:], fg.partition_broadcast(128), s_fg)
        ncb = D // NB6
        t = inc(DVE.tensor_reduce(out=tot[:], in_=ssq[:].rearrange("p (cb tb) -> p tb cb", tb=NTB), axis=mybir.AxisListType.X, op=ALU.add), s_t)
        wait(ACT, t)
        t = inc(ACT.activation(out=sq[:], in_=tot[:], func=AF.Sqrt, bias=epsc[:], scale=1.0 / D), s_t)
        wait(DVE, t)
        t = inc(DVE.reciprocal(out=rs[:], in_=sq[:]), s_t)
        wait(DVE, t); wait(DVE, tf)
        y_tok = {}
        o_tok = {}
        ld_tok = {}

        def fload(tb):
            if tb >= 2:
                wait(SP, y_tok[tb - 2])
            ld_tok[tb] = dma("sync", xt[tb % 2][:], x3_scr[tb * 128:(tb + 1) * 128, :], s_ld[tb % 2])

        fload(0)
        for tb in range(NTB):
            if tb + 1 < NTB:
                fload(tb + 1)
            wait(DVE, ld_tok[tb])
            if tb >= 2:
                wait(DVE, o_tok[tb - 2])
            y_tok[tb] = inc(DVE.scalar_tensor_tensor(out=yo[tb % 2][:], in0=xt[tb % 2][:], scalar=rs[:, tb:tb + 1], in1=fgb[:], op0=ALU.mult, op1=ALU.mult), s_y)
            wait(SP, y_tok[tb])
            o_tok[tb] = dma("sync", y[tb * 128:(tb + 1) * 128, :], yo[tb % 2][:], s_o[tb % 2])
        b.barrier(dummies)
    return nc


def _bias_tables(rpb):
    reps = [0, 1, 2, 3, 14, 15]
    sb_ = np.empty((16, 128, 6, 5, 128), np.float32)
    pb_ = np.empty((16, 128, 6, 5, 128), np.float32)
    ql = np.arange(128)
    kl = np.arange(128)
    for ti, i in enumerate(reps):
        base = min(max(i - 2, 0), 11)
        qr = 2 * i + ql // 64
        qc = ql % 64
        rs_ = np.clip(qr - 4, 0, 24)
        cs_ = np.clip(qc - 8, 0, 48)
        for s in range(5):
            kb = base + s
            kr = 2 * kb + kl // 64
            kc = kl % 64
            valid = ((kr[:, None] >= rs_[None, :]) & (kr[:, None] < rs_[None, :] + 8)
                     & (kc[:, None] >= cs_[None, :]) & (kc[:, None] < cs_[None, :] + 16))
            dr = np.clip(kr[:, None] - qr[None, :] + 7, 0, 14)
            dc = np.clip(kc[:, None] - qc[None, :] + 15, 0, 30)
            vals = rpb[:, dr, dc]
            sb_[:, :, ti, s, :] = np.where(valid[None], vals, np.float32(NEG))
            pb_[:, :, ti, s, :] = 0.0 if (kb // 2 == i // 2) else NEG
    return sb_.reshape(16, 128, -1), pb_.reshape(16, 128, -1)


def _dft_tables():
    def cs(n, scale):
        idx = np.arange(n)
        m = (idx[:, None] * idx[None, :]) % n
        ang = 2.0 * np.pi * m / n
        return (np.cos(ang) * scale), (np.sin(ang) * scale)
    c2048, s2048 = cs(2048, 1.0 / np.sqrt(2048 * 512.0))
    c256, s256 = cs(256, 1.0 / np.sqrt(256 * 512.0))
    ctp = np.zeros((2048, 2048)); stp = np.zeros((2048, 2048))
    for bq in range(8):
        ctp[bq * 256:(bq + 1) * 256, bq * 256:(bq + 1) * 256] = c256
        stp[bq * 256:(bq + 1) * 256, bq * 256:(bq + 1) * 256] = s256
    cc, sc = cs(512, 1.0)
    csm = np.concatenate([cc, -sc], axis=1)
    f = np.float32
    return c2048.astype(f), s2048.astype(f), ctp.astype(f), stp.astype(f), csm.astype(f)


def _prep(x_prompt, x_sample, cache_k, cache_v, c, c_ctx, w_ada, b_ada, norm1_g, w_in,
          rpb, w_out, norm2_g, w_gate, w_up, w_down, final_g):
    f = np.float32
    A = lambda a: np.ascontiguousarray(np.asarray(a), dtype=f)
    x_prompt = A(x_prompt); x_sample = A(x_sample)
    sbias, pbias = _bias_tables(A(rpb)[0])
    c2048, s2048, ctp, stp, csm = _dft_tables()
    common = {
        "w_ada": A(w_ada)[0], "b_ada": A(b_ada)[0].reshape(1, -1), "n1g": A(norm1_g)[0].reshape(32, 128),
        "n2g": A(norm2_g)[0].reshape(32, 128), "fg": A(final_g).reshape(1, -1), "w_in": A(w_in)[0],
        "w_out": A(w_out)[0], "w_gate": A(w_gate)[0], "w_up": A(w_up)[0], "w_down": A(w_down)[0],
        "cs": csm, "ident": np.eye(128, dtype=f),
    }
    zkv = np.zeros((256, 2048), f)
    in_maps = []
    for core in range(8):
        m = dict(common)
        if core < 4:
            m["x"] = x_prompt[core * 8:(core + 1) * 8].reshape(T, D)
            m["cvec"] = A(c_ctx).reshape(32, 128)
            m["kc"] = zkv; m["vc"] = zkv
            m["bias"] = pbias
            m["ctxb"] = np.full((128, 1), NEG, f)
            m["ct"] = ctp; m["st"] = stp
        else:
            bi = core - 4
            m["x"] = x_sample[bi]
            m["cvec"] = A(c)[bi].reshape(32, 128)
            m["kc"] = A(cache_k)[bi, 0].reshape(256, 2048)
            m["vc"] = A(cache_v)[bi, 0].reshape(256, 2048)
            m["bias"] = sbias
            m["ctxb"] = np.zeros((128, 1), f)
            m["ct"] = c2048; m["st"] = s2048
        in_maps.append(m)
    return in_maps


def kernel(**inputs):
    f = np.float32
    in_maps = _prep(**inputs)
    nc = build()
    res = run_bass_kernel_spmd(nc, in_maps, core_ids=list(range(8)))
    r = res.results
    y_prompt = np.stack([r[i]["y"] for i in range(4)]).reshape(32, 256, D)
    y_sample = np.stack([r[4 + i]["y"] for i in range(4)])
    nkk = np.stack([r[i]["nk"] for i in range(4)]).reshape(32, 1, 256, 16, 128)
    nvv = np.stack([r[i]["nv"] for i in range(4)]).reshape(32, 1, 256, 16, 128)
    return (y_prompt.astype(f), y_sample.astype(f), nkk.astype(f), nvv.astype(f))
```
